# Optimizing a Trainium2 kernel written in Bass

```python
import math
import jax, jax.numpy as jnp
from jax import lax
import numpy as np

D_MODEL = 2048
BATCH = 1
SEQ = 8192
DEPTH = 4

ATTN_HEADS = 8
ATTN_HEAD_DIM = 128
REL_BUCKETS = 32
REL_MAX_DISTANCE = 2048

SSD_HEADS = 16
SSD_HEAD_DIM = 64
SSD_INNER = SSD_HEADS * SSD_HEAD_DIM
SSD_GROUPS = 2
SSD_STATE = 128
SSD_CONV = 4
SSD_CHUNK = 256
SSD_CONV_DIM = SSD_INNER + 2 * SSD_GROUPS * SSD_STATE

MOBA_HEADS = ATTN_HEADS
MOBA_WIDTH = MOBA_HEADS * ATTN_HEAD_DIM
MOBA_BLOCK = 256
MOBA_TOPK = 3
MOBA_Q_CHUNK = 32

NSA_Q_HEADS = ATTN_HEADS
NSA_KV_HEADS = 2
NSA_GROUP = NSA_Q_HEADS // NSA_KV_HEADS
NSA_WIDTH = NSA_Q_HEADS * ATTN_HEAD_DIM
NSA_KV_WIDTH = NSA_KV_HEADS * ATTN_HEAD_DIM
NSA_CMP_BLOCK = 32
NSA_CMP_STRIDE = 16
NSA_SEL_BLOCK = 64
NSA_SEL_TOPN = 16
NSA_WINDOW = 512
NSA_Q_CHUNK = 128

RET_HEADS = 8
RET_KEY_DIM = 128
RET_VALUE_DIM = 128
RET_WIDTH = RET_HEADS * RET_VALUE_DIM
RET_CHUNK = 256
ROPE_BASE = 10000.0

FFN_DIM = 5632
FFN_CONV = 3

MIX_WIDTH = SSD_INNER + MOBA_WIDTH
EVEN_IN = SSD_INNER + SSD_CONV_DIM + SSD_HEADS + 3 * MOBA_WIDTH
ODD_IN = NSA_WIDTH + 6 * NSA_KV_WIDTH + 3 * NSA_Q_HEADS + 4 * RET_WIDTH
SEQ_MULTIPLE = 256
LN_EPS = 1e-5
NEG = -1e30
DEEPNORM_ALPHA = (2 * DEPTH) ** 0.25
DEEPNORM_BETA = (8 * DEPTH) ** -0.25

kernel_name = "hybrid_ssd_moba_nsa_retention_trunk"


def layer_norm(x, g, b):
    xf = x.astype(jnp.float32)
    mu = xf.mean(-1, keepdims=True)
    var = jnp.square(xf - mu).mean(-1, keepdims=True)
    return ((xf - mu) * lax.rsqrt(var + LN_EPS) * g + b).astype(x.dtype)


def causal_depthwise_conv(x, w, b):
    k, c = w.shape
    y = lax.conv_general_dilated(x, w[:, None, :].astype(x.dtype), window_strides=(1,),
                                 padding=[(k - 1, 0)], dimension_numbers=('NWC', 'WIO', 'NWC'),
                                 feature_group_count=c)
    return y + b


def rel_bucket(dist):
    n = jnp.maximum(dist, 0)
    max_exact = REL_BUCKETS // 2
    nf = jnp.maximum(n, max_exact).astype(jnp.float32)
    large = max_exact + (jnp.log(nf / max_exact) / math.log(REL_MAX_DISTANCE / max_exact)
                         * (REL_BUCKETS - max_exact)).astype(jnp.int32)
    large = jnp.minimum(large, REL_BUCKETS - 1)
    return jnp.where(n < max_exact, n, large)


def head_bias(table, bucket):
    return jax.vmap(lambda tab_h, b_h: tab_h[b_h], in_axes=(1, 1), out_axes=1)(table.astype(jnp.float32), bucket)


def masked_softmax(logits, mask):
    p = jax.nn.softmax(jnp.where(mask, logits, NEG), axis=-1)
    return jnp.where(mask, p, 0.0)


def segsum_decay(a_cum):
    L = a_cum.shape[-1]
    diff = a_cum[..., :, None] - a_cum[..., None, :]
    causal = jnp.tril(jnp.ones((L, L), bool))
    return jnp.where(causal, jnp.exp(jnp.where(causal, diff, 0.0)), 0.0)


def ssd_mixer(xbc_raw, z, dt_raw, conv_w, conv_b, dt_bias, a_log, d_skip, norm_w):
    f32 = jnp.float32
    B_, T, _ = z.shape
    G, hpg, P, N, L = SSD_GROUPS, SSD_HEADS // SSD_GROUPS, SSD_HEAD_DIM, SSD_STATE, SSD_CHUNK
    nc = T // L
    xbc = jax.nn.silu(causal_depthwise_conv(xbc_raw, conv_w, conv_b)).astype(f32)
    xs, bs, cs = jnp.split(xbc, [SSD_INNER, SSD_INNER + G * N], axis=-1)
    xs = xs.reshape(B_, nc, L, SSD_HEADS, P)
    bs = bs.reshape(B_, nc, L, G, N)
    cs = cs.reshape(B_, nc, L, G, N)
    dt = jax.nn.softplus(dt_raw.astype(f32) + dt_bias.astype(f32))
    a = -jnp.exp(a_log.astype(f32))
    dt_c = dt.reshape(B_, nc, L, SSD_HEADS)
    a_cum = jnp.cumsum((dt_c * a).transpose(0, 3, 1, 2), axis=-1)
    xdt_g = (xs * dt_c[..., None]).reshape(B_, nc, L, G, hpg, P)
    cb = jnp.einsum('bclgn,bcsgn->bgcls', cs, bs)
    decay = segsum_decay(a_cum).reshape(B_, G, hpg, nc, L, L)
    y_diag = jnp.einsum('bgcls,bghcls,bcsghp->bclghp', cb, decay, xdt_g)
    decay_to_end = jnp.exp(a_cum[..., -1:] - a_cum).reshape(B_, G, hpg, nc, L)
    states = jnp.einsum('bcsgn,bghcs,bcsghp->bcghpn', bs, decay_to_end, xdt_g)
    chunk_decay = jnp.exp(a_cum[..., -1]).reshape(B_, G, hpg, nc).transpose(3, 0, 1, 2)

    def step(h, inp):
        s, dec = inp
        return h * dec[..., None, None] + s, h

    h0 = jnp.zeros((B_, G, hpg, P, N), f32)
    _, prev = lax.scan(step, h0, (states.transpose(1, 0, 2, 3, 4, 5), chunk_decay))
    prev = prev.transpose(1, 0, 2, 3, 4, 5)
    decay_in = jnp.exp(a_cum).reshape(B_, G, hpg, nc, L)
    y_off = jnp.einsum('bclgn,bcghpn,bghcl->bclghp', cs, prev, decay_in)
    y = (y_diag + y_off).reshape(B_, T, SSD_HEADS, P) + xs.reshape(B_, T, SSD_HEADS, P) * d_skip.astype(f32)[:, None]
    yg = (y.reshape(B_, T, SSD_INNER) * jax.nn.silu(z.astype(f32))).reshape(B_, T, G, SSD_INNER // G)
    yg = yg * lax.rsqrt(jnp.mean(jnp.square(yg), axis=-1, keepdims=True) + LN_EPS)
    return (yg.reshape(B_, T, SSD_INNER) * norm_w).astype(z.dtype)


def moba_attention(q, k, v, rel_bias):
    f32 = jnp.float32
    B_, T, H, Dh = q.shape
    nb = T // MOBA_BLOCK
    scale = Dh ** -0.5
    pos = jnp.arange(T)
    kb = k.reshape(B_, nb, MOBA_BLOCK, H, Dh)
    vb = v.reshape(B_, nb, MOBA_BLOCK, H, Dh)
    kmean = kb.astype(f32).mean(2)
    gate = jnp.einsum('bthd,bnhd->bhtn', q.astype(f32), kmean)
    cur = pos // MOBA_BLOCK
    past = jnp.arange(nb)[None, :] < cur[:, None]
    sc, idx = lax.top_k(jnp.where(past, gate, -jnp.inf), min(MOBA_TOPK, nb))
    blocks = jnp.concatenate([idx, jnp.broadcast_to(cur[None, None, :, None], (B_, H, T, 1)).astype(idx.dtype)], -1)
    ok = jnp.concatenate([sc > -jnp.inf, jnp.ones((B_, H, T, 1), bool)], -1)
    nsel = blocks.shape[-1]
    Cq = MOBA_Q_CHUNK
    nqc = T // Cq
    q_c = q.astype(f32).reshape(B_, nqc, Cq, H, Dh).transpose(1, 0, 3, 2, 4)
    blk_c = blocks.reshape(B_, H, nqc, Cq, nsel).transpose(2, 0, 1, 3, 4)
    ok_c = ok.reshape(B_, H, nqc, Cq, nsel).transpose(2, 0, 1, 3, 4)
    pos_c = pos.reshape(nqc, Cq)
    kbh = kb.transpose(0, 3, 1, 2, 4)
    vbh = vb.transpose(0, 3, 1, 2, 4)
    bi = jnp.arange(B_)[:, None, None, None]
    hi = jnp.arange(H)[None, :, None, None]
    koff = jnp.arange(MOBA_BLOCK)

    def chunk(args):
        q_i, blk_i, ok_i, qp = args
        kg = kbh[bi, hi, blk_i]
        vg = vbh[bi, hi, blk_i].reshape(B_, H, Cq, nsel * MOBA_BLOCK, Dh)
        kpos = blk_i[..., None] * MOBA_BLOCK + koff
        qpb = qp[None, None, :, None, None]
        logits = jnp.einsum('bhqd,bhqskd->bhqsk', q_i, kg) * scale + head_bias(rel_bias, rel_bucket(qpb - kpos))
        mask = ok_i[..., None] & (kpos <= qpb)
        p = masked_softmax(logits.reshape(B_, H, Cq, -1), mask.reshape(B_, H, Cq, -1))
        return jnp.einsum('bhqk,bhqkd->bhqd', p, vg)

    out = lax.map(chunk, (q_c, blk_c, ok_c, pos_c))
    return out.transpose(1, 0, 3, 2, 4).reshape(B_, T, H * Dh).astype(q.dtype)


def nsa_attention(q, kc, vc, ks, vs, kw, vw, gates, cmp_pe, cmp_w1, cmp_w2, rel_bias):
    f32 = jnp.float32
    B_, T, Hq, Dh = q.shape
    Hkv, G = NSA_KV_HEADS, NSA_GROUP
    scale = Dh ** -0.5
    pos = jnp.arange(T)
    qg = q.astype(f32).reshape(B_, T, Hkv, G, Dh)

    ncb = (T - NSA_CMP_BLOCK) // NSA_CMP_STRIDE + 1
    cmp_start = jnp.arange(ncb) * NSA_CMP_STRIDE
    cidx = cmp_start[:, None] + jnp.arange(NSA_CMP_BLOCK)[None, :]

    def compress(x, pe, w1, w2):
        xb = x[:, cidx] + pe[:, None, :]
        xb = xb.transpose(0, 1, 3, 2, 4).reshape(B_, ncb, Hkv, NSA_CMP_BLOCK * Dh)
        return jax.nn.gelu(xb @ w1) @ w2

    k_cmp = compress(kc, cmp_pe[0], cmp_w1[0], cmp_w2[0])
    v_cmp = compress(vc, cmp_pe[1], cmp_w1[1], cmp_w2[1])
    cmp_end = cmp_start + NSA_CMP_BLOCK - 1
    cmp_mask = cmp_end[None, :] <= pos[:, None]
    bias_c = rel_bias.astype(f32)[rel_bucket(pos[:, None] - cmp_end[None, :])].transpose(2, 0, 1).reshape(Hkv, G, T, ncb)
    logits_c = jnp.einsum('btkgd,bnkd->bkgtn', qg, k_cmp) * scale + bias_c
    p_cmp = masked_softmax(logits_c, cmp_mask)
    o_cmp = jnp.einsum('bkgtn,bnkd->btkgd', p_cmp, v_cmp)

    ns = T // NSA_SEL_BLOCK
    sel_start = jnp.arange(ns) * NSA_SEL_BLOCK
    overlap = ((cmp_start[:, None] < sel_start[None, :] + NSA_SEL_BLOCK)
               & (cmp_start[:, None] + NSA_CMP_BLOCK > sel_start[None, :])).astype(f32)
    imp = jnp.einsum('bkgtn,nj->bktj', p_cmp, overlap)
    cur = pos // NSA_SEL_BLOCK
    j = jnp.arange(ns)[None, :]
    forced = (j == 0) | (j == cur[:, None]) | (j == cur[:, None] - 1)
    score = jnp.where(forced, jnp.inf, jnp.where(j <= cur[:, None], imp, -jnp.inf))
    n_sel = min(NSA_SEL_TOPN, ns)
    sc, sidx = lax.top_k(score, n_sel)
    sok = sc > -jnp.inf
    ksb = ks.reshape(B_, ns, NSA_SEL_BLOCK, Hkv, Dh).transpose(0, 3, 1, 2, 4)
    vsb = vs.reshape(B_, ns, NSA_SEL_BLOCK, Hkv, Dh).transpose(0, 3, 1, 2, 4)
    Cq = NSA_Q_CHUNK
    nqc = T // Cq
    q_c = qg.reshape(B_, nqc, Cq, Hkv, G, Dh).transpose(1, 0, 3, 4, 2, 5)
    idx_c = sidx.reshape(B_, Hkv, nqc, Cq, n_sel).transpose(2, 0, 1, 3, 4)
    ok_c = sok.reshape(B_, Hkv, nqc, Cq, n_sel).transpose(2, 0, 1, 3, 4)
    pos_c = pos.reshape(nqc, Cq)
    bi = jnp.arange(B_)[:, None, None, None]
    hi = jnp.arange(Hkv)[None, :, None, None]
    koff = jnp.arange(NSA_SEL_BLOCK)

    def sel_chunk(args):
        q_i, idx_i, ok_i, qp = args
        kg = ksb[bi, hi, idx_i].reshape(B_, Hkv, Cq, n_sel * NSA_SEL_BLOCK, Dh)
        vg = vsb[bi, hi, idx_i].reshape(B_, Hkv, Cq, n_sel * NSA_SEL_BLOCK, Dh)
        kpos = idx_i[..., None] * NSA_SEL_BLOCK + koff
        qpb = qp[None, None, :, None, None]
        mask = (ok_i[..., None] & (kpos <= qpb)).reshape(B_, Hkv, 1, Cq, -1)
        bucket = rel_bucket(qpb - kpos).reshape(B_, Hkv, Cq, -1)
        bias = head_bias(rel_bias, jnp.repeat(bucket, G, axis=1)).reshape(B_, Hkv, G, Cq, -1)
        logits = jnp.einsum('bkgqd,bkqsd->bkgqs', q_i, kg) * scale + bias
        p = masked_softmax(logits, mask)
        return jnp.einsum('bkgqs,bkqsd->bkgqd', p, vg)

    o_sel = lax.map(sel_chunk, (q_c, idx_c, ok_c, pos_c))
    o_sel = o_sel.transpose(1, 0, 4, 2, 3, 5).reshape(B_, T, Hkv, G, Dh)

    WB = NSA_Q_CHUNK
    nwb = T // WB
    span = WB + NSA_WINDOW
    widx = jnp.arange(nwb)[:, None] * WB + jnp.arange(span)[None, :]
    kwin = jnp.pad(kw, ((0, 0), (NSA_WINDOW, 0), (0, 0), (0, 0)))[:, widx]
    vwin = jnp.pad(vw, ((0, 0), (NSA_WINDOW, 0), (0, 0), (0, 0)))[:, widx]
    rel = jnp.arange(WB)[:, None] + NSA_WINDOW - jnp.arange(span)[None, :]
    wmask = ((rel >= 0) & (rel < NSA_WINDOW))[None] & ((widx - NSA_WINDOW) >= 0)[:, None, :]
    bias_w = rel_bias.astype(f32)[rel_bucket(rel)].transpose(2, 0, 1).reshape(Hkv, G, 1, WB, span)
    qw = qg.reshape(B_, nwb, WB, Hkv, G, Dh)
    logits_w = jnp.einsum('bnqkgd,bnskd->bkgnqs', qw, kwin) * scale + bias_w
    p_w = masked_softmax(logits_w, wmask)
    o_win = jnp.einsum('bkgnqs,bnskd->bnqkgd', p_w, vwin).reshape(B_, T, Hkv, G, Dh)

    g = jax.nn.sigmoid(gates.astype(f32)).reshape(B_, T, Hkv, G, 3)
    o = g[..., 0:1] * o_cmp + g[..., 1:2] * o_sel + g[..., 2:3] * o_win
    return o.reshape(B_, T, Hq * Dh).astype(q.dtype)


def rotary(x, pos):
    D = x.shape[-1]
    inv = 1.0 / (ROPE_BASE ** (jnp.arange(0, D, 2, dtype=jnp.float32) / D))
    ang = pos[:, None].astype(jnp.float32) * inv[None, :]
    cos = jnp.cos(ang)[None, :, None, :]
    sin = jnp.sin(ang)[None, :, None, :]
    x1, x2 = x[..., 0::2], x[..., 1::2]
    return jnp.stack([x1 * cos - x2 * sin, x1 * sin + x2 * cos], axis=-1).reshape(x.shape)


def retention(q, k, v, gate):
    f32 = jnp.float32
    B_, T, H, Dk = q.shape
    Dv = v.shape[-1]
    C = RET_CHUNK
    nc = T // C
    pos = jnp.arange(T)
    q = rotary(q.astype(f32), pos).reshape(B_, nc, C, H, Dk)
    k = (rotary(k.astype(f32), pos) * Dk ** -0.5).reshape(B_, nc, C, H, Dk)
    v = v.astype(f32).reshape(B_, nc, C, H, Dv)
    log_g = jnp.log(1.0 - 2.0 ** (-5.0 - jnp.arange(H, dtype=f32)))
    i = jnp.arange(C)
    rel = i[:, None] - i[None, :]
    inner_decay = jnp.where(rel >= 0, jnp.exp(log_g[:, None, None] * jnp.maximum(rel, 0)), 0.0)
    inner = jnp.einsum('bcihd,bcjhd->bchij', q, k) * inner_decay
    y_in = jnp.einsum('bchij,bcjhe->bcihe', inner, v)
    k_decay = jnp.exp(log_g[None, :] * (C - 1 - i)[:, None])
    kv = jnp.einsum('bcjhd,jh,bcjhe->bchde', k, k_decay, v)
    chunk_decay = jnp.exp(log_g * C)[None, :, None, None]

    def step(R, kv_c):
        return R * chunk_decay + kv_c, R

    _, prev = lax.scan(step, jnp.zeros((B_, H, Dk, Dv), f32), kv.transpose(1, 0, 2, 3, 4))
    prev = prev.transpose(1, 0, 2, 3, 4)
    q_decay = jnp.exp(log_g[None, :] * (i + 1)[:, None])
    y_cross = jnp.einsum('bcihd,ih,bchde->bcihe', q, q_decay, prev)
    y = (y_in + y_cross).reshape(B_, T, H, Dv)
    mu = y.mean(-1, keepdims=True)
    var = jnp.square(y - mu).mean(-1, keepdims=True)
    y = ((y - mu) * lax.rsqrt(var + LN_EPS)).reshape(B_, T, H * Dv)
    return (jax.nn.silu(gate.astype(f32)) * y).astype(gate.dtype)


def even_mixer(h, w_in, conv_w, conv_b, dt_bias, a_log, d_skip, norm_w, w_out, rel_bias):
    B_, T, _ = h.shape
    p = h @ w_in
    sizes = [SSD_INNER, SSD_CONV_DIM, SSD_HEADS, MOBA_WIDTH, MOBA_WIDTH, MOBA_WIDTH]
    z, xbc, dt, q, k, v = jnp.split(p, np.cumsum(sizes)[:-1].tolist(), axis=-1)
    y_ssd = ssd_mixer(xbc, z, dt, conv_w, conv_b, dt_bias, a_log, d_skip, norm_w)
    shp = (B_, T, MOBA_HEADS, ATTN_HEAD_DIM)
    y_moba = moba_attention(q.reshape(shp), k.reshape(shp), v.reshape(shp), rel_bias)
    return jnp.concatenate([y_ssd, y_moba], axis=-1) @ w_out


def odd_mixer(h, w_in, cmp_pe, cmp_w1, cmp_w2, w_out, rel_bias):
    B_, T, _ = h.shape
    p = h @ w_in
    sizes = [NSA_WIDTH] + [NSA_KV_WIDTH] * 6 + [3 * NSA_Q_HEADS] + [RET_WIDTH] * 4
    q, kc, vc, ks, vs, kw, vw, gates, rq, rk, rv, rg = jnp.split(p, np.cumsum(sizes)[:-1].tolist(), axis=-1)
    kvs = (B_, T, NSA_KV_HEADS, ATTN_HEAD_DIM)
    y_nsa = nsa_attention(q.reshape(B_, T, NSA_Q_HEADS, ATTN_HEAD_DIM), kc.reshape(kvs), vc.reshape(kvs),
                          ks.reshape(kvs), vs.reshape(kvs), kw.reshape(kvs), vw.reshape(kvs), gates,
                          cmp_pe, cmp_w1, cmp_w2, rel_bias)
    y_ret = retention(rq.reshape(B_, T, RET_HEADS, RET_KEY_DIM), rk.reshape(B_, T, RET_HEADS, RET_KEY_DIM),
                      rv.reshape(B_, T, RET_HEADS, RET_VALUE_DIM), rg)
    return jnp.concatenate([y_nsa, y_ret], axis=-1) @ w_out


def conv_ffn(x, w_up, conv_w, conv_b, w_down):
    a, u = jnp.split(x @ w_up, 2, axis=-1)
    a = causal_depthwise_conv(a, conv_w, conv_b)
    return (jax.nn.silu(a) * u) @ w_down


def setup_inputs(seed: int = 0) -> dict:
    key = jax.random.key(seed)
    ks = jax.random.split(key, 24)
    n_even = (DEPTH + 1) // 2
    n_odd = DEPTH // 2
    f32 = jnp.float32

    def normal(k, shape, scale):
        return jax.random.normal(k, shape, f32) * scale

    dt0 = jnp.exp(jax.random.uniform(ks[5], (n_even, SSD_HEADS), f32, math.log(1e-3), math.log(1e-1)))
    return {
        "x": normal(ks[0], (BATCH, SEQ, D_MODEL), 1.0),
        "rel_bias": normal(ks[1], (REL_BUCKETS, ATTN_HEADS), 0.2),
        "ev_w_in": normal(ks[2], (n_even, D_MODEL, EVEN_IN), D_MODEL ** -0.5),
        "ev_conv_w": normal(ks[3], (n_even, SSD_CONV, SSD_CONV_DIM), SSD_CONV ** -0.5),
        "ev_conv_b": normal(ks[4], (n_even, SSD_CONV_DIM), 0.02),
        "ev_dt_bias": dt0 + jnp.log(-jnp.expm1(-dt0)),
        "ev_a_log": jnp.log(jax.random.uniform(ks[6], (n_even, SSD_HEADS), f32, 1.0, 16.0)),
        "ev_d_skip": 1.0 + normal(ks[7], (n_even, SSD_HEADS), 0.1),
        "ev_norm_w": 1.0 + normal(ks[8], (n_even, SSD_INNER), 0.02),
        "ev_w_out": normal(ks[9], (n_even, MIX_WIDTH, D_MODEL), MIX_WIDTH ** -0.5 * DEEPNORM_BETA),
        "od_w_in": normal(ks[10], (n_odd, D_MODEL, ODD_IN), D_MODEL ** -0.5),
        "od_cmp_pe": normal(ks[11], (n_odd, 2, NSA_CMP_BLOCK, ATTN_HEAD_DIM), 0.02),
        "od_cmp_w1": normal(ks[12], (n_odd, 2, NSA_CMP_BLOCK * ATTN_HEAD_DIM, ATTN_HEAD_DIM), (NSA_CMP_BLOCK * ATTN_HEAD_DIM) ** -0.5),
        "od_cmp_w2": normal(ks[13], (n_odd, 2, ATTN_HEAD_DIM, ATTN_HEAD_DIM), ATTN_HEAD_DIM ** -0.5),
        "od_w_out": normal(ks[14], (n_odd, MIX_WIDTH, D_MODEL), MIX_WIDTH ** -0.5 * DEEPNORM_BETA),
        "ffn_w_up": normal(ks[15], (DEPTH, D_MODEL, 2 * FFN_DIM), D_MODEL ** -0.5),
        "ffn_conv_w": normal(ks[16], (DEPTH, FFN_CONV, FFN_DIM), FFN_CONV ** -0.5),
        "ffn_conv_b": normal(ks[17], (DEPTH, FFN_DIM), 0.02),
        "ffn_w_down": normal(ks[18], (DEPTH, FFN_DIM, D_MODEL), FFN_DIM ** -0.5 * DEEPNORM_BETA),
        "ln_g": 1.0 + normal(ks[19], (DEPTH, 2, D_MODEL), 0.02),
        "ln_b": normal(ks[20], (DEPTH, 2, D_MODEL), 0.02),
    }


def reference(x, rel_bias, ev_w_in, ev_conv_w, ev_conv_b, ev_dt_bias, ev_a_log, ev_d_skip, ev_norm_w, ev_w_out,
              od_w_in, od_cmp_pe, od_cmp_w1, od_cmp_w2, od_w_out,
              ffn_w_up, ffn_conv_w, ffn_conv_b, ffn_w_down, ln_g, ln_b):
    B_, T0, _ = x.shape
    T = ((T0 + SEQ_MULTIPLE - 1) // SEQ_MULTIPLE) * SEQ_MULTIPLE
    h = jnp.pad(x, ((0, 0), (0, T - T0), (0, 0)))
    for layer in range(DEPTH):
        i = layer // 2
        if layer % 2 == 0:
            mix = even_mixer(h, ev_w_in[i], ev_conv_w[i], ev_conv_b[i], ev_dt_bias[i], ev_a_log[i],
                             ev_d_skip[i], ev_norm_w[i], ev_w_out[i], rel_bias)
        else:
            mix = odd_mixer(h, od_w_in[i], od_cmp_pe[i], od_cmp_w1[i], od_cmp_w2[i], od_w_out[i], rel_bias)
        h = layer_norm(DEEPNORM_ALPHA * h + mix, ln_g[layer, 0], ln_b[layer, 0])
        ffn = conv_ffn(h, ffn_w_up[layer], ffn_conv_w[layer], ffn_conv_b[layer], ffn_w_down[layer])
        h = layer_norm(DEEPNORM_ALPHA * h + ffn, ln_g[layer, 1], ln_b[layer, 1])
    return h[:, :T0]
```

```python
from contextlib import ExitStack
import numpy as np
import concourse.bass as bass
import concourse.mybir as mybir
from concourse.bass_utils import run_bass_kernel_spmd

F32 = mybir.dt.float32
BF16 = mybir.dt.bfloat16
ALU = mybir.AluOpType
AF = mybir.ActivationFunctionType
AX = mybir.AxisListType


class Tok:
    __slots__ = ("w", "rd")

    def __init__(self):
        self.w = None
        self.rd = {}


class V:
    __slots__ = ("ap", "toks")

    def __init__(self, ap, toks):
        self.ap = ap
        self.toks = toks

    def __getitem__(self, idx):
        return V(self.ap[idx], self.toks)

    def re(self, pat, **kw):
        return V(self.ap.rearrange(pat, **kw), self.toks)


def _ap(x):
    return x.ap if isinstance(x, V) else x


def _toks(xs):
    out = []
    for x in xs:
        if isinstance(x, V):
            out.extend(x.toks)
        elif isinstance(x, Tok):
            out.append(x)
    return out


class Prog:
    ENGS = ("tensor", "vector", "scalar", "gpsimd", "sync")
    NDMA = 8

    def __init__(self):
        self.nc = bass.Bass("TRN2", target_bir_lowering=False)
        self.es = ExitStack()
        self.ops = {e: [] for e in self.ENGS}
        self.cnt = {e: 0 for e in self.ENGS}
        self.waited = {e: {} for e in self.ENGS}
        self.sems = {}
        for e in self.ENGS:
            self.sems[e] = self.es.enter_context(self.nc.semaphore("s_" + e))
        self.dsem = {}
        self.dcnt = {}
        self.dnext = {}
        for q in ("sync", "gpsimd", "scalar"):
            for i in range(self.NDMA):
                k = "d_%s_%d" % (q, i)
                self.sems[k] = self.es.enter_context(self.nc.semaphore(k))
                self.dcnt[k] = 0
            self.dnext[q] = 0
        self.out_events = []
        self.strict = ("vector", "scalar", "gpsimd")
        self.nalloc = 0

    def dram(self, name, shape, dt, kind):
        return self.nc.dram_tensor(name, list(shape), dt, kind=kind).ap()

    def sb(self, shape, dt, name=None):
        self.nalloc += 1
        t = self.es.enter_context(self.nc.sbuf_tensor(name or ("sb%d" % self.nalloc), list(shape), dt))
        return V(t[:], [Tok()])

    def sb_rot(self, key, shape, dt, n):
        if not hasattr(self, "_rot"):
            self._rot = {}
        if key not in self._rot:
            self._rot[key] = [[self.sb(shape, dt, name="%s_%d" % (key, j)) for j in range(n)], 0]
        ent = self._rot[key]
        v = ent[0][ent[1] % n]
        ent[1] += 1
        return v

    def ps(self, shape, dt=F32, name=None):
        self.nalloc += 1
        t = self.es.enter_context(self.nc.psum_tensor(name or ("ps%d" % self.nalloc), list(shape), dt))
        return V(t[:], [Tok()])

    def _deps(self, eng, reads, writes):
        need = {}

        def add(ev):
            if ev is None:
                return
            k, v = ev
            if need.get(k, 0) < v:
                need[k] = v
        for t in _toks(reads):
            add(t.w)
        for t in _toks(writes):
            add(t.w)
            for k, v in t.rd.items():
                add((k, v))
        waits = []
        wd = self.waited[eng]
        for k, v in need.items():
            if k == eng and not (self.strict and eng in self.strict):
                continue
            if wd.get(k, 0) >= v:
                continue
            wd[k] = v
            waits.append((k, v))
        return waits

    def _commit(self, ev, reads, writes):
        k, v = ev
        for t in _toks(reads):
            if t.rd.get(k, 0) < v:
                t.rd[k] = v
        for t in _toks(writes):
            t.w = ev
            t.rd = {}

    def op(self, eng, fn, reads, writes):
        waits = self._deps(eng, reads, writes)
        self.cnt[eng] += 1
        ev = (eng, self.cnt[eng])
        self.ops[eng].append((waits, fn, (eng, 1)))
        self._commit(ev, reads, writes)
        return ev

    def dma(self, out, in_, q="sync", is_output=False, **kw):
        reads = [in_]
        writes = [out]
        i = self.dnext[q]
        self.dnext[q] = (i + 1) % self.NDMA
        k = "d_%s_%d" % (q, i)
        waits = self._deps(q, reads, writes)
        prev = self.dcnt[k]
        if prev > 0 and self.waited[q].get(k, 0) < prev:
            self.waited[q][k] = prev
            waits.append((k, prev))
        self.dcnt[k] += 16
        ev = (k, self.dcnt[k])
        o, i_ = _ap(out), _ap(in_)
        self.ops[q].append((waits, lambda e: e.dma_start(out=o, in_=i_, **kw), (k, 16)))
        self._commit(ev, reads, writes)
        if is_output:
            self.out_events.append(ev)
        return ev

    def finish(self):
        nc = self.nc
        fin = {}
        for k, v in self.out_events:
            fin[k] = max(fin.get(k, 0), v)
        sems = self.sems
        ops = self.ops
        with nc.Block() as block:
            def runner(name):
                def body(e):
                    for waits, fn, (sk, inc) in ops[name]:
                        for k, v in waits:
                            e.wait_ge(sems[k], v)
                        ins = fn(e)
                        ins.then_inc(sems[sk], inc)
                    if name == "sync":
                        for k, v in fin.items():
                            e.wait_ge(sems[k], v)
                return body
            block.tensor(runner("tensor"))
            block.vector(runner("vector"))
            block.scalar(runner("scalar"))
            block.gpsimd(runner("gpsimd"))
            block.sync(runner("sync"))
        self.es.close()
        return nc

    def matmul(self, out, lhsT, rhs, start=True, stop=True):
        o, l, r = _ap(out), _ap(lhsT), _ap(rhs)
        return self.op("tensor", lambda e: e.matmul(o, l, r, start=start, stop=stop), [lhsT, rhs], [out])

    def transpose(self, out, in_, ident):
        o, i, d = _ap(out), _ap(in_), _ap(ident)
        return self.op("tensor", lambda e: e.transpose(o, i, d), [in_, ident], [out])

    def act(self, out, in_, func, bias=None, scale=1.0, accum_out=None, eng="scalar"):
        o, i = _ap(out), _ap(in_)
        b = _ap(bias) if bias is not None else None
        s = _ap(scale)
        a = _ap(accum_out) if accum_out is not None else None
        kw = {}
        if b is not None:
            kw["bias"] = b
        if a is not None:
            kw["accum_out"] = a
        return self.op("scalar", lambda e: e.activation(o, i, func, scale=s, **kw),
                       [in_, bias, scale], [out, accum_out])

    def tt(self, out, in0, in1, op, eng="vector"):
        o, a, b = _ap(out), _ap(in0), _ap(in1)
        return self.op(eng, lambda e: e.tensor_tensor(o, a, b, op), [in0, in1], [out])

    def ts(self, out, in0, s1, op0, s2=None, op1=None, accum_out=None, eng="vector"):
        o, a = _ap(out), _ap(in0)
        x1, x2 = _ap(s1), _ap(s2)
        acc = _ap(accum_out) if accum_out is not None else None
        kw = {}
        if op1 is not None:
            kw["op1"] = op1
        if acc is not None:
            kw["accum_out"] = acc
        return self.op(eng, lambda e: e.tensor_scalar(o, a, x1, x2, op0, **kw),
                       [in0, s1, s2], [out, accum_out])

    def stt(self, out, in0, scalar, in1, op0, op1, eng="vector"):
        o, a, s, b = _ap(out), _ap(in0), _ap(scalar), _ap(in1)
        return self.op(eng, lambda e: e.scalar_tensor_tensor(o, a, s, b, op0, op1), [in0, scalar, in1], [out])

    def copy(self, out, in_, eng="vector"):
        o, i = _ap(out), _ap(in_)
        if eng == "scalar":
            return self.op(eng, lambda e: e.copy(o, i), [in_], [out])
        return self.op(eng, lambda e: e.tensor_copy(o, i), [in_], [out])

    def memset(self, out, val, eng="vector"):
        o = _ap(out)
        return self.op(eng, lambda e: e.memset(o, val), [], [out])

    def reduce(self, out, in_, op, axis=AX.X, eng="vector"):
        o, i = _ap(out), _ap(in_)
        return self.op(eng, lambda e: e.tensor_reduce(o, i, axis, op), [in_], [out])

    def recip(self, out, in_):
        o, i = _ap(out), _ap(in_)
        return self.op("vector", lambda e: e.reciprocal(o, i), [in_], [out])

    def max8(self, out, in_):
        o, i = _ap(out), _ap(in_)
        return self.op("vector", lambda e: e.max(o, i), [in_], [out])

    def match_replace(self, out, to_replace, values, imm):
        o, r, v = _ap(out), _ap(to_replace), _ap(values)
        return self.op("vector", lambda e: e.match_replace(o, r, v, imm), [to_replace, values], [out])

    def bn_stats(self, out, in_):
        o, i = _ap(out), _ap(in_)
        return self.op("vector", lambda e: e.bn_stats(o, i), [in_], [out])

    def bn_aggr(self, out, in_):
        o, i = _ap(out), _ap(in_)
        return self.op("vector", lambda e: e.bn_aggr(o, i), [in_], [out])


T = 8192
D = 2048
NCORE = 8
TPC = T // NCORE
KC = D // 128
EVEN_IN = 5648
ODD_IN = 6680
FFN = 5632
FCH = FFN // 128
ALPHA = 8.0 ** 0.25
LN_EPS = 1e-5
SCALE = 128.0 ** -0.5
NEGBIG = -30000.0


def make_ident(P, dt):
    idf = P.sb([128, 128], F32)
    P.memset(idf, 0.0, eng="gpsimd")
    o = idf.ap
    P.op("gpsimd", lambda e: e.affine_select(out=o, in_=o, pattern=[[-1, 128]], compare_op=ALU.not_equal,
                                              fill=1.0, base=0, channel_multiplier=1), [idf], [idf])
    if dt == F32:
        return idf
    idb = P.sb([128, 128], dt)
    P.copy(idb, idf)
    return idb


class PsumPool:
    def __init__(self, P, nf32, nbf16=0):
        self.f = [P.ps([128, 512], F32) for _ in range(nf32)]
        self.b = [P.ps([128, 1024], BF16) for _ in range(nbf16)]
        self.fi = 0
        self.bi = 0

    def nf(self):
        x = self.f[self.fi % len(self.f)]
        self.fi += 1
        return x

    def nb(self):
        x = self.b[self.bi % len(self.b)]
        self.bi += 1
        return x


def load_hT(P, pp, ident_bf, h_dram, ntiles, hT, col0=0):
    for i in range(ntiles):
        hf = P.sb_rot("ldh_f", [128, D], F32, 2)
        P.dma(hf, h_dram[i * 128:(i + 1) * 128, :], q="sync")
        hb = P.sb_rot("ldh_b", [128, D], BF16, 2)
        P.copy(hb[:, 0:1024], hf[:, 0:1024], eng="vector")
        P.copy(hb[:, 1024:2048], hf[:, 1024:2048], eng="gpsimd")
        for g in range(KC // 8):
            pt = pp.nb()
            for j in range(8):
                kc = g * 8 + j
                P.transpose(pt[:, j * 128:(j + 1) * 128], hb[:, kc * 128:(kc + 1) * 128], ident_bf)
            dst = hT[:, g * 8:(g + 1) * 8, col0 + i * 128: col0 + (i + 1) * 128]
            src = pt.re("p (j t) -> p j t", t=128)
            if g % 2 == 0:
                P.copy(dst, src, eng="vector")
            else:
                P.copy(dst, src, eng="scalar")


def layer_norm_tile(P, pre, out, g_bc, b_bc):
    st = P.sb_rot("ln_st", [128, 4, 6], F32, 2)
    for c in range(4):
        P.bn_stats(st[:, c, :], pre[:, c * 512:(c + 1) * 512])
    mv = P.sb_rot("ln_mv", [128, 2], F32, 2)
    P.bn_aggr(mv, st.re("p a b -> p (a b)"))
    rs = P.sb_rot("ln_rs", [128, 1], F32, 2)
    P.ts(rs, mv[:, 1:2], LN_EPS, ALU.add)
    P.act(rs, rs, AF.Sqrt)
    P.recip(rs, rs)
    P.ts(out, pre, mv[:, 0:1], ALU.subtract, rs, ALU.mult)
    P.tt(out, out, g_bc, ALU.mult, eng="gpsimd")
    P.tt(out, out, b_bc, ALU.add)


def build_inproj(IN):
    P = Prog()
    h = P.dram("h", [TPC, D], F32, "ExternalInput")
    w = P.dram("w", [D, IN], F32, "ExternalInput")
    p = P.dram("p", [TPC, IN], F32, "ExternalOutput")
    pp = PsumPool(P, 4, 2)
    idb = make_ident(P, BF16)
    hT = P.sb([128, KC, TPC], BF16)
    load_hT(P, pp, idb, h, TPC // 128, hT)
    wv = w.rearrange("(kc p) n -> p kc n", p=128)
    nblk = (IN + 511) // 512
    for cb in range(nblk):
        c0 = cb * 512
        nc_ = min(512, IN - c0)
        wb = P.sb_rot("wblk", [128, KC, 512], BF16, 2)
        P.dma(wb[:, :, 0:nc_], wv[:, :, c0:c0 + nc_], q="gpsimd")
        for i in range(TPC // 128):
            ps = pp.nf()
            for kc in range(KC):
                P.matmul(ps[:, 0:nc_], hT[:, kc, i * 128:(i + 1) * 128], wb[:, kc, 0:nc_],
                         start=(kc == 0), stop=(kc == KC - 1))
            ob = P.sb_rot("ob", [128, 512], F32, 3)
            if i % 2 == 0:
                P.copy(ob[:, 0:nc_], ps[:, 0:nc_], eng="vector")
            else:
                P.copy(ob[:, 0:nc_], ps[:, 0:nc_], eng="scalar")
            P.dma(p[i * 128:(i + 1) * 128, c0:c0 + nc_], ob[:, 0:nc_], q="sync", is_output=True)
    return P.finish()


def build_p1(even):
    P = Prog()
    ymix = P.dram("ymix", [TPC, D], F32, "ExternalInput")
    hprev = P.dram("hprev", [TPC, D], F32, "ExternalInput")
    w = P.dram("w", [D, D], F32, "ExternalInput")
    lng = P.dram("lng", [128, D], F32, "ExternalInput")
    lnb = P.dram("lnb", [128, D], F32, "ExternalInput")
    if even:
        nw = P.dram("nw", [128, 1024], F32, "ExternalInput")
    hmid = P.dram("hmid", [TPC, D], F32, "ExternalOutput")
    pp = PsumPool(P, 4, 2)
    idb = make_ident(P, BF16)
    wb = P.sb([128, KC, D], BF16)
    wv = w.rearrange("(kc p) n -> p kc n", p=128)
    for c in range(4):
        P.dma(wb[:, :, c * 512:(c + 1) * 512], wv[:, :, c * 512:(c + 1) * 512], q="gpsimd")
    g_bc = P.sb([128, D], F32)
    b_bc = P.sb([128, D], F32)
    P.dma(g_bc, lng)
    P.dma(b_bc, lnb)
    if even:
        nw_bc = P.sb([128, 1024], F32)
        P.dma(nw_bc, nw)
    for i in range(TPC // 128):
        yf = P.sb_rot("yf", [128, D], F32, 2)
        P.dma(yf, ymix[i * 128:(i + 1) * 128, :])
        hp = P.sb_rot("hp", [128, D], F32, 2)
        P.dma(hp, hprev[i * 128:(i + 1) * 128, :])
        yb = P.sb_rot("yb", [128, D], BF16, 2)
        if even:
            ss = P.sb_rot("ss", [128, 2], F32, 2)
            junk = P.sb_rot("junk", [128, 512], F32, 1)
            for g in range(2):
                P.act(junk, yf[:, g * 512:(g + 1) * 512], AF.Square, accum_out=ss[:, g:g + 1])
            P.ts(ss, ss, 1.0 / 512.0, ALU.mult, LN_EPS, ALU.add)
            P.act(ss, ss, AF.Sqrt)
            P.recip(ss, ss)
            for g in range(2):
                P.stt(yb[:, g * 512:(g + 1) * 512], yf[:, g * 512:(g + 1) * 512], ss[:, g:g + 1],
                      nw_bc[:, g * 512:(g + 1) * 512], ALU.mult, ALU.mult)
            P.copy(yb[:, 1024:2048], yf[:, 1024:2048], eng="gpsimd")
        else:
            P.copy(yb[:, 0:1024], yf[:, 0:1024], eng="vector")
            P.copy(yb[:, 1024:2048], yf[:, 1024:2048], eng="gpsimd")
        yT = P.sb_rot("yT", [128, KC, 128], BF16, 2)
        for g in range(2):
            pt = pp.nb()
            for j in range(8):
                kc = g * 8 + j
                P.transpose(pt[:, j * 128:(j + 1) * 128], yb[:, kc * 128:(kc + 1) * 128], idb)
            if g == 0:
                P.copy(yT[:, 0:8, :], pt.re("p (j t) -> p j t", t=128), eng="vector")
            else:
                P.copy(yT[:, 8:16, :], pt.re("p (j t) -> p j t", t=128), eng="scalar")
        pre = P.sb_rot("pre", [128, D], F32, 2)
        for cb in range(4):
            ps = pp.nf()
            for kc in range(KC):
                P.matmul(ps, yT[:, kc, :], wb[:, kc, cb * 512:(cb + 1) * 512], start=(kc == 0), stop=(kc == KC - 1))
            P.stt(pre[:, cb * 512:(cb + 1) * 512], hp[:, cb * 512:(cb + 1) * 512], ALPHA, ps, ALU.mult, ALU.add)
        ot = P.sb_rot("ot", [128, D], F32, 2)
        layer_norm_tile(P, pre, ot, g_bc, b_bc)
        P.dma(hmid[i * 128:(i + 1) * 128, :], ot, is_output=True)
    return P.finish()


def build_p2():
    P = Prog()
    hmid = P.dram("hmid", [TPC, D], F32, "ExternalInput")
    haloT = P.dram("haloT", [D, 2], F32, "ExternalInput")
    wup = P.dram("wup", [D, 2 * FFN], F32, "ExternalInput")
    cw = P.dram("cw", [128, FCH, 4], F32, "ExternalInput")
    wdn = P.dram("wdn", [FFN, D], F32, "ExternalInput")
    lng = P.dram("lng", [128, D], F32, "ExternalInput")
    lnb = P.dram("lnb", [128, D], F32, "ExternalInput")
    hout = P.dram("hout", [TPC, D], F32, "ExternalOutput")
    pp = PsumPool(P, 6, 2)
    idb = make_ident(P, BF16)
    hT = P.sb([128, KC, 2 + TPC], BF16)
    P.dma(hT[:, :, 0:2], haloT.rearrange("(kc p) t -> p kc t", p=128), q="gpsimd")
    load_hT(P, pp, idb, hmid, TPC // 128, hT, col0=2)
    g_bc = P.sb([128, D], F32)
    b_bc = P.sb([128, D], F32)
    P.dma(g_bc, lng)
    P.dma(b_bc, lnb)
    cws = P.sb([128, FCH, 4], F32)
    P.dma(cws, cw)
    wupv = wup.rearrange("(kc p) n -> p kc n", p=128)
    wdnv = wdn.rearrange("(f p) n -> p f n", p=128)
    HB = 512
    gT = P.sb([128, FCH, HB], BF16)
    pre = [P.sb([128, D], F32) for _ in range(HB // 128)]
    for hf in range(TPC // HB):
        t0 = hf * HB
        for i in range(HB // 128):
            P.dma(pre[i], hmid[t0 + i * 128: t0 + (i + 1) * 128, :])
        for f in range(FCH):
            wa = P.sb_rot("wa", [128, KC, 128], BF16, 2)
            wu = P.sb_rot("wu", [128, KC, 128], BF16, 2)
            P.dma(wa, wupv[:, :, f * 128:(f + 1) * 128], q="gpsimd")
            P.dma(wu, wupv[:, :, FFN + f * 128: FFN + (f + 1) * 128], q="gpsimd")
            for b in range(HB // 256):
                c0 = t0 + b * 256
                pa = pp.nf()
                pu = pp.nf()
                for kc in range(KC):
                    P.matmul(pa[:, 0:258], wa[:, kc, :], hT[:, kc, c0:c0 + 258], start=(kc == 0), stop=(kc == KC - 1))
                for kc in range(KC):
                    P.matmul(pu[:, 0:256], wu[:, kc, :], hT[:, kc, c0 + 2:c0 + 258], start=(kc == 0), stop=(kc == KC - 1))
                ac = P.sb_rot("ac", [128, 256], F32, 2)
                P.act(ac, pa[:, 2:258], AF.Identity, bias=cws[:, f, 3:4], scale=cws[:, f, 2:3])
                P.stt(ac, pa[:, 1:257], cws[:, f, 1:2], ac, ALU.mult, ALU.add)
                P.stt(ac, pa[:, 0:256], cws[:, f, 0:1], ac, ALU.mult, ALU.add)
                sg = P.sb_rot("sg", [128, 256], F32, 2)
                P.act(sg, ac, AF.Silu)
                P.tt(gT[:, f, b * 256:(b + 1) * 256], sg, pu[:, 0:256], ALU.mult)
        for cb in range(D // 128):
            wd = P.sb_rot("wd", [128, FCH, 128], BF16, 2)
            P.dma(wd, wdnv[:, :, cb * 128:(cb + 1) * 128], q="gpsimd")
            for i in range(HB // 128):
                ps = pp.nf()
                for f in range(FCH):
                    P.matmul(ps[:, 0:128], gT[:, f, i * 128:(i + 1) * 128], wd[:, f, :], start=(f == 0), stop=(f == FCH - 1))
                P.stt(pre[i][:, cb * 128:(cb + 1) * 128], pre[i][:, cb * 128:(cb + 1) * 128], ALPHA, ps[:, 0:128],
                      ALU.mult, ALU.add)
        for i in range(HB // 128):
            ot = P.sb_rot("ot", [128, D], F32, 2)
            layer_norm_tile(P, pre[i], ot, g_bc, b_bc)
            P.dma(hout[t0 + i * 128: t0 + (i + 1) * 128, :], ot, is_output=True)
    return P.finish()


def host_cw(conv_w, conv_b):
    a = np.concatenate([conv_w, conv_b[None, :]], axis=0)
    return np.ascontiguousarray(a.reshape(4, FCH, 128).transpose(2, 1, 0)).astype(np.float32)


TZW = 2560


def flash_block(P, pp, O, QT_blk, nq, k_tiles, scale_bias_fn, pv_fn, n_out):
    nk = len(k_tiles)
    nsub = nq // 128
    for jj, kt in enumerate(k_tiles):
        ps = pp.nf()
        has_mask = kt.get("mask") is not None
        P.matmul(ps[:, 0:nq], kt["lhsT"], QT_blk, start=True, stop=not has_mask)
        if has_mask:
            ml, mr = kt["mask"]
            P.matmul(ps[:, 0:nq], ml, mr, start=False, stop=True)
        pt = P.sb_rot("fl_pt", [128, 512], BF16, 3)
        kind, bap = kt["bias"]
        if kind == "far":
            P.act(pt[:, 0:nq], ps[:, 0:nq], AF.Exp, bias=bap, scale=SCALE)
        else:
            tmp = P.sb_rot("fl_tmp", [128, 512], F32, 2)
            P.stt(tmp[:, 0:nq], ps[:, 0:nq], SCALE, bap, ALU.mult, ALU.add)
            P.act(pt[:, 0:nq], tmp[:, 0:nq], AF.Exp)
        for s in range(nsub):
            P.matmul(O[s][:, 0:n_out], pt[:, s * 128:(s + 1) * 128], pv_fn(jj), start=(jj == 0), stop=(jj == nk - 1))


def flash_finish(P, Os, dst_fn, ncols=128, extra=None):
    for s, O in enumerate(Os):
        den = P.sb_rot("fl_den", [128, 1], F32, 4)
        P.ts(den, O[:, 128:129], 1e-30, ALU.max)
        P.recip(den, den)
        P.ts(dst_fn(s), O[:, 0:ncols], den, ALU.mult)
        if extra is not None:
            extra(s, O, den)


def build_meven():
    P = Prog()
    z = P.dram("z", [T, 128], F32, "ExternalInput")
    xbcT = P.dram("xbcT", [3, 128, T + 3], F32, "ExternalInput")
    cwx = P.dram("cwx", [128, 3, 5], F32, "ExternalInput")
    dtr = P.dram("dtr", [T, 2], F32, "ExternalInput")
    hpar = P.dram("hpar", [128, 6], F32, "ExternalInput")
    qT = P.dram("qT", [128, T], F32, "ExternalInput")
    kT = P.dram("kT", [128, T], F32, "ExternalInput")
    v = P.dram("v", [T, 128], F32, "ExternalInput")
    tz = P.dram("tz", [128, TZW], F32, "ExternalInput")
    pastneg = P.dram("pastneg", [128, 64, 32], F32, "ExternalInput")
    ownneg = P.dram("ownneg", [128, 64, 32], F32, "ExternalInput")
    ejd = P.dram("ej", [32, 32, 128], F32, "ExternalInput")
    cst = P.dram("cst", [128, 3, 128], F32, "ExternalInput")
    y = P.dram("y", [T, 256], F32, "ExternalOutput")

    pp = PsumPool(P, 7, 1)
    idb = make_ident(P, BF16)
    idf = make_ident(P, F32)
    csts = P.sb([128, 3, 128], F32)
    P.dma(csts, cst)
    triu, ones, causneg = csts[:, 0, :], csts[:, 1, :], csts[:, 2, :]
    cw = P.sb([128, 3, 5], F32)
    P.dma(cw, cwx)
    hp = P.sb([128, 6], F32)
    P.dma(hp, hpar)
    a_bc = P.sb([128, 2], F32)
    P.act(a_bc, hp[:, 2:4], AF.Exp)
    P.ts(a_bc, a_bc, -1.0, ALU.mult)

    H = P.sb([128, 128], F32)
    Hb = P.sb([128, 128], BF16)
    P.memset(H, 0.0)
    P.memset(Hb, 0.0)
    for sc in range(T // 512):
        t0 = sc * 512
        fm = []
        for wch in range(3):
            raw = P.sb_rot("raw%d" % wch, [128, 515], F32, 2)
            P.dma(raw, xbcT[wch, :, t0:t0 + 515])
            acc = P.sb_rot("cacc%d" % wch, [128, 512], F32, 2)
            P.act(acc, raw[:, 3:515], AF.Identity, bias=cw[:, wch, 4:5], scale=cw[:, wch, 3:4])
            for k in (2, 1, 0):
                P.stt(acc, raw[:, k:k + 512], cw[:, wch, k:k + 1], acc, ALU.mult, ALU.add)
            if wch == 0:
                o = P.sb_rot("xs", [128, 512], F32, 2)
            else:
                o = P.sb_rot("bc%d" % wch, [128, 512], BF16, 2)
            P.act(o, acc, AF.Silu)
            fm.append(o)
        xs, BT, CT = fm
        zt = P.sb_rot("zt", [128, 4, 128], F32, 2)
        P.dma(zt, z[t0:t0 + 512, :].rearrange("(i p) d -> p i d", p=128))
        dtt = P.sb_rot("dtt", [128, 4, 2], F32, 2)
        P.dma(dtt, dtr[t0:t0 + 512, :].rearrange("(i p) d -> p i d", p=128))
        dt = P.sb_rot("dt", [128, 4, 2], F32, 2)
        for i in range(4):
            P.tt(dt[:, i, :], dtt[:, i, :], hp[:, 0:2], ALU.add)
        P.act(dt, dt, AF.Exp)
        P.act(dt, dt, AF.Ln, bias=1.0)
        dtA = P.sb_rot("dtA", [128, 4, 2], F32, 2)
        for i in range(4):
            P.tt(dtA[:, i, :], dt[:, i, :], a_bc, ALU.mult)
        sz = P.sb_rot("sz", [128, 4, 128], F32, 2)
        P.act(sz, zt, AF.Silu)
        for i in range(4):
            cs = slice(i * 128, (i + 1) * 128)
            pa = pp.nf()
            P.matmul(pa[:, 0:2], triu, dtA[:, i, :])
            P.matmul(pa[:, 2:4], ones, dtA[:, i, :])
            ac = P.sb_rot("ac", [128, 4], F32, 2)
            P.copy(ac, pa[:, 0:4])
            dec = P.sb_rot("dec", [128, 6], F32, 2)
            P.act(dec[:, 0:2], ac[:, 0:2], AF.Exp)
            P.tt(dec[:, 2:4], ac[:, 2:4], ac[:, 0:2], ALU.subtract)
            P.act(dec[:, 2:4], dec[:, 2:4], AF.Exp)
            P.act(dec[:, 4:6], ac[:, 2:4], AF.Exp)
            pS = pp.nf()
            P.matmul(pS[:, 0:128], BT[:, cs], CT[:, cs])
            SmT = []
            for h in range(2):
                dab = P.sb_rot("dab", [128, 128], F32, 2)
                P.copy(dab, V(dtA.ap[:, i, h:h + 1].to_broadcast([128, 128]), dtA.toks), eng="gpsimd")
                pb = pp.nf()
                P.matmul(pb[:, 0:128], dab, triu)
                dm = P.sb_rot("dm", [128, 128], F32, 2)
                P.stt(dm, pb[:, 0:128], ac[:, h:h + 1], causneg, ALU.subtract, ALU.add)
                P.act(dm, dm, AF.Exp)
                sm = P.sb_rot("smT%d" % h, [128, 128], BF16, 2)
                P.tt(sm, pS[:, 0:128], dm, ALU.mult)
                SmT.append(sm)
            px = pp.nf()
            P.transpose(px[:, 0:128], xs[:, cs], idf)
            xtok = P.sb_rot("xtok", [128, 128], F32, 2)
            P.copy(xtok, px[:, 0:128], eng="scalar")
            xdt = P.sb_rot("xdt", [128, 128], BF16, 2)
            vd = P.sb_rot("vd", [128, 128], BF16, 2)
            for h in range(2):
                hs = slice(h * 64, (h + 1) * 64)
                P.ts(xdt[:, hs], xtok[:, hs], dt[:, i, h:h + 1], ALU.mult)
                P.ts(vd[:, hs], xtok[:, hs], dt[:, i, h:h + 1], ALU.mult, dec[:, 2 + h:3 + h], ALU.mult)
            pbt = pp.nb()
            P.transpose(pbt[:, 0:128], BT[:, cs], idb)
            btok = P.sb_rot("btok", [128, 128], BF16, 2)
            P.copy(btok, pbt[:, 0:128], eng="scalar")
            pyd = pp.nf()
            for h in range(2):
                hs = slice(h * 64, (h + 1) * 64)
                P.matmul(pyd[:, hs], SmT[h], xdt[:, hs])
            pyo = pp.nf()
            P.matmul(pyo[:, 0:128], CT[:, cs], Hb)
            ph = pp.nf()
            P.matmul(ph[:, 0:128], btok, vd)
            yt = P.sb_rot("yt", [128, 128], F32, 2)
            P.copy(yt, pyd[:, 0:128], eng="scalar")
            for h in range(2):
                hs = slice(h * 64, (h + 1) * 64)
                P.stt(yt[:, hs], pyo[:, hs], dec[:, h:h + 1], yt[:, hs], ALU.mult, ALU.add)
                P.stt(yt[:, hs], xtok[:, hs], hp[:, 4 + h:5 + h], yt[:, hs], ALU.mult, ALU.add)
            yo = P.sb_rot("yo", [128, 128], F32, 2)
            P.tt(yo, yt, sz[:, i, :], ALU.mult, eng="gpsimd")
            P.dma(y[t0 + i * 128: t0 + (i + 1) * 128, 0:128], yo, is_output=True)
            for h in range(2):
                hs = slice(h * 64, (h + 1) * 64)
                P.stt(H[:, hs], H[:, hs], dec[:, 4 + h:5 + h], ph[:, hs], ALU.mult, ALU.add)
            P.copy(Hb, H, eng="gpsimd")

    QT = P.sb([128, T], BF16)
    KT = P.sb([128, T], BF16)
    V1 = P.sb([128, T // 128, 129], BF16)
    for c in range(4):
        P.dma(QT[:, c * 2048:(c + 1) * 2048], qT[:, c * 2048:(c + 1) * 2048], q="gpsimd")
        P.dma(KT[:, c * 2048:(c + 1) * 2048], kT[:, c * 2048:(c + 1) * 2048], q="gpsimd")
        P.dma(V1[:, c * 16:(c + 1) * 16, 0:128], v[c * 2048:(c + 1) * 2048, :].rearrange("(j p) d -> p j d", p=128), q="gpsimd")
    P.memset(V1[:, :, 128:129], 1.0)
    tzs = P.sb([128, TZW], F32)
    P.dma(tzs, tz)
    pn = P.sb([128, 64, 32], F32)
    on = P.sb([128, 64, 32], F32)
    P.dma(pn, pastneg)
    P.dma(on, ownneg)
    EJ = P.sb([32, 32, 128], BF16)
    P.dma(EJ, ejd, q="gpsimd")
    kmT = P.sb([128, 32], F32)
    for c in range(4):
        kf = P.sb_rot("kf", [128, 2048], F32, 2)
        P.dma(kf, kT[:, c * 2048:(c + 1) * 2048])
        P.reduce(kmT[:, c * 8:(c + 1) * 8], kf.re("p (b t) -> p b t", t=256), ALU.add)
    P.ts(kmT, kmT, 1.0 / 256.0, ALU.mult)
    negT = P.sb([32, T], BF16)
    for i in range(T // 128):
        qf = P.sb_rot("qf", [128, 128], F32, 3)
        P.dma(qf, qT[:, i * 128:(i + 1) * 128])
        pg = pp.nf()
        P.matmul(pg[:, 0:32], qf, kmT)
        gm = P.sb_rot("gm", [128, 32], F32, 2)
        P.tt(gm, pg[:, 0:32], pn[:, i, :], ALU.add)
        m8 = P.sb_rot("m8", [128, 8], F32, 2)
        P.max8(m8, gm)
        thr = P.sb_rot("thr", [128, 1], F32, 2)
        P.ts(thr, m8[:, 2:3], -1e29, ALU.max)
        P.ts(gm, gm, thr, ALU.is_ge)
        ng = P.sb_rot("ng", [128, 32], F32, 2)
        P.stt(ng, gm, -NEGBIG, on[:, i, :], ALU.mult, ALU.add)
        pt = pp.nf()
        P.transpose(pt[0:32, 0:128], ng, idf)
        P.copy(negT[:, i * 128:(i + 1) * 128], pt[0:32, 0:128], eng="scalar")
    O = [pp.f[k] for k in range(4)]
    pp.f = pp.f[4:]
    f31 = P.sb([128, 1], F32)
    P.copy(f31, tzs[:, TZW - 1:TZW])
    for Q in range(T // 512):
        t0 = Q * 512
        kts = []
        for j in range(4 * Q + 4):
            off = t0 - j * 128
            if off >= 1664:
                b = ("far", f31)
            else:
                b = ("near", tzs[:, off + 384: off + 384 + 512])
            kts.append({"lhsT": KT[:, j * 128:(j + 1) * 128],
                        "mask": (EJ[:, j // 2, :], negT[:, t0:t0 + 512]),
                        "bias": b})
        flash_block(P, pp, O, QT[:, t0:t0 + 512], 512, kts, None, lambda jj: V1[:, jj, :], 129)
        yq = P.sb_rot("yq", [128, 4, 128], F32, 2)
        flash_finish(P, O, lambda s: yq[:, s, :])
        P.dma(y[t0:t0 + 512, 128:256].rearrange("(s p) d -> p s d", p=128), yq, is_output=True)
    return P.finish()


def rel_bucket_np(dist):
    n = np.maximum(dist, 0)
    max_exact = 16
    nf = np.maximum(n, max_exact).astype(np.float32)
    large = max_exact + (np.log(nf / np.float32(max_exact)) / np.float32(np.log(2048 / max_exact))
                         * np.float32(32 - max_exact)).astype(np.int32)
    large = np.minimum(large, 31)
    return np.where(n < max_exact, n, large)


def host_tz(rel_bias_h, m_lo, width, stride_p, base, band=None):
    p = np.arange(128)[:, None]
    m = np.arange(width)[None, :] + m_lo
    dist = m - stride_p * p - base
    tab = rel_bias_h[rel_bucket_np(dist)].astype(np.float32)
    bad = dist < 0
    if band is not None:
        bad = bad | (dist >= band)
    return np.where(bad, np.float32(-1e30), tab).astype(np.float32)


def host_meven_inputs(p, c, conv_w, conv_b, dt_bias, a_log, d_skip, rel_bias):
    g = c // 4
    f32 = np.float32
    def padT(a):
        o = np.zeros((128, T + 3), f32)
        o[:, 3:] = a.T
        return o
    xb = p[:, 1024:2560]
    chans = [np.arange(128 * c, 128 * c + 128), 1024 + np.arange(128 * g, 128 * g + 128),
             1280 + np.arange(128 * g, 128 * g + 128)]
    xbcT = np.stack([padT(xb[:, ch]) for ch in chans])
    cwx = np.stack([np.concatenate([conv_w[:, ch], conv_b[None, ch]], 0).T for ch in chans], axis=1).astype(f32)
    hpar = np.array([dt_bias[2 * c], dt_bias[2 * c + 1], a_log[2 * c], a_log[2 * c + 1], d_skip[2 * c], d_skip[2 * c + 1]], f32)
    i = np.arange(64)[:, None]
    n = np.arange(32)[None, :]
    pastneg = np.where(n < i // 2, 0.0, -1e30).astype(f32)
    ownneg = np.where(n == i // 2, 0.0, NEGBIG).astype(f32)
    ej = np.zeros((32, 32, 128), f32)
    for J in range(32):
        ej[J, J, :] = 1.0
    k = np.arange(128)[:, None]
    l = np.arange(128)[None, :]
    cst = np.stack([(k <= l).astype(f32), np.ones((128, 128), f32), np.where(l >= k, 0.0, -1e30).astype(f32)], axis=1)
    return {
        "z": np.ascontiguousarray(p[:, 128 * c:128 * c + 128]),
        "xbcT": xbcT, "cwx": np.ascontiguousarray(cwx),
        "dtr": np.ascontiguousarray(p[:, 2560 + 2 * c: 2560 + 2 * c + 2]),
        "hpar": np.ascontiguousarray(np.broadcast_to(hpar, (128, 6))),
        "qT": np.ascontiguousarray(p[:, 2576 + 128 * c: 2576 + 128 * c + 128].T),
        "kT": np.ascontiguousarray(p[:, 3600 + 128 * c: 3600 + 128 * c + 128].T),
        "v": np.ascontiguousarray(p[:, 4624 + 128 * c: 4624 + 128 * c + 128]),
        "tz": host_tz(rel_bias[:, c], -384, TZW, 1, 0),
        "pastneg": np.ascontiguousarray(np.broadcast_to(pastneg, (128, 64, 32))),
        "ownneg": np.ascontiguousarray(np.broadcast_to(ownneg, (128, 64, 32))),
        "ej": ej, "cst": np.ascontiguousarray(cst),
    }


TCW = 3600
TWW = 1408
GELU_C = 1.5957691216057308


def build_modd_a(parts=("cmp", "att", "ret")):
    P = Prog()
    qT = P.dram("qT", [128, T], F32, "ExternalInput")
    kvcT = P.dram("kvcT", [2, 128, T], F32, "ExternalInput")
    w1 = P.dram("w1", [2, 4096, 128], F32, "ExternalInput")
    w2 = P.dram("w2", [2, 128, 128], F32, "ExternalInput")
    peT = P.dram("peT", [2, 128, 32], F32, "ExternalInput")
    tc = P.dram("tc", [128, TCW], F32, "ExternalInput")
    ovl = P.dram("ovl", [128, 4, 128], F32, "ExternalInput")
    rq4T = P.dram("rq4T", [4, 128, T], F32, "ExternalInput")
    csT = P.dram("csT", [2, 128, T], F32, "ExternalInput")
    rv = P.dram("rv", [T, 128], F32, "ExternalInput")
    rg = P.dram("rg", [T, 128], F32, "ExternalInput")
    rdec = P.dram("rdec", [128, 4], F32, "ExternalInput")
    drt = P.dram("drt", [128, 128], F32, "ExternalInput")
    ocmp = P.dram("ocmp", [T, 128], F32, "ExternalOutput")
    imp = P.dram("imp", [T, 128], F32, "ExternalOutput")
    yret = P.dram("yret", [T, 128], F32, "ExternalOutput")

    pp = PsumPool(P, 7, 1)
    idb = make_ident(P, BF16)

    KcT = P.sb([128, 512], BF16)
    VO = P.sb([128, 4, 257], BF16)
    P.memset(VO[:, :, 128:129], 1.0)
    P.dma(VO[:, :, 129:257], ovl, q="gpsimd")
    if "cmp" in parts:
        for kind in range(2):
            W1 = P.sb_rot("W1", [128, 32, 128], BF16, 1)
            P.dma(W1, w1[kind].rearrange("(l d) e -> d l e", d=128), q="gpsimd")
            W2 = P.sb_rot("W2", [128, 128], BF16, 1)
            P.dma(W2, w2[kind], q="gpsimd")
            pe = P.sb_rot("pe", [128, 32], BF16, 1)
            P.dma(pe, peT[kind], q="gpsimd")
            XT = P.sb_rot("XT", [128, T], BF16, 1)
            for c in range(4):
                P.dma(XT[:, c * 2048:(c + 1) * 2048], kvcT[kind, :, c * 2048:(c + 1) * 2048], q="gpsimd")
            pc = pp.nf()
            for l in range(32):
                P.matmul(pc[:, 0:1], W1[:, l, :], pe[:, l:l + 1], start=(l == 0), stop=(l == 31))
            cvec = P.sb_rot("cvec", [128, 1], F32, 1)
            P.copy(cvec, pc[:, 0:1])
            ph = pp.nf()
            for l in range(32):
                P.matmul(ph[:, 0:511], W1[:, l, :], XT[:, l:l + 8161:16], start=(l == 0), stop=(l == 31))
            u = P.sb_rot("cu", [128, 511], F32, 1)
            P.act(u, ph[:, 0:511], AF.Identity, bias=cvec)
            wk = P.sb_rot("cw_", [128, 511], F32, 1)
            P.tt(wk, u, u, ALU.mult)
            P.ts(wk, wk, 0.044715, ALU.mult, 1.0, ALU.add)
            P.tt(wk, wk, u, ALU.mult)
            P.act(wk, wk, AF.Sigmoid, scale=GELU_C)
            hid = P.sb_rot("hid", [128, 512], BF16, 1)
            P.memset(hid[:, 511:512], 0.0)
            P.tt(hid[:, 0:511], u, wk, ALU.mult)
            if kind == 0:
                pk = pp.nf()
                P.matmul(pk, W2, hid)
                P.copy(KcT, pk)
            else:
                for jt in range(4):
                    pv = pp.nf()
                    P.matmul(pv[:, 0:128], hid[:, jt * 128:(jt + 1) * 128], W2)
                    P.copy(VO[:, jt, 0:128], pv[:, 0:128])

    if "att" in parts:
        QT = P.sb([128, T], BF16)
        for c in range(4):
            P.dma(QT[:, c * 2048:(c + 1) * 2048], qT[:, c * 2048:(c + 1) * 2048], q="gpsimd")
        tcs = P.sb([128, TCW], F32)
        P.dma(tcs, tc)
        O = [pp.f[k] for k in range(4)]
        pp.f = pp.f[4:]
        fconst = P.sb([128, 1], F32)
        P.copy(fconst, tcs[:, TCW - 1:TCW])
        qlim = [int(x[1:]) for x in parts if x[0] == "q" and x[1:].isdigit()]
        for Q in range(qlim[0] if qlim else T // 512):
            t0 = Q * 512
            kts = []
            for jt in range(t0 // 2048 + 1):
                m0 = t0 - 2048 * jt
                if m0 >= 3584:
                    b = ("far", fconst)
                else:
                    b = ("near", tcs[:, m0:m0 + 512])
                kts.append({"lhsT": KcT[:, jt * 128:(jt + 1) * 128], "mask": None, "bias": b})
            NO_ = 257 if "n129" not in parts else 129
            flash_block(P, pp, O, QT[:, t0:t0 + 512], 512, kts, None, lambda jj: VO[:, jj, 0:NO_], NO_)
            oc = P.sb_rot("oc", [128, 4, 128], F32, 2)
            im = P.sb_rot("im", [128, 4, 128], F32, 2)

            def extra(s, Ot, den, im=im):
                if "n129" in parts:
                    P.ts(im[:, s, :], Ot[:, 0:128], den, ALU.mult)
                else:
                    P.ts(im[:, s, :], Ot[:, 129:257], den, ALU.mult)
            flash_finish(P, O, lambda s: oc[:, s, :], extra=extra)
            P.dma(ocmp[t0:t0 + 512, :].rearrange("(s p) d -> p s d", p=128), oc, is_output=True)
            P.dma(imp[t0:t0 + 512, :].rearrange("(s p) d -> p s d", p=128), im, is_output=True)
        pp.f = O + pp.f

    if "ret" in parts:
        rd = P.sb([128, 4], F32)
        P.dma(rd, rdec)
        DRT = P.sb([128, 128], F32)
        P.dma(DRT, drt)
        R = P.sb([128, 128], F32)
        Rb = P.sb([128, 128], BF16)
        P.memset(R, 0.0)
        P.memset(Rb, 0.0)
        for sc in range(T // 512):
            t0 = sc * 512
            rot = []
            cs_ = []
            for w in range(2):
                tbl = P.sb_rot("cs%d" % w, [128, 512], F32, 2)
                P.dma(tbl, csT[w, :, t0:t0 + 512])
                cs_.append(tbl)
            for w in range(2):
                a = P.sb_rot("rqa%d" % w, [128, 512], F32, 2)
                b = P.sb_rot("rqb%d" % w, [128, 512], F32, 2)
                P.dma(a, rq4T[2 * w, :, t0:t0 + 512])
                P.dma(b, rq4T[2 * w + 1, :, t0:t0 + 512])
                P.tt(a, a, cs_[0], ALU.mult)
                P.tt(b, b, cs_[1], ALU.mult, eng="gpsimd")
                o = P.sb_rot("rot%d" % w, [128, 512], BF16, 2)
                P.tt(o, a, b, ALU.add)
                rot.append(o)
            QrT, KrT = rot
            vt = P.sb_rot("rvt", [128, 4, 128], F32, 2)
            P.dma(vt, rv[t0:t0 + 512, :].rearrange("(i p) d -> p i d", p=128))
            vb = P.sb_rot("rvb", [128, 4, 128], BF16, 2)
            P.copy(vb, vt, eng="gpsimd")
            vd = P.sb_rot("rvd", [128, 4, 128], BF16, 2)
            P.ts(vd, vt, rd[:, 1:2], ALU.mult)
            gt = P.sb_rot("rgt", [128, 4, 128], F32, 2)
            P.dma(gt, rg[t0:t0 + 512, :].rearrange("(i p) d -> p i d", p=128))
            P.act(gt, gt, AF.Silu)
            yo4 = P.sb_rot("ryo", [128, 4, 128], F32, 2)
            for i in range(4):
                cs = slice(i * 128, (i + 1) * 128)
                pS = pp.nf()
                P.matmul(pS[:, 0:128], KrT[:, cs], QrT[:, cs])
                sm = P.sb_rot("rsm", [128, 128], BF16, 2)
                P.tt(sm, pS[:, 0:128], DRT, ALU.mult)
                pkt = pp.nb()
                P.transpose(pkt[:, 0:128], KrT[:, cs], idb)
                ktok = P.sb_rot("rktok", [128, 128], BF16, 2)
                P.copy(ktok, pkt[:, 0:128], eng="scalar")
                pY = pp.nf()
                P.matmul(pY[:, 0:128], sm, vb[:, i, :])
                pYo = pp.nf()
                P.matmul(pYo[:, 0:128], QrT[:, cs], Rb)
                pR = pp.nf()
                P.matmul(pR[:, 0:128], ktok, vd[:, i, :])
                yt = P.sb_rot("ryt", [128, 128], F32, 2)
                P.copy(yt, pY[:, 0:128], eng="scalar")
                P.stt(yt, pYo[:, 0:128], rd[:, 0:1], yt, ALU.mult, ALU.add)
                st = P.sb_rot("rst", [128, 6], F32, 2)
                P.bn_stats(st, yt)
                mv = P.sb_rot("rmv", [128, 2], F32, 2)
                P.bn_aggr(mv, st)
                rs = P.sb_rot("rrs", [128, 1], F32, 2)
                P.ts(rs, mv[:, 1:2], LN_EPS, ALU.add)
                P.act(rs, rs, AF.Sqrt)
                P.recip(rs, rs)
                P.ts(yt, yt, mv[:, 0:1], ALU.subtract, rs, ALU.mult)
                P.tt(yo4[:, i, :], yt, gt[:, i, :], ALU.mult, eng="gpsimd")
                P.stt(R, R, rd[:, 2:3], pR[:, 0:128], ALU.mult, ALU.add)
                P.copy(Rb, R, eng="gpsimd")
            P.dma(yret[t0:t0 + 512, :].rearrange("(i p) d -> p i d", p=128), yo4, is_output=True)
    return P.finish()


def build_modd_b():
    P = Prog()
    qT = P.dram("qT", [128, T], F32, "ExternalInput")
    kswT = P.dram("kswT", [2, 128, T], F32, "ExternalInput")
    vsw = P.dram("vsw", [2, T, 128], F32, "ExternalInput")
    gates = P.dram("gates", [T, 3], F32, "ExternalInput")
    imp4 = P.dram("imp4", [4, T, 128], F32, "ExternalInput")
    ocmp = P.dram("ocmp", [T, 128], F32, "ExternalInput")
    tz = P.dram("tz", [128, TZW], F32, "ExternalInput")
    tw = P.dram("tw", [128, TWW], F32, "ExternalInput")
    addmask = P.dram("addmask", [128, 64, 128], F32, "ExternalInput")
    eseld = P.dram("esel", [128, 64, 128], F32, "ExternalInput")
    y = P.dram("y", [T, 128], F32, "ExternalOutput")

    pp = PsumPool(P, 8, 0)
    idf = make_ident(P, F32)
    QT = P.sb([128, T], BF16)
    KsT = P.sb([128, T], BF16)
    KwT = P.sb([128, T], BF16)
    Vs1 = P.sb([128, 64, 129], BF16)
    Vw1 = P.sb([128, 64, 129], BF16)
    for c in range(4):
        sl = slice(c * 2048, (c + 1) * 2048)
        P.dma(QT[:, sl], qT[:, sl], q="gpsimd")
        P.dma(KsT[:, sl], kswT[0, :, sl], q="gpsimd")
        P.dma(KwT[:, sl], kswT[1, :, sl], q="gpsimd")
        P.dma(Vs1[:, c * 16:(c + 1) * 16, 0:128], vsw[0, sl, :].rearrange("(j p) d -> p j d", p=128), q="gpsimd")
        P.dma(Vw1[:, c * 16:(c + 1) * 16, 0:128], vsw[1, sl, :].rearrange("(j p) d -> p j d", p=128), q="gpsimd")
    P.memset(Vs1[:, :, 128:129], 1.0)
    P.memset(Vw1[:, :, 128:129], 1.0)
    tzs = P.sb([128, TZW], F32)
    P.dma(tzs, tz)
    tws = P.sb([128, TWW], F32)
    P.dma(tws, tw)
    ES = P.sb([128, 64, 128], BF16)
    for c in range(4):
        P.dma(ES[:, c * 16:(c + 1) * 16, :], eseld[:, c * 16:(c + 1) * 16, :], q="gpsimd")
    sig = P.sb([128, 64, 3], F32)
    P.dma(sig, gates.rearrange("(i p) d -> p i d", p=128))
    P.act(sig, sig, AF.Sigmoid)
    negT = P.sb([128, T], BF16)
    for i in range(T // 128):
        im4 = P.sb_rot("im4", [128, 4, 128], F32, 2)
        P.dma(im4, imp4[:, i * 128:(i + 1) * 128, :].rearrange("g t j -> t g j"))
        am = P.sb_rot("am", [128, 128], F32, 2)
        P.dma(am, addmask[:, i, :])
        sc = P.sb_rot("sc", [128, 128], F32, 2)
        P.tt(sc, im4[:, 0, :], im4[:, 1, :], ALU.add)
        P.tt(sc, sc, im4[:, 2, :], ALU.add)
        P.tt(sc, sc, im4[:, 3, :], ALU.add)
        P.tt(sc, sc, am, ALU.add)
        m8 = P.sb_rot("m8", [128, 16], F32, 2)
        wk = P.sb_rot("wk", [128, 128], F32, 2)
        P.max8(m8[:, 0:8], sc)
        P.match_replace(wk, m8[:, 0:8], sc, -1e30)
        P.max8(m8[:, 8:16], wk)
        thr = P.sb_rot("thr", [128, 1], F32, 2)
        P.ts(thr, m8[:, 15:16], -1e29, ALU.max)
        P.ts(sc, sc, thr, ALU.is_ge)
        P.ts(sc, sc, -NEGBIG, ALU.mult, NEGBIG, ALU.add)
        pt = pp.nf()
        P.transpose(pt[:, 0:128], sc, idf)
        P.copy(negT[:, i * 128:(i + 1) * 128], pt[:, 0:128], eng="scalar")
    O = [pp.f[k] for k in range(4)]
    pp.f = pp.f[4:]
    f31 = P.sb([128, 1], F32)
    P.copy(f31, tzs[:, TZW - 1:TZW])
    for Q in range(T // 512):
        t0 = Q * 512
        kts = []
        for j in range(4 * Q + 4):
            off = t0 - j * 128
            if off >= 1664:
                b = ("far", f31)
            else:
                b = ("near", tzs[:, off + 384: off + 384 + 512])
            kts.append({"lhsT": KsT[:, j * 128:(j + 1) * 128], "mask": (ES[:, j, :], negT[:, t0:t0 + 512]), "bias": b})
        flash_block(P, pp, O, QT[:, t0:t0 + 512], 512, kts, None, lambda jj: Vs1[:, jj, :], 129)
        osel = P.sb_rot("osel", [128, 4, 128], F32, 2)
        flash_finish(P, O, lambda s: osel[:, s, :])
        j0 = max(0, 4 * Q - 4)
        kts = []
        for j in range(j0, 4 * Q + 4):
            off = t0 - j * 128
            kts.append({"lhsT": KwT[:, j * 128:(j + 1) * 128], "mask": None,
                        "bias": ("near", tws[:, off + 384: off + 384 + 512])})
        flash_block(P, pp, O, QT[:, t0:t0 + 512], 512, kts, None, lambda jj, j0=j0: Vw1[:, j0 + jj, :], 129)
        owin = P.sb_rot("owin", [128, 4, 128], F32, 2)
        flash_finish(P, O, lambda s: owin[:, s, :])
        oc = P.sb_rot("occ", [128, 4, 128], F32, 2)
        P.dma(oc, ocmp[t0:t0 + 512, :].rearrange("(s p) d -> p s d", p=128))
        yo = P.sb_rot("yo", [128, 4, 128], F32, 2)
        for s in range(4):
            ti = Q * 4 + s
            P.ts(yo[:, s, :], oc[:, s, :], sig[:, ti, 0:1], ALU.mult, eng="gpsimd")
            P.stt(yo[:, s, :], osel[:, s, :], sig[:, ti, 1:2], yo[:, s, :], ALU.mult, ALU.add)
            P.stt(yo[:, s, :], owin[:, s, :], sig[:, ti, 2:3], yo[:, s, :], ALU.mult, ALU.add)
        P.dma(y[t0:t0 + 512, :].rearrange("(s p) d -> p s d", p=128), yo, is_output=True)
    return P.finish()


def host_modd_consts():
    f32 = np.float32
    n = (np.arange(4)[None, :, None] * 128 + np.arange(128)[:, None, None])
    j = np.arange(128)[None, None, :]
    ovl = ((16 * n < 64 * j + 64) & (16 * n + 32 > 64 * j) & (n < 511)).astype(f32)
    t = np.arange(T)[:, None]
    jj = np.arange(128)[None, :]
    cur = t // 64
    forced = (jj == 0) | (jj == cur) | (jj == cur - 1)
    am = np.where(forced, 100.0, np.where(jj <= cur, 0.0, -1e30)).astype(f32)
    addmask = np.ascontiguousarray(am.reshape(64, 128, 128).transpose(1, 0, 2))
    b = np.arange(128)[:, None, None]
    jt = np.arange(64)[None, :, None]
    k = np.arange(128)[None, None, :]
    esel = (b == 2 * jt + k // 64).astype(f32)
    inv = (1.0 / (np.float32(10000.0) ** (np.arange(0, 128, 2, dtype=f32) / np.float32(128)))).astype(f32)
    ang = (np.arange(T, dtype=f32)[:, None] * inv[None, :]).astype(f32)
    cos = np.cos(ang).astype(f32)
    sin = np.sin(ang).astype(f32)
    d = np.arange(128)
    sgn = np.where(d % 2 == 0, -1.0, 1.0).astype(f32)
    C = np.ascontiguousarray(cos[:, d // 2].T)
    S = np.ascontiguousarray((sin[:, d // 2] * sgn[None, :]).T)
    csT = np.stack([C, S]).astype(f32)
    return {"ovl": ovl, "addmask": addmask, "esel": esel, "csT": csT}


def host_ret_consts(hh):
    f32 = np.float32
    log_g = np.log(f32(1.0) - f32(2.0) ** (f32(-5.0) - f32(hh))).astype(f32)
    s = np.arange(128)[:, None]
    l = np.arange(128)[None, :]
    drt = np.where(l >= s, SCALE * np.exp(log_g * np.maximum(l - s, 0)), 0.0).astype(f32)
    i = np.arange(128)
    rdec = np.stack([np.exp(log_g * (i + 1)), SCALE * np.exp(log_g * (127 - i)),
                     np.full(128, np.exp(log_g * 128)), np.zeros(128)], axis=1).astype(f32)
    return drt, rdec


def host_modd_a_inputs(p, c, cmp_pe, cmp_w1, cmp_w2, rel_bias, consts):
    kvh = c // 4
    f32 = np.float32
    def colsT(c0):
        return np.ascontiguousarray(p[:, c0:c0 + 128].T)
    perm = np.arange(128) ^ 1
    rqT = colsT(2584 + 128 * c)
    rkT = colsT(3608 + 128 * c)
    drt, rdec = host_ret_consts(c)
    return {
        "qT": colsT(128 * c),
        "kvcT": np.stack([colsT(1024 + 128 * kvh), colsT(1280 + 128 * kvh)]),
        "w1": np.ascontiguousarray(cmp_w1), "w2": np.ascontiguousarray(cmp_w2),
        "peT": np.ascontiguousarray(cmp_pe.transpose(0, 2, 1)),
        "tc": host_tz(rel_bias[:, c], 0, TCW, 16, 31),
        "ovl": consts["ovl"],
        "rq4T": np.stack([rqT, rqT[perm], rkT, rkT[perm]]),
        "csT": consts["csT"],
        "rv": np.ascontiguousarray(p[:, 4632 + 128 * c: 4632 + 128 * c + 128]),
        "rg": np.ascontiguousarray(p[:, 5656 + 128 * c: 5656 + 128 * c + 128]),
        "rdec": rdec, "drt": drt,
    }


def host_modd_b_inputs(p, c, imps, ocmp_c, rel_bias, consts):
    kvh = c // 4
    def colsT(c0):
        return np.ascontiguousarray(p[:, c0:c0 + 128].T)
    return {
        "qT": colsT(128 * c),
        "kswT": np.stack([colsT(1536 + 128 * kvh), colsT(2048 + 128 * kvh)]),
        "vsw": np.stack([p[:, 1792 + 128 * kvh: 1792 + 128 * kvh + 128], p[:, 2304 + 128 * kvh: 2304 + 128 * kvh + 128]]),
        "gates": np.ascontiguousarray(p[:, 2560 + 3 * c: 2560 + 3 * c + 3]),
        "imp4": np.stack([imps[4 * kvh + g] for g in range(4)]),
        "ocmp": ocmp_c,
        "tz": host_tz(rel_bias[:, c], -384, TZW, 1, 0),
        "tw": host_tz(rel_bias[:, c], -384, TWW, 1, 0, band=512),
        "addmask": consts["addmask"], "esel": consts["esel"],
    }


def _run(nc, maps):
    res = run_bass_kernel_spmd(nc, maps, core_ids=list(range(NCORE)))
    return res.results


def _bc(v, n=128):
    return np.ascontiguousarray(np.broadcast_to(np.asarray(v, np.float32), (n, v.shape[-1])))


def kernel(x, rel_bias, ev_w_in, ev_conv_w, ev_conv_b, ev_dt_bias, ev_a_log, ev_d_skip, ev_norm_w, ev_w_out,
           od_w_in, od_cmp_pe, od_cmp_w1, od_cmp_w2, od_w_out,
           ffn_w_up, ffn_conv_w, ffn_conv_b, ffn_w_down, ln_g, ln_b):
    f32 = np.float32
    A = lambda a: np.ascontiguousarray(np.asarray(a, f32))
    rel_bias = A(rel_bias)
    h = A(x)[0]
    consts = host_modd_consts()
    rows = [slice(c * TPC, (c + 1) * TPC) for c in range(NCORE)]
    depth = ln_g.shape[0]
    for layer in range(depth):
        i = layer // 2
        even = layer % 2 == 0
        w_in = A(ev_w_in[i] if even else od_w_in[i])
        IN = w_in.shape[1]
        res = _run(build_inproj(IN), [{"h": h[rows[c]], "w": w_in} for c in range(NCORE)])
        p = np.concatenate([r["p"] for r in res], 0)
        del res
        if even:
            res = _run(build_meven(), [host_meven_inputs(p, c, A(ev_conv_w[i]), A(ev_conv_b[i]), A(ev_dt_bias[i]),
                                                         A(ev_a_log[i]), A(ev_d_skip[i]), rel_bias) for c in range(NCORE)])
            ymix = np.concatenate([r["y"][:, 0:128] for r in res] + [r["y"][:, 128:256] for r in res], 1)
            w_out = A(ev_w_out[i])
        else:
            ra = _run(build_modd_a(), [host_modd_a_inputs(p, c, A(od_cmp_pe[i]), A(od_cmp_w1[i]), A(od_cmp_w2[i]),
                                                          rel_bias, consts) for c in range(NCORE)])
            imps = [r["imp"] for r in ra]
            rb = _run(build_modd_b(), [host_modd_b_inputs(p, c, imps, ra[c]["ocmp"], rel_bias, consts)
                                       for c in range(NCORE)])
            ymix = np.concatenate([r["y"] for r in rb] + [r["yret"] for r in ra], 1)
            w_out = A(od_w_out[i])
        ymix = np.ascontiguousarray(ymix)
        lng, lnb = _bc(A(ln_g[layer, 0])), _bc(A(ln_b[layer, 0]))
        maps = []
        for c in range(NCORE):
            m = {"ymix": ymix[rows[c]], "hprev": h[rows[c]], "w": w_out, "lng": lng, "lnb": lnb}
            if even:
                m["nw"] = _bc(A(ev_norm_w[i]))
            maps.append(m)
        res = _run(build_p1(even), maps)
        hm = np.concatenate([r["hmid"] for r in res], 0)
        lng, lnb = _bc(A(ln_g[layer, 1])), _bc(A(ln_b[layer, 1]))
        cw = host_cw(A(ffn_conv_w[layer]), A(ffn_conv_b[layer]))
        wup, wdn = A(ffn_w_up[layer]), A(ffn_w_down[layer])
        maps = []
        for c in range(NCORE):
            halo = hm[c * TPC - 2:c * TPC] if c > 0 else np.zeros((2, D), f32)
            maps.append({"hmid": hm[rows[c]], "haloT": np.ascontiguousarray(halo.T), "wup": wup, "cw": cw,
                         "wdn": wdn, "lng": lng, "lnb": lnb})
        res = _run(build_p2(), maps)
        h = np.concatenate([r["hout"] for r in res], 0)
    return h[None].astype(f32)
```

```python
from contextlib import ExitStack
import numpy as np
import concourse.bass as bass
import concourse.mybir as mybir
from concourse.bass_utils import run_bass_kernel_spmd

F32 = mybir.dt.float32
BF16 = mybir.dt.bfloat16
ALU = mybir.AluOpType
AF = mybir.ActivationFunctionType
AX = mybir.AxisListType


class Tok:
    __slots__ = ("w", "rd")

    def __init__(self):
        self.w = None
        self.rd = {}


class V:
    __slots__ = ("ap", "toks")

    def __init__(self, ap, toks):
        self.ap = ap
        self.toks = toks

    def __getitem__(self, idx):
        return V(self.ap[idx], self.toks)

    def re(self, pat, **kw):
        return V(self.ap.rearrange(pat, **kw), self.toks)


def _ap(x):
    return x.ap if isinstance(x, V) else x


def _toks(xs):
    out = []
    for x in xs:
        if isinstance(x, V):
            out.extend(x.toks)
        elif isinstance(x, Tok):
            out.append(x)
    return out


NDEV = 8


class Prog:
    ENGS = ("tensor", "vector", "scalar", "gpsimd", "sync")
    NDMA = 8

    def __init__(self):
        self.nc = bass.Bass("TRN2", target_bir_lowering=False, num_devices=NDEV)
        self.es = ExitStack()
        self.ops = {e: [] for e in self.ENGS}
        self.cnt = {e: 0 for e in self.ENGS}
        self.waited = {e: {} for e in self.ENGS}
        self.sems = {}
        for e in self.ENGS:
            self.sems[e] = self.es.enter_context(self.nc.semaphore("s_" + e))
        self.dsem = {}
        self.dcnt = {}
        self.dnext = {}
        for q in ("sync", "gpsimd", "scalar"):
            for i in range(self.NDMA):
                k = "d_%s_%d" % (q, i)
                self.sems[k] = self.es.enter_context(self.nc.semaphore(k))
                self.dcnt[k] = 0
            self.dnext[q] = 0
        self.out_events = []
        self.strict = ("vector", "scalar", "gpsimd")
        self.nalloc = 0

    def dram(self, name, shape, dt, kind):
        return self.nc.dram_tensor(name, list(shape), dt, kind=kind).ap()

    def sb(self, shape, dt, name=None):
        self.nalloc += 1
        t = self.es.enter_context(self.nc.sbuf_tensor(name or ("sb%d" % self.nalloc), list(shape), dt))
        return V(t[:], [Tok()])

    def sb_rot(self, key, shape, dt, n):
        if not hasattr(self, "_rot"):
            self._rot = {}
        if key not in self._rot:
            self._rot[key] = [[self.sb(shape, dt, name="%s_%d" % (key, j)) for j in range(n)], 0]
        ent = self._rot[key]
        v = ent[0][ent[1] % n]
        ent[1] += 1
        return v

    def ps(self, shape, dt=F32, name=None):
        self.nalloc += 1
        t = self.es.enter_context(self.nc.psum_tensor(name or ("ps%d" % self.nalloc), list(shape), dt))
        return V(t[:], [Tok()])

    def _deps(self, eng, reads, writes):
        need = {}

        def add(ev):
            if ev is None:
                return
            k, v = ev
            if need.get(k, 0) < v:
                need[k] = v
        for t in _toks(reads):
            add(t.w)
        for t in _toks(writes):
            add(t.w)
            for k, v in t.rd.items():
                add((k, v))
        waits = []
        wd = self.waited[eng]
        for k, v in need.items():
            if k == eng and not (self.strict and eng in self.strict):
                continue
            if wd.get(k, 0) >= v:
                continue
            wd[k] = v
            waits.append((k, v))
        return waits

    def _commit(self, ev, reads, writes):
        k, v = ev
        for t in _toks(reads):
            if t.rd.get(k, 0) < v:
                t.rd[k] = v
        for t in _toks(writes):
            t.w = ev
            t.rd = {}

    def op(self, eng, fn, reads, writes):
        waits = self._deps(eng, reads, writes)
        self.cnt[eng] += 1
        ev = (eng, self.cnt[eng])
        self.ops[eng].append((waits, fn, (eng, 1)))
        self._commit(ev, reads, writes)
        return ev

    def dma(self, out, in_, q="sync", is_output=False, **kw):
        reads = [in_]
        writes = [out]
        i = self.dnext[q]
        self.dnext[q] = (i + 1) % self.NDMA
        k = "d_%s_%d" % (q, i)
        waits = self._deps(q, reads, writes)
        prev = self.dcnt[k]
        if prev > 0 and self.waited[q].get(k, 0) < prev:
            self.waited[q][k] = prev
            waits.append((k, prev))
        self.dcnt[k] += 16
        ev = (k, self.dcnt[k])
        o, i_ = _ap(out), _ap(in_)
        self.ops[q].append((waits, lambda e: e.dma_start(out=o, in_=i_, **kw), (k, 16)))
        self._commit(ev, reads, writes)
        if is_output:
            self.out_events.append(ev)
        return ev

    def dma_fn(self, fn, reads, writes, q="sync", is_output=False):
        i = self.dnext[q]
        self.dnext[q] = (i + 1) % self.NDMA
        k = "d_%s_%d" % (q, i)
        waits = self._deps(q, reads, writes)
        prev = self.dcnt[k]
        if prev > 0 and self.waited[q].get(k, 0) < prev:
            self.waited[q][k] = prev
            waits.append((k, prev))
        self.dcnt[k] += 16
        ev = (k, self.dcnt[k])
        self.ops[q].append((waits, fn, (k, 16)))
        self._commit(ev, reads, writes)
        if is_output:
            self.out_events.append(ev)
        return ev

    def coll(self, kind, in_, out, groups, q="gpsimd", op=ALU.bypass):
        if "cc" not in self.sems:
            self.sems["cc"] = self.es.enter_context(self.nc.semaphore("cc_sem"))
            self.cccnt = 0
        waits = self._deps(q, [in_], [out])
        if self.cccnt > 0 and self.waited[q].get("cc", 0) < self.cccnt:
            self.waited[q]["cc"] = self.cccnt
            waits.append(("cc", self.cccnt))
        self.cccnt += 1
        ev = ("cc", self.cccnt)
        i_, o_ = _ap(in_).opt(), _ap(out).opt()
        self.ops[q].append((waits, lambda e: e.collective_compute(kind, op, groups, [i_], [o_]), ("cc", None)))
        self._commit(ev, [in_], [out])
        return ev

    def finish(self):
        nc = self.nc
        fin = {}
        for k, v in self.out_events:
            fin[k] = max(fin.get(k, 0), v)
        sems = self.sems
        ops = self.ops
        with nc.Block() as block:
            def runner(name):
                def body(e):
                    for waits, fn, (sk, inc) in ops[name]:
                        for k, v in waits:
                            e.wait_ge(sems[k], v)
                        ins = fn(e)
                        if inc is None:
                            ins.then_inc(sems[sk])
                        else:
                            ins.then_inc(sems[sk], inc)
                    if name == "sync":
                        for k, v in fin.items():
                            e.wait_ge(sems[k], v)
                return body
            block.tensor(runner("tensor"))
            block.vector(runner("vector"))
            block.scalar(runner("scalar"))
            block.gpsimd(runner("gpsimd"))
            block.sync(runner("sync"))
        self.es.close()
        return nc

    def matmul(self, out, lhsT, rhs, start=True, stop=True):
        o, l, r = _ap(out), _ap(lhsT), _ap(rhs)
        return self.op("tensor", lambda e: e.matmul(o, l, r, start=start, stop=stop), [lhsT, rhs], [out])

    def transpose(self, out, in_, ident):
        o, i, d = _ap(out), _ap(in_), _ap(ident)
        return self.op("tensor", lambda e: e.transpose(o, i, d), [in_, ident], [out])

    def act(self, out, in_, func, bias=None, scale=1.0, accum_out=None, eng="scalar"):
        o, i = _ap(out), _ap(in_)
        b = _ap(bias) if bias is not None else None
        s = _ap(scale)
        a = _ap(accum_out) if accum_out is not None else None
        kw = {}
        if b is not None:
            kw["bias"] = b
        if a is not None:
            kw["accum_out"] = a
        return self.op("scalar", lambda e: e.activation(o, i, func, scale=s, **kw),
                       [in_, bias, scale], [out, accum_out])

    def tt(self, out, in0, in1, op, eng="vector"):
        o, a, b = _ap(out), _ap(in0), _ap(in1)
        return self.op(eng, lambda e: e.tensor_tensor(o, a, b, op), [in0, in1], [out])

    def ts(self, out, in0, s1, op0, s2=None, op1=None, accum_out=None, eng="vector"):
        o, a = _ap(out), _ap(in0)
        x1, x2 = _ap(s1), _ap(s2)
        acc = _ap(accum_out) if accum_out is not None else None
        kw = {}
        if op1 is not None:
            kw["op1"] = op1
        if acc is not None:
            kw["accum_out"] = acc
        return self.op(eng, lambda e: e.tensor_scalar(o, a, x1, x2, op0, **kw),
                       [in0, s1, s2], [out, accum_out])

    def stt(self, out, in0, scalar, in1, op0, op1, eng="vector"):
        o, a, s, b = _ap(out), _ap(in0), _ap(scalar), _ap(in1)
        return self.op(eng, lambda e: e.scalar_tensor_tensor(o, a, s, b, op0, op1), [in0, scalar, in1], [out])

    def copy(self, out, in_, eng="vector"):
        o, i = _ap(out), _ap(in_)
        if eng == "scalar":
            return self.op(eng, lambda e: e.copy(o, i), [in_], [out])
        return self.op(eng, lambda e: e.tensor_copy(o, i), [in_], [out])

    def memset(self, out, val, eng="vector"):
        o = _ap(out)
        return self.op(eng, lambda e: e.memset(o, val), [], [out])

    def reduce(self, out, in_, op, axis=AX.X, eng="vector"):
        o, i = _ap(out), _ap(in_)
        return self.op(eng, lambda e: e.tensor_reduce(o, i, axis, op), [in_], [out])

    def recip(self, out, in_):
        o, i = _ap(out), _ap(in_)
        return self.op("vector", lambda e: e.reciprocal(o, i), [in_], [out])

    def max8(self, out, in_):
        o, i = _ap(out), _ap(in_)
        return self.op("vector", lambda e: e.max(o, i), [in_], [out])

    def match_replace(self, out, to_replace, values, imm):
        o, r, v = _ap(out), _ap(to_replace), _ap(values)
        return self.op("vector", lambda e: e.match_replace(o, r, v, imm), [to_replace, values], [out])

    def bn_stats(self, out, in_):
        o, i = _ap(out), _ap(in_)
        return self.op("vector", lambda e: e.bn_stats(o, i), [in_], [out])

    def bn_aggr(self, out, in_):
        o, i = _ap(out), _ap(in_)
        return self.op("vector", lambda e: e.bn_aggr(o, i), [in_], [out])


T = 8192
D = 2048
NCORE = 8
TPC = T // NCORE
KC = D // 128
EVEN_IN = 5648
ODD_IN = 6680
FFN = 5632
FCH = FFN // 128
ALPHA = 8.0 ** 0.25
LN_EPS = 1e-5
SCALE = 128.0 ** -0.5
NEGBIG = -30000.0


def make_ident(P, dt):
    idf = P.sb([128, 128], F32)
    P.memset(idf, 0.0, eng="gpsimd")
    o = idf.ap
    P.op("gpsimd", lambda e: e.affine_select(out=o, in_=o, pattern=[[-1, 128]], compare_op=ALU.not_equal,
                                              fill=1.0, base=0, channel_multiplier=1), [idf], [idf])
    if dt == F32:
        return idf
    idb = P.sb([128, 128], dt)
    P.copy(idb, idf)
    return idb


class PsumPool:
    def __init__(self, P, nf32, nbf16=0):
        self.f = [P.ps([128, 512], F32) for _ in range(nf32)]
        self.b = [P.ps([128, 1024], BF16) for _ in range(nbf16)]
        self.fi = 0
        self.bi = 0

    def nf(self):
        x = self.f[self.fi % len(self.f)]
        self.fi += 1
        return x

    def nb(self):
        x = self.b[self.bi % len(self.b)]
        self.bi += 1
        return x


def load_hT(P, pp, ident_bf, h_dram, ntiles, hT, col0=0):
    for i in range(ntiles):
        hf = P.sb_rot("ldh_f", [128, D], F32, 2)
        P.dma(hf, h_dram[i * 128:(i + 1) * 128, :], q="sync")
        hb = P.sb_rot("ldh_b", [128, D], BF16, 2)
        P.copy(hb[:, 0:1024], hf[:, 0:1024], eng="vector")
        P.copy(hb[:, 1024:2048], hf[:, 1024:2048], eng="gpsimd")
        for g in range(KC // 8):
            pt = pp.nb()
            for j in range(8):
                kc = g * 8 + j
                P.transpose(pt[:, j * 128:(j + 1) * 128], hb[:, kc * 128:(kc + 1) * 128], ident_bf)
            dst = hT[:, g * 8:(g + 1) * 8, col0 + i * 128: col0 + (i + 1) * 128]
            src = pt.re("p (j t) -> p j t", t=128)
            if g % 2 == 0:
                P.copy(dst, src, eng="vector")
            else:
                P.copy(dst, src, eng="scalar")


def layer_norm_tile(P, pre, out, g_bc, b_bc):
    st = P.sb_rot("ln_st", [128, 4, 6], F32, 2)
    for c in range(4):
        P.bn_stats(st[:, c, :], pre[:, c * 512:(c + 1) * 512])
    mv = P.sb_rot("ln_mv", [128, 2], F32, 2)
    P.bn_aggr(mv, st.re("p a b -> p (a b)"))
    rs = P.sb_rot("ln_rs", [128, 1], F32, 2)
    P.ts(rs, mv[:, 1:2], LN_EPS, ALU.add)
    P.act(rs, rs, AF.Sqrt)
    P.recip(rs, rs)
    P.ts(out, pre, mv[:, 0:1], ALU.subtract, rs, ALU.mult)
    P.tt(out, out, g_bc, ALU.mult, eng="gpsimd")
    P.tt(out, out, b_bc, ALU.add)


def build_inproj(IN):
    P = Prog()
    h = P.dram("h", [TPC, D], F32, "ExternalInput")
    w = P.dram("w", [D, IN], F32, "ExternalInput")
    p = P.dram("p", [TPC, IN], F32, "ExternalOutput")
    pp = PsumPool(P, 4, 2)
    idb = make_ident(P, BF16)
    hT = P.sb([128, KC, TPC], BF16)
    load_hT(P, pp, idb, h, TPC // 128, hT)
    wv = w.rearrange("(kc p) n -> p kc n", p=128)
    nblk = (IN + 511) // 512
    for cb in range(nblk):
        c0 = cb * 512
        nc_ = min(512, IN - c0)
        wb = P.sb_rot("wblk", [128, KC, 512], BF16, 2)
        P.dma(wb[:, :, 0:nc_], wv[:, :, c0:c0 + nc_], q="gpsimd")
        for i in range(TPC // 128):
            ps = pp.nf()
            for kc in range(KC):
                P.matmul(ps[:, 0:nc_], hT[:, kc, i * 128:(i + 1) * 128], wb[:, kc, 0:nc_],
                         start=(kc == 0), stop=(kc == KC - 1))
            ob = P.sb_rot("ob", [128, 512], F32, 3)
            if i % 2 == 0:
                P.copy(ob[:, 0:nc_], ps[:, 0:nc_], eng="vector")
            else:
                P.copy(ob[:, 0:nc_], ps[:, 0:nc_], eng="scalar")
            P.dma(p[i * 128:(i + 1) * 128, c0:c0 + nc_], ob[:, 0:nc_], q="sync", is_output=True)
    return P.finish()


def build_p1(even):
    P = Prog()
    ymix = P.dram("ymix", [TPC, D], F32, "ExternalInput")
    hprev = P.dram("hprev", [TPC, D], F32, "ExternalInput")
    w = P.dram("w", [D, D], F32, "ExternalInput")
    lng = P.dram("lng", [128, D], F32, "ExternalInput")
    lnb = P.dram("lnb", [128, D], F32, "ExternalInput")
    if even:
        nw = P.dram("nw", [128, 1024], F32, "ExternalInput")
    hmid = P.dram("hmid", [TPC, D], F32, "ExternalOutput")
    pp = PsumPool(P, 4, 2)
    idb = make_ident(P, BF16)
    wb = P.sb([128, KC, D], BF16)
    wv = w.rearrange("(kc p) n -> p kc n", p=128)
    for c in range(4):
        P.dma(wb[:, :, c * 512:(c + 1) * 512], wv[:, :, c * 512:(c + 1) * 512], q="gpsimd")
    g_bc = P.sb([128, D], F32)
    b_bc = P.sb([128, D], F32)
    P.dma(g_bc, lng)
    P.dma(b_bc, lnb)
    if even:
        nw_bc = P.sb([128, 1024], F32)
        P.dma(nw_bc, nw)
    for i in range(TPC // 128):
        yf = P.sb_rot("yf", [128, D], F32, 2)
        P.dma(yf, ymix[i * 128:(i + 1) * 128, :])
        hp = P.sb_rot("hp", [128, D], F32, 2)
        P.dma(hp, hprev[i * 128:(i + 1) * 128, :])
        yb = P.sb_rot("yb", [128, D], BF16, 2)
        if even:
            ss = P.sb_rot("ss", [128, 2], F32, 2)
            junk = P.sb_rot("junk", [128, 512], F32, 1)
            for g in range(2):
                P.act(junk, yf[:, g * 512:(g + 1) * 512], AF.Square, accum_out=ss[:, g:g + 1])
            P.ts(ss, ss, 1.0 / 512.0, ALU.mult, LN_EPS, ALU.add)
            P.act(ss, ss, AF.Sqrt)
            P.recip(ss, ss)
            for g in range(2):
                P.stt(yb[:, g * 512:(g + 1) * 512], yf[:, g * 512:(g + 1) * 512], ss[:, g:g + 1],
                      nw_bc[:, g * 512:(g + 1) * 512], ALU.mult, ALU.mult)
            P.copy(yb[:, 1024:2048], yf[:, 1024:2048], eng="gpsimd")
        else:
            P.copy(yb[:, 0:1024], yf[:, 0:1024], eng="vector")
            P.copy(yb[:, 1024:2048], yf[:, 1024:2048], eng="gpsimd")
        yT = P.sb_rot("yT", [128, KC, 128], BF16, 2)
        for g in range(2):
            pt = pp.nb()
            for j in range(8):
                kc = g * 8 + j
                P.transpose(pt[:, j * 128:(j + 1) * 128], yb[:, kc * 128:(kc + 1) * 128], idb)
            if g == 0:
                P.copy(yT[:, 0:8, :], pt.re("p (j t) -> p j t", t=128), eng="vector")
            else:
                P.copy(yT[:, 8:16, :], pt.re("p (j t) -> p j t", t=128), eng="scalar")
        pre = P.sb_rot("pre", [128, D], F32, 2)
        for cb in range(4):
            ps = pp.nf()
            for kc in range(KC):
                P.matmul(ps, yT[:, kc, :], wb[:, kc, cb * 512:(cb + 1) * 512], start=(kc == 0), stop=(kc == KC - 1))
            P.stt(pre[:, cb * 512:(cb + 1) * 512], hp[:, cb * 512:(cb + 1) * 512], ALPHA, ps, ALU.mult, ALU.add)
        ot = P.sb_rot("ot", [128, D], F32, 2)
        layer_norm_tile(P, pre, ot, g_bc, b_bc)
        P.dma(hmid[i * 128:(i + 1) * 128, :], ot, is_output=True)
    return P.finish()


def build_p2():
    P = Prog()
    hmid = P.dram("hmid", [TPC, D], F32, "ExternalInput")
    haloT = P.dram("haloT", [D, 2], F32, "ExternalInput")
    wup = P.dram("wup", [D, 2 * FFN], F32, "ExternalInput")
    cw = P.dram("cw", [128, FCH, 4], F32, "ExternalInput")
    wdn = P.dram("wdn", [FFN, D], F32, "ExternalInput")
    lng = P.dram("lng", [128, D], F32, "ExternalInput")
    lnb = P.dram("lnb", [128, D], F32, "ExternalInput")
    hout = P.dram("hout", [TPC, D], F32, "ExternalOutput")
    pp = PsumPool(P, 6, 2)
    idb = make_ident(P, BF16)
    hT = P.sb([128, KC, 2 + TPC], BF16)
    P.dma(hT[:, :, 0:2], haloT.rearrange("(kc p) t -> p kc t", p=128), q="gpsimd")
    load_hT(P, pp, idb, hmid, TPC // 128, hT, col0=2)
    g_bc = P.sb([128, D], F32)
    b_bc = P.sb([128, D], F32)
    P.dma(g_bc, lng)
    P.dma(b_bc, lnb)
    cws = P.sb([128, FCH, 4], F32)
    P.dma(cws, cw)
    wupv = wup.rearrange("(kc p) n -> p kc n", p=128)
    wdnv = wdn.rearrange("(f p) n -> p f n", p=128)
    HB = 512
    gT = P.sb([128, FCH, HB], BF16)
    pre = [P.sb([128, D], F32) for _ in range(HB // 128)]
    for hf in range(TPC // HB):
        t0 = hf * HB
        for i in range(HB // 128):
            P.dma(pre[i], hmid[t0 + i * 128: t0 + (i + 1) * 128, :])
        for f in range(FCH):
            wa = P.sb_rot("wa", [128, KC, 128], BF16, 2)
            wu = P.sb_rot("wu", [128, KC, 128], BF16, 2)
            P.dma(wa, wupv[:, :, f * 128:(f + 1) * 128], q="gpsimd")
            P.dma(wu, wupv[:, :, FFN + f * 128: FFN + (f + 1) * 128], q="gpsimd")
            for b in range(HB // 256):
                c0 = t0 + b * 256
                pa = pp.nf()
                pu = pp.nf()
                for kc in range(KC):
                    P.matmul(pa[:, 0:258], wa[:, kc, :], hT[:, kc, c0:c0 + 258], start=(kc == 0), stop=(kc == KC - 1))
                for kc in range(KC):
                    P.matmul(pu[:, 0:256], wu[:, kc, :], hT[:, kc, c0 + 2:c0 + 258], start=(kc == 0), stop=(kc == KC - 1))
                ac = P.sb_rot("ac", [128, 256], F32, 2)
                P.act(ac, pa[:, 2:258], AF.Identity, bias=cws[:, f, 3:4], scale=cws[:, f, 2:3])
                P.stt(ac, pa[:, 1:257], cws[:, f, 1:2], ac, ALU.mult, ALU.add)
                P.stt(ac, pa[:, 0:256], cws[:, f, 0:1], ac, ALU.mult, ALU.add)
                sg = P.sb_rot("sg", [128, 256], F32, 2)
                P.act(sg, ac, AF.Silu)
                P.tt(gT[:, f, b * 256:(b + 1) * 256], sg, pu[:, 0:256], ALU.mult)
        for cb in range(D // 128):
            wd = P.sb_rot("wd", [128, FCH, 128], BF16, 2)
            P.dma(wd, wdnv[:, :, cb * 128:(cb + 1) * 128], q="gpsimd")
            for i in range(HB // 128):
                ps = pp.nf()
                for f in range(FCH):
                    P.matmul(ps[:, 0:128], gT[:, f, i * 128:(i + 1) * 128], wd[:, f, :], start=(f == 0), stop=(f == FCH - 1))
                P.stt(pre[i][:, cb * 128:(cb + 1) * 128], pre[i][:, cb * 128:(cb + 1) * 128], ALPHA, ps[:, 0:128],
                      ALU.mult, ALU.add)
        for i in range(HB // 128):
            ot = P.sb_rot("ot", [128, D], F32, 2)
            layer_norm_tile(P, pre[i], ot, g_bc, b_bc)
            P.dma(hout[t0 + i * 128: t0 + (i + 1) * 128, :], ot, is_output=True)
    return P.finish()


def host_cw(conv_w, conv_b):
    a = np.concatenate([conv_w, conv_b[None, :]], axis=0)
    return np.ascontiguousarray(a.reshape(4, FCH, 128).transpose(2, 1, 0)).astype(np.float32)


TZW = 2560


def flash_block(P, pp, O, QT_blk, nq, k_tiles, scale_bias_fn, pv_fn, n_out):
    nk = len(k_tiles)
    nsub = nq // 128
    pss = {}

    def emit_s(jj):
        kt = k_tiles[jj]
        ps = pp.nf()
        has_mask = kt.get("mask") is not None
        P.matmul(ps[:, 0:nq], kt["lhsT"], QT_blk, start=True, stop=not has_mask)
        if has_mask:
            ml, mr = kt["mask"]
            P.matmul(ps[:, 0:nq], ml, mr, start=False, stop=True)
        pss[jj] = ps

    def emit_rest(jj):
        kt = k_tiles[jj]
        ps = pss.pop(jj)
        pt = P.sb_rot("fl_pt", [128, 512], BF16, 3)
        kind, bap = kt["bias"]
        if kind == "far":
            P.act(pt[:, 0:nq], ps[:, 0:nq], AF.Exp, bias=bap, scale=SCALE)
        else:
            tmp = P.sb_rot("fl_tmp", [128, 512], F32, 2)
            P.stt(tmp[:, 0:nq], ps[:, 0:nq], SCALE, bap, ALU.mult, ALU.add)
            P.act(pt[:, 0:nq], tmp[:, 0:nq], AF.Exp)
        for s in range(nsub):
            P.matmul(O[s][:, 0:n_out], pt[:, s * 128:(s + 1) * 128], pv_fn(jj), start=(jj == 0), stop=(jj == nk - 1))

    emit_s(0)
    for jj in range(nk):
        if jj + 1 < nk:
            emit_s(jj + 1)
        emit_rest(jj)


def flash_finish(P, Os, dst_fn, ncols=128, extra=None):
    for s, O in enumerate(Os):
        den = P.sb_rot("fl_den", [128, 1], F32, 4)
        P.ts(den, O[:, 128:129], 1e-30, ALU.max)
        P.recip(den, den)
        P.ts(dst_fn(s), O[:, 0:ncols], den, ALU.mult)
        if extra is not None:
            extra(s, O, den)


def build_meven():
    P = Prog()
    z = P.dram("z", [T, 128], F32, "ExternalInput")
    xbcT = P.dram("xbcT", [3, 128, T + 3], F32, "ExternalInput")
    cwx = P.dram("cwx", [128, 3, 5], F32, "ExternalInput")
    dtr = P.dram("dtr", [T, 2], F32, "ExternalInput")
    hpar = P.dram("hpar", [128, 6], F32, "ExternalInput")
    qT = P.dram("qT", [128, T], F32, "ExternalInput")
    kT = P.dram("kT", [128, T], F32, "ExternalInput")
    v = P.dram("v", [T, 128], F32, "ExternalInput")
    tz = P.dram("tz", [128, TZW], F32, "ExternalInput")
    pastneg = P.dram("pastneg", [128, 64, 32], F32, "ExternalInput")
    ownneg = P.dram("ownneg", [128, 64, 32], F32, "ExternalInput")
    ejd = P.dram("ej", [32, 32, 128], F32, "ExternalInput")
    cst = P.dram("cst", [128, 3, 128], F32, "ExternalInput")
    y = P.dram("y", [T, 256], F32, "ExternalOutput")

    pp = PsumPool(P, 7, 1)
    idb = make_ident(P, BF16)
    idf = make_ident(P, F32)
    csts = P.sb([128, 3, 128], F32)
    P.dma(csts, cst)
    triu, ones, causneg = csts[:, 0, :], csts[:, 1, :], csts[:, 2, :]
    cw = P.sb([128, 3, 5], F32)
    P.dma(cw, cwx)
    hp = P.sb([128, 6], F32)
    P.dma(hp, hpar)
    a_bc = P.sb([128, 2], F32)
    P.act(a_bc, hp[:, 2:4], AF.Exp)
    P.ts(a_bc, a_bc, -1.0, ALU.mult)

    H = P.sb([128, 128], F32)
    Hb = P.sb([128, 128], BF16)
    P.memset(H, 0.0)
    P.memset(Hb, 0.0)
    for sc in range(T // 512):
        t0 = sc * 512
        fm = []
        for wch in range(3):
            raw = P.sb_rot("raw%d" % wch, [128, 515], F32, 2)
            P.dma(raw, xbcT[wch, :, t0:t0 + 515])
            acc = P.sb_rot("cacc%d" % wch, [128, 512], F32, 2)
            P.act(acc, raw[:, 3:515], AF.Identity, bias=cw[:, wch, 4:5], scale=cw[:, wch, 3:4])
            for k in (2, 1, 0):
                P.stt(acc, raw[:, k:k + 512], cw[:, wch, k:k + 1], acc, ALU.mult, ALU.add)
            if wch == 0:
                o = P.sb_rot("xs", [128, 512], F32, 2)
            else:
                o = P.sb_rot("bc%d" % wch, [128, 512], BF16, 2)
            P.act(o, acc, AF.Silu)
            fm.append(o)
        xs, BT, CT = fm
        zt = P.sb_rot("zt", [128, 4, 128], F32, 2)
        P.dma(zt, z[t0:t0 + 512, :].rearrange("(i p) d -> p i d", p=128))
        dtt = P.sb_rot("dtt", [128, 4, 2], F32, 2)
        P.dma(dtt, dtr[t0:t0 + 512, :].rearrange("(i p) d -> p i d", p=128))
        dt = P.sb_rot("dt", [128, 4, 2], F32, 2)
        for i in range(4):
            P.tt(dt[:, i, :], dtt[:, i, :], hp[:, 0:2], ALU.add)
        P.act(dt, dt, AF.Exp)
        P.act(dt, dt, AF.Ln, bias=1.0)
        dtA = P.sb_rot("dtA", [128, 4, 2], F32, 2)
        for i in range(4):
            P.tt(dtA[:, i, :], dt[:, i, :], a_bc, ALU.mult)
        sz = P.sb_rot("sz", [128, 4, 128], F32, 2)
        P.act(sz, zt, AF.Silu)
        for i in range(4):
            cs = slice(i * 128, (i + 1) * 128)
            pa = pp.nf()
            P.matmul(pa[:, 0:2], triu, dtA[:, i, :])
            P.matmul(pa[:, 2:4], ones, dtA[:, i, :])
            ac = P.sb_rot("ac", [128, 4], F32, 2)
            P.copy(ac, pa[:, 0:4])
            dec = P.sb_rot("dec", [128, 6], F32, 2)
            P.act(dec[:, 0:2], ac[:, 0:2], AF.Exp)
            P.tt(dec[:, 2:4], ac[:, 2:4], ac[:, 0:2], ALU.subtract)
            P.act(dec[:, 2:4], dec[:, 2:4], AF.Exp)
            P.act(dec[:, 4:6], ac[:, 2:4], AF.Exp)
            pS = pp.nf()
            P.matmul(pS[:, 0:128], BT[:, cs], CT[:, cs])
            SmT = []
            for h in range(2):
                dab = P.sb_rot("dab", [128, 128], F32, 2)
                P.copy(dab, V(dtA.ap[:, i, h:h + 1].to_broadcast([128, 128]), dtA.toks), eng="gpsimd")
                pb = pp.nf()
                P.matmul(pb[:, 0:128], dab, triu)
                dm = P.sb_rot("dm", [128, 128], F32, 2)
                P.stt(dm, pb[:, 0:128], ac[:, h:h + 1], causneg, ALU.subtract, ALU.add)
                P.act(dm, dm, AF.Exp)
                sm = P.sb_rot("smT%d" % h, [128, 128], BF16, 2)
                P.tt(sm, pS[:, 0:128], dm, ALU.mult)
                SmT.append(sm)
            px = pp.nf()
            P.transpose(px[:, 0:128], xs[:, cs], idf)
            xtok = P.sb_rot("xtok", [128, 128], F32, 2)
            P.copy(xtok, px[:, 0:128], eng="scalar")
            xdt = P.sb_rot("xdt", [128, 128], BF16, 2)
            vd = P.sb_rot("vd", [128, 128], BF16, 2)
            for h in range(2):
                hs = slice(h * 64, (h + 1) * 64)
                P.ts(xdt[:, hs], xtok[:, hs], dt[:, i, h:h + 1], ALU.mult)
                P.ts(vd[:, hs], xtok[:, hs], dt[:, i, h:h + 1], ALU.mult, dec[:, 2 + h:3 + h], ALU.mult)
            pbt = pp.nb()
            P.transpose(pbt[:, 0:128], BT[:, cs], idb)
            btok = P.sb_rot("btok", [128, 128], BF16, 2)
            P.copy(btok, pbt[:, 0:128], eng="scalar")
            pyd = pp.nf()
            for h in range(2):
                hs = slice(h * 64, (h + 1) * 64)
                P.matmul(pyd[:, hs], SmT[h], xdt[:, hs])
            pyo = pp.nf()
            P.matmul(pyo[:, 0:128], CT[:, cs], Hb)
            ph = pp.nf()
            P.matmul(ph[:, 0:128], btok, vd)
            yt = P.sb_rot("yt", [128, 128], F32, 2)
            P.copy(yt, pyd[:, 0:128], eng="scalar")
            for h in range(2):
                hs = slice(h * 64, (h + 1) * 64)
                P.stt(yt[:, hs], pyo[:, hs], dec[:, h:h + 1], yt[:, hs], ALU.mult, ALU.add)
                P.stt(yt[:, hs], xtok[:, hs], hp[:, 4 + h:5 + h], yt[:, hs], ALU.mult, ALU.add)
            yo = P.sb_rot("yo", [128, 128], F32, 2)
            P.tt(yo, yt, sz[:, i, :], ALU.mult, eng="gpsimd")
            P.dma(y[t0 + i * 128: t0 + (i + 1) * 128, 0:128], yo, is_output=True)
            for h in range(2):
                hs = slice(h * 64, (h + 1) * 64)
                P.stt(H[:, hs], H[:, hs], dec[:, 4 + h:5 + h], ph[:, hs], ALU.mult, ALU.add)
            P.copy(Hb, H, eng="gpsimd")

    QT = P.sb([128, T], BF16)
    KT = P.sb([128, T], BF16)
    V1 = P.sb([128, T // 128, 129], BF16)
    for c in range(4):
        P.dma(QT[:, c * 2048:(c + 1) * 2048], qT[:, c * 2048:(c + 1) * 2048], q="gpsimd")
        P.dma(KT[:, c * 2048:(c + 1) * 2048], kT[:, c * 2048:(c + 1) * 2048], q="gpsimd")
        P.dma(V1[:, c * 16:(c + 1) * 16, 0:128], v[c * 2048:(c + 1) * 2048, :].rearrange("(j p) d -> p j d", p=128), q="gpsimd")
    P.memset(V1[:, :, 128:129], 1.0)
    tzs = P.sb([128, TZW], F32)
    P.dma(tzs, tz)
    pn = P.sb([128, 64, 32], F32)
    on = P.sb([128, 64, 32], F32)
    P.dma(pn, pastneg)
    P.dma(on, ownneg)
    EJ = P.sb([32, 32, 128], BF16)
    P.dma(EJ, ejd, q="gpsimd")
    kmT = P.sb([128, 32], F32)
    for c in range(4):
        kf = P.sb_rot("kf", [128, 2048], F32, 2)
        P.dma(kf, kT[:, c * 2048:(c + 1) * 2048])
        P.reduce(kmT[:, c * 8:(c + 1) * 8], kf.re("p (b t) -> p b t", t=256), ALU.add)
    P.ts(kmT, kmT, 1.0 / 256.0, ALU.mult)
    negT = P.sb([32, T], BF16)
    for i in range(T // 128):
        qf = P.sb_rot("qf", [128, 128], F32, 3)
        P.dma(qf, qT[:, i * 128:(i + 1) * 128])
        pg = pp.nf()
        P.matmul(pg[:, 0:32], qf, kmT)
        gm = P.sb_rot("gm", [128, 32], F32, 2)
        P.tt(gm, pg[:, 0:32], pn[:, i, :], ALU.add)
        m8 = P.sb_rot("m8", [128, 8], F32, 2)
        P.max8(m8, gm)
        thr = P.sb_rot("thr", [128, 1], F32, 2)
        P.ts(thr, m8[:, 2:3], -1e29, ALU.max)
        P.ts(gm, gm, thr, ALU.is_ge)
        ng = P.sb_rot("ng", [128, 32], F32, 2)
        P.stt(ng, gm, -NEGBIG, on[:, i, :], ALU.mult, ALU.add)
        pt = pp.nf()
        P.transpose(pt[0:32, 0:128], ng, idf)
        P.copy(negT[:, i * 128:(i + 1) * 128], pt[0:32, 0:128], eng="scalar")
    O = [pp.f[k] for k in range(4)]
    pp.f = pp.f[4:]
    f31 = P.sb([128, 1], F32)
    P.copy(f31, tzs[:, TZW - 1:TZW])
    for Q in range(T // 512):
        t0 = Q * 512
        kts = []
        for j in range(4 * Q + 4):
            off = t0 - j * 128
            if off >= 1664:
                b = ("far", f31)
            else:
                b = ("near", tzs[:, off + 384: off + 384 + 512])
            kts.append({"lhsT": KT[:, j * 128:(j + 1) * 128],
                        "mask": (EJ[:, j // 2, :], negT[:, t0:t0 + 512]),
                        "bias": b})
        flash_block(P, pp, O, QT[:, t0:t0 + 512], 512, kts, None, lambda jj: V1[:, jj, :], 129)
        yq = P.sb_rot("yq", [128, 4, 128], F32, 2)
        flash_finish(P, O, lambda s: yq[:, s, :])
        P.dma(y[t0:t0 + 512, 128:256].rearrange("(s p) d -> p s d", p=128), yq, is_output=True)
    return P.finish()


def rel_bucket_np(dist):
    n = np.maximum(dist, 0)
    max_exact = 16
    nf = np.maximum(n, max_exact).astype(np.float32)
    large = max_exact + (np.log(nf / np.float32(max_exact)) / np.float32(np.log(2048 / max_exact))
                         * np.float32(32 - max_exact)).astype(np.int32)
    large = np.minimum(large, 31)
    return np.where(n < max_exact, n, large)


def host_tz(rel_bias_h, m_lo, width, stride_p, base, band=None):
    p = np.arange(128)[:, None]
    m = np.arange(width)[None, :] + m_lo
    dist = m - stride_p * p - base
    tab = rel_bias_h[rel_bucket_np(dist)].astype(np.float32)
    bad = dist < 0
    if band is not None:
        bad = bad | (dist >= band)
    return np.where(bad, np.float32(-1e30), tab).astype(np.float32)


def host_meven_inputs(p, c, conv_w, conv_b, dt_bias, a_log, d_skip, rel_bias):
    g = c // 4
    f32 = np.float32
    def padT(a):
        o = np.zeros((128, T + 3), f32)
        o[:, 3:] = a.T
        return o
    xb = p[:, 1024:2560]
    chans = [np.arange(128 * c, 128 * c + 128), 1024 + np.arange(128 * g, 128 * g + 128),
             1280 + np.arange(128 * g, 128 * g + 128)]
    xbcT = np.stack([padT(xb[:, ch]) for ch in chans])
    cwx = np.stack([np.concatenate([conv_w[:, ch], conv_b[None, ch]], 0).T for ch in chans], axis=1).astype(f32)
    hpar = np.array([dt_bias[2 * c], dt_bias[2 * c + 1], a_log[2 * c], a_log[2 * c + 1], d_skip[2 * c], d_skip[2 * c + 1]], f32)
    i = np.arange(64)[:, None]
    n = np.arange(32)[None, :]
    pastneg = np.where(n < i // 2, 0.0, -1e30).astype(f32)
    ownneg = np.where(n == i // 2, 0.0, NEGBIG).astype(f32)
    ej = np.zeros((32, 32, 128), f32)
    for J in range(32):
        ej[J, J, :] = 1.0
    k = np.arange(128)[:, None]
    l = np.arange(128)[None, :]
    cst = np.stack([(k <= l).astype(f32), np.ones((128, 128), f32), np.where(l >= k, 0.0, -1e30).astype(f32)], axis=1)
    return {
        "z": np.ascontiguousarray(p[:, 128 * c:128 * c + 128]),
        "xbcT": xbcT, "cwx": np.ascontiguousarray(cwx),
        "dtr": np.ascontiguousarray(p[:, 2560 + 2 * c: 2560 + 2 * c + 2]),
        "hpar": np.ascontiguousarray(np.broadcast_to(hpar, (128, 6))),
        "qT": np.ascontiguousarray(p[:, 2576 + 128 * c: 2576 + 128 * c + 128].T),
        "kT": np.ascontiguousarray(p[:, 3600 + 128 * c: 3600 + 128 * c + 128].T),
        "v": np.ascontiguousarray(p[:, 4624 + 128 * c: 4624 + 128 * c + 128]),
        "tz": host_tz(rel_bias[:, c], -384, TZW, 1, 0),
        "pastneg": np.ascontiguousarray(np.broadcast_to(pastneg, (128, 64, 32))),
        "ownneg": np.ascontiguousarray(np.broadcast_to(ownneg, (128, 64, 32))),
        "ej": ej, "cst": np.ascontiguousarray(cst),
    }


TCW = 3600
TWW = 1408
GELU_C = 1.5957691216057308


def build_modd_a(parts=("cmp", "att", "ret")):
    P = Prog()
    qT = P.dram("qT", [128, T], F32, "ExternalInput")
    kvcT = P.dram("kvcT", [2, 128, T], F32, "ExternalInput")
    w1 = P.dram("w1", [2, 4096, 128], F32, "ExternalInput")
    w2 = P.dram("w2", [2, 128, 128], F32, "ExternalInput")
    peT = P.dram("peT", [2, 128, 32], F32, "ExternalInput")
    tc = P.dram("tc", [128, TCW], F32, "ExternalInput")
    ovl = P.dram("ovl", [128, 4, 128], F32, "ExternalInput")
    rq4T = P.dram("rq4T", [4, 128, T], F32, "ExternalInput")
    csT = P.dram("csT", [2, 128, T], F32, "ExternalInput")
    rv = P.dram("rv", [T, 128], F32, "ExternalInput")
    rg = P.dram("rg", [T, 128], F32, "ExternalInput")
    rdec = P.dram("rdec", [128, 4], F32, "ExternalInput")
    drt = P.dram("drt", [128, 128], F32, "ExternalInput")
    ocmp = P.dram("ocmp", [T, 128], F32, "ExternalOutput")
    imp = P.dram("imp", [T, 128], F32, "ExternalOutput")
    yret = P.dram("yret", [T, 128], F32, "ExternalOutput")

    pp = PsumPool(P, 7, 1)
    idb = make_ident(P, BF16)

    KcT = P.sb([128, 512], BF16)
    VO = P.sb([128, 4, 257], BF16)
    P.memset(VO[:, :, 128:129], 1.0)
    P.dma(VO[:, :, 129:257], ovl, q="gpsimd")
    if "cmp" in parts:
        for kind in range(2):
            W1 = P.sb_rot("W1", [128, 32, 128], BF16, 1)
            P.dma(W1, w1[kind].rearrange("(l d) e -> d l e", d=128), q="gpsimd")
            W2 = P.sb_rot("W2", [128, 128], BF16, 1)
            P.dma(W2, w2[kind], q="gpsimd")
            pe = P.sb_rot("pe", [128, 32], BF16, 1)
            P.dma(pe, peT[kind], q="gpsimd")
            XT = P.sb_rot("XT", [128, T], BF16, 1)
            for c in range(4):
                P.dma(XT[:, c * 2048:(c + 1) * 2048], kvcT[kind, :, c * 2048:(c + 1) * 2048], q="gpsimd")
            pc = pp.nf()
            for l in range(32):
                P.matmul(pc[:, 0:1], W1[:, l, :], pe[:, l:l + 1], start=(l == 0), stop=(l == 31))
            cvec = P.sb_rot("cvec", [128, 1], F32, 1)
            P.copy(cvec, pc[:, 0:1])
            ph = pp.nf()
            for l in range(32):
                P.matmul(ph[:, 0:511], W1[:, l, :], XT[:, l:l + 8161:16], start=(l == 0), stop=(l == 31))
            u = P.sb_rot("cu", [128, 511], F32, 1)
            P.act(u, ph[:, 0:511], AF.Identity, bias=cvec)
            wk = P.sb_rot("cw_", [128, 511], F32, 1)
            P.tt(wk, u, u, ALU.mult)
            P.ts(wk, wk, 0.044715, ALU.mult, 1.0, ALU.add)
            P.tt(wk, wk, u, ALU.mult)
            P.act(wk, wk, AF.Sigmoid, scale=GELU_C)
            hid = P.sb_rot("hid", [128, 512], BF16, 1)
            P.memset(hid[:, 511:512], 0.0)
            P.tt(hid[:, 0:511], u, wk, ALU.mult)
            if kind == 0:
                pk = pp.nf()
                P.matmul(pk, W2, hid)
                P.copy(KcT, pk)
            else:
                for jt in range(4):
                    pv = pp.nf()
                    P.matmul(pv[:, 0:128], hid[:, jt * 128:(jt + 1) * 128], W2)
                    P.copy(VO[:, jt, 0:128], pv[:, 0:128])

    if "att" in parts:
        QT = P.sb([128, T], BF16)
        for c in range(4):
            P.dma(QT[:, c * 2048:(c + 1) * 2048], qT[:, c * 2048:(c + 1) * 2048], q="gpsimd")
        tcs = P.sb([128, TCW], F32)
        P.dma(tcs, tc)
        O = [pp.f[k] for k in range(4)]
        pp.f = pp.f[4:]
        fconst = P.sb([128, 1], F32)
        P.copy(fconst, tcs[:, TCW - 1:TCW])
        qlim = [int(x[1:]) for x in parts if x[0] == "q" and x[1:].isdigit()]
        for Q in range(qlim[0] if qlim else T // 512):
            t0 = Q * 512
            kts = []
            for jt in range(t0 // 2048 + 1):
                m0 = t0 - 2048 * jt
                if m0 >= 3584:
                    b = ("far", fconst)
                else:
                    b = ("near", tcs[:, m0:m0 + 512])
                kts.append({"lhsT": KcT[:, jt * 128:(jt + 1) * 128], "mask": None, "bias": b})
            NO_ = 257 if "n129" not in parts else 129
            flash_block(P, pp, O, QT[:, t0:t0 + 512], 512, kts, None, lambda jj: VO[:, jj, 0:NO_], NO_)
            oc = P.sb_rot("oc", [128, 4, 128], F32, 2)
            im = P.sb_rot("im", [128, 4, 128], F32, 2)

            def extra(s, Ot, den, im=im):
                if "n129" in parts:
                    P.ts(im[:, s, :], Ot[:, 0:128], den, ALU.mult)
                else:
                    P.ts(im[:, s, :], Ot[:, 129:257], den, ALU.mult)
            flash_finish(P, O, lambda s: oc[:, s, :], extra=extra)
            P.dma(ocmp[t0:t0 + 512, :].rearrange("(s p) d -> p s d", p=128), oc, is_output=True)
            P.dma(imp[t0:t0 + 512, :].rearrange("(s p) d -> p s d", p=128), im, is_output=True)
        pp.f = O + pp.f

    if "ret" in parts:
        rd = P.sb([128, 4], F32)
        P.dma(rd, rdec)
        DRT = P.sb([128, 128], F32)
        P.dma(DRT, drt)
        R = P.sb([128, 128], F32)
        Rb = P.sb([128, 128], BF16)
        P.memset(R, 0.0)
        P.memset(Rb, 0.0)
        for sc in range(T // 512):
            t0 = sc * 512
            rot = []
            cs_ = []
            for w in range(2):
                tbl = P.sb_rot("cs%d" % w, [128, 512], F32, 2)
                P.dma(tbl, csT[w, :, t0:t0 + 512])
                cs_.append(tbl)
            for w in range(2):
                a = P.sb_rot("rqa%d" % w, [128, 512], F32, 2)
                b = P.sb_rot("rqb%d" % w, [128, 512], F32, 2)
                P.dma(a, rq4T[2 * w, :, t0:t0 + 512])
                P.dma(b, rq4T[2 * w + 1, :, t0:t0 + 512])
                P.tt(a, a, cs_[0], ALU.mult)
                P.tt(b, b, cs_[1], ALU.mult, eng="gpsimd")
                o = P.sb_rot("rot%d" % w, [128, 512], BF16, 2)
                P.tt(o, a, b, ALU.add)
                rot.append(o)
            QrT, KrT = rot
            vt = P.sb_rot("rvt", [128, 4, 128], F32, 2)
            P.dma(vt, rv[t0:t0 + 512, :].rearrange("(i p) d -> p i d", p=128))
            vb = P.sb_rot("rvb", [128, 4, 128], BF16, 2)
            P.copy(vb, vt, eng="gpsimd")
            vd = P.sb_rot("rvd", [128, 4, 128], BF16, 2)
            P.ts(vd, vt, rd[:, 1:2], ALU.mult)
            gt = P.sb_rot("rgt", [128, 4, 128], F32, 2)
            P.dma(gt, rg[t0:t0 + 512, :].rearrange("(i p) d -> p i d", p=128))
            P.act(gt, gt, AF.Silu)
            yo4 = P.sb_rot("ryo", [128, 4, 128], F32, 2)
            for i in range(4):
                cs = slice(i * 128, (i + 1) * 128)
                pS = pp.nf()
                P.matmul(pS[:, 0:128], KrT[:, cs], QrT[:, cs])
                sm = P.sb_rot("rsm", [128, 128], BF16, 2)
                P.tt(sm, pS[:, 0:128], DRT, ALU.mult)
                pkt = pp.nb()
                P.transpose(pkt[:, 0:128], KrT[:, cs], idb)
                ktok = P.sb_rot("rktok", [128, 128], BF16, 2)
                P.copy(ktok, pkt[:, 0:128], eng="scalar")
                pY = pp.nf()
                P.matmul(pY[:, 0:128], sm, vb[:, i, :])
                pYo = pp.nf()
                P.matmul(pYo[:, 0:128], QrT[:, cs], Rb)
                pR = pp.nf()
                P.matmul(pR[:, 0:128], ktok, vd[:, i, :])
                yt = P.sb_rot("ryt", [128, 128], F32, 2)
                P.copy(yt, pY[:, 0:128], eng="scalar")
                P.stt(yt, pYo[:, 0:128], rd[:, 0:1], yt, ALU.mult, ALU.add)
                st = P.sb_rot("rst", [128, 6], F32, 2)
                P.bn_stats(st, yt)
                mv = P.sb_rot("rmv", [128, 2], F32, 2)
                P.bn_aggr(mv, st)
                rs = P.sb_rot("rrs", [128, 1], F32, 2)
                P.ts(rs, mv[:, 1:2], LN_EPS, ALU.add)
                P.act(rs, rs, AF.Sqrt)
                P.recip(rs, rs)
                P.ts(yt, yt, mv[:, 0:1], ALU.subtract, rs, ALU.mult)
                P.tt(yo4[:, i, :], yt, gt[:, i, :], ALU.mult, eng="gpsimd")
                P.stt(R, R, rd[:, 2:3], pR[:, 0:128], ALU.mult, ALU.add)
                P.copy(Rb, R, eng="gpsimd")
            P.dma(yret[t0:t0 + 512, :].rearrange("(i p) d -> p i d", p=128), yo4, is_output=True)
    return P.finish()


def build_modd_b():
    P = Prog()
    qT = P.dram("qT", [128, T], F32, "ExternalInput")
    kswT = P.dram("kswT", [2, 128, T], F32, "ExternalInput")
    vsw = P.dram("vsw", [2, T, 128], F32, "ExternalInput")
    gates = P.dram("gates", [T, 3], F32, "ExternalInput")
    imp4 = P.dram("imp4", [4, T, 128], F32, "ExternalInput")
    ocmp = P.dram("ocmp", [T, 128], F32, "ExternalInput")
    tz = P.dram("tz", [128, TZW], F32, "ExternalInput")
    tw = P.dram("tw", [128, TWW], F32, "ExternalInput")
    addmask = P.dram("addmask", [128, 64, 128], F32, "ExternalInput")
    eseld = P.dram("esel", [128, 64, 128], F32, "ExternalInput")
    y = P.dram("y", [T, 128], F32, "ExternalOutput")

    pp = PsumPool(P, 8, 0)
    idf = make_ident(P, F32)
    QT = P.sb([128, T], BF16)
    KsT = P.sb([128, T], BF16)
    KwT = P.sb([128, T], BF16)
    Vs1 = P.sb([128, 64, 129], BF16)
    Vw1 = P.sb([128, 64, 129], BF16)
    for c in range(4):
        sl = slice(c * 2048, (c + 1) * 2048)
        P.dma(QT[:, sl], qT[:, sl], q="gpsimd")
        P.dma(KsT[:, sl], kswT[0, :, sl], q="gpsimd")
        P.dma(KwT[:, sl], kswT[1, :, sl], q="gpsimd")
        P.dma(Vs1[:, c * 16:(c + 1) * 16, 0:128], vsw[0, sl, :].rearrange("(j p) d -> p j d", p=128), q="gpsimd")
        P.dma(Vw1[:, c * 16:(c + 1) * 16, 0:128], vsw[1, sl, :].rearrange("(j p) d -> p j d", p=128), q="gpsimd")
    P.memset(Vs1[:, :, 128:129], 1.0)
    P.memset(Vw1[:, :, 128:129], 1.0)
    tzs = P.sb([128, TZW], F32)
    P.dma(tzs, tz)
    tws = P.sb([128, TWW], F32)
    P.dma(tws, tw)
    ES = P.sb([128, 64, 128], BF16)
    for c in range(4):
        P.dma(ES[:, c * 16:(c + 1) * 16, :], eseld[:, c * 16:(c + 1) * 16, :], q="gpsimd")
    sig = P.sb([128, 64, 3], F32)
    P.dma(sig, gates.rearrange("(i p) d -> p i d", p=128))
    P.act(sig, sig, AF.Sigmoid)
    negT = P.sb([128, T], BF16)
    for i in range(T // 128):
        im4 = P.sb_rot("im4", [128, 4, 128], F32, 2)
        P.dma(im4, imp4[:, i * 128:(i + 1) * 128, :].rearrange("g t j -> t g j"))
        am = P.sb_rot("am", [128, 128], F32, 2)
        P.dma(am, addmask[:, i, :])
        sc = P.sb_rot("sc", [128, 128], F32, 2)
        P.tt(sc, im4[:, 0, :], im4[:, 1, :], ALU.add)
        P.tt(sc, sc, im4[:, 2, :], ALU.add)
        P.tt(sc, sc, im4[:, 3, :], ALU.add)
        P.tt(sc, sc, am, ALU.add)
        m8 = P.sb_rot("m8", [128, 16], F32, 2)
        wk = P.sb_rot("wk", [128, 128], F32, 2)
        P.max8(m8[:, 0:8], sc)
        P.match_replace(wk, m8[:, 0:8], sc, -1e30)
        P.max8(m8[:, 8:16], wk)
        thr = P.sb_rot("thr", [128, 1], F32, 2)
        P.ts(thr, m8[:, 15:16], -1e29, ALU.max)
        P.ts(sc, sc, thr, ALU.is_ge)
        P.ts(sc, sc, -NEGBIG, ALU.mult, NEGBIG, ALU.add)
        pt = pp.nf()
        P.transpose(pt[:, 0:128], sc, idf)
        P.copy(negT[:, i * 128:(i + 1) * 128], pt[:, 0:128], eng="scalar")
    O = [pp.f[k] for k in range(4)]
    pp.f = pp.f[4:]
    f31 = P.sb([128, 1], F32)
    P.copy(f31, tzs[:, TZW - 1:TZW])
    for Q in range(T // 512):
        t0 = Q * 512
        kts = []
        for j in range(4 * Q + 4):
            off = t0 - j * 128
            if off >= 1664:
                b = ("far", f31)
            else:
                b = ("near", tzs[:, off + 384: off + 384 + 512])
            kts.append({"lhsT": KsT[:, j * 128:(j + 1) * 128], "mask": (ES[:, j, :], negT[:, t0:t0 + 512]), "bias": b})
        flash_block(P, pp, O, QT[:, t0:t0 + 512], 512, kts, None, lambda jj: Vs1[:, jj, :], 129)
        osel = P.sb_rot("osel", [128, 4, 128], F32, 2)
        flash_finish(P, O, lambda s: osel[:, s, :])
        j0 = max(0, 4 * Q - 4)
        kts = []
        for j in range(j0, 4 * Q + 4):
            off = t0 - j * 128
            kts.append({"lhsT": KwT[:, j * 128:(j + 1) * 128], "mask": None,
                        "bias": ("near", tws[:, off + 384: off + 384 + 512])})
        flash_block(P, pp, O, QT[:, t0:t0 + 512], 512, kts, None, lambda jj, j0=j0: Vw1[:, j0 + jj, :], 129)
        owin = P.sb_rot("owin", [128, 4, 128], F32, 2)
        flash_finish(P, O, lambda s: owin[:, s, :])
        oc = P.sb_rot("occ", [128, 4, 128], F32, 2)
        P.dma(oc, ocmp[t0:t0 + 512, :].rearrange("(s p) d -> p s d", p=128))
        yo = P.sb_rot("yo", [128, 4, 128], F32, 2)
        for s in range(4):
            ti = Q * 4 + s
            P.ts(yo[:, s, :], oc[:, s, :], sig[:, ti, 0:1], ALU.mult, eng="gpsimd")
            P.stt(yo[:, s, :], osel[:, s, :], sig[:, ti, 1:2], yo[:, s, :], ALU.mult, ALU.add)
            P.stt(yo[:, s, :], owin[:, s, :], sig[:, ti, 2:3], yo[:, s, :], ALU.mult, ALU.add)
        P.dma(y[t0:t0 + 512, :].rearrange("(s p) d -> p s d", p=128), yo, is_output=True)
    return P.finish()


def host_modd_consts():
    f32 = np.float32
    n = (np.arange(4)[None, :, None] * 128 + np.arange(128)[:, None, None])
    j = np.arange(128)[None, None, :]
    ovl = ((16 * n < 64 * j + 64) & (16 * n + 32 > 64 * j) & (n < 511)).astype(f32)
    t = np.arange(T)[:, None]
    jj = np.arange(128)[None, :]
    cur = t // 64
    forced = (jj == 0) | (jj == cur) | (jj == cur - 1)
    am = np.where(forced, 100.0, np.where(jj <= cur, 0.0, -1e30)).astype(f32)
    addmask = np.ascontiguousarray(am.reshape(64, 128, 128).transpose(1, 0, 2))
    b = np.arange(128)[:, None, None]
    jt = np.arange(64)[None, :, None]
    k = np.arange(128)[None, None, :]
    esel = (b == 2 * jt + k // 64).astype(f32)
    inv = (1.0 / (np.float32(10000.0) ** (np.arange(0, 128, 2, dtype=f32) / np.float32(128)))).astype(f32)
    ang = (np.arange(T, dtype=f32)[:, None] * inv[None, :]).astype(f32)
    cos = np.cos(ang).astype(f32)
    sin = np.sin(ang).astype(f32)
    d = np.arange(128)
    sgn = np.where(d % 2 == 0, -1.0, 1.0).astype(f32)
    C = np.ascontiguousarray(cos[:, d // 2].T)
    S = np.ascontiguousarray((sin[:, d // 2] * sgn[None, :]).T)
    csT = np.stack([C, S]).astype(f32)
    return {"ovl": ovl, "addmask": addmask, "esel": esel, "csT": csT}


def host_ret_consts(hh):
    f32 = np.float32
    log_g = np.log(f32(1.0) - f32(2.0) ** (f32(-5.0) - f32(hh))).astype(f32)
    s = np.arange(128)[:, None]
    l = np.arange(128)[None, :]
    drt = np.where(l >= s, SCALE * np.exp(log_g * np.maximum(l - s, 0)), 0.0).astype(f32)
    i = np.arange(128)
    rdec = np.stack([np.exp(log_g * (i + 1)), SCALE * np.exp(log_g * (127 - i)),
                     np.full(128, np.exp(log_g * 128)), np.zeros(128)], axis=1).astype(f32)
    return drt, rdec


def host_modd_a_inputs(p, c, cmp_pe, cmp_w1, cmp_w2, rel_bias, consts):
    kvh = c // 4
    f32 = np.float32
    def colsT(c0):
        return np.ascontiguousarray(p[:, c0:c0 + 128].T)
    perm = np.arange(128) ^ 1
    rqT = colsT(2584 + 128 * c)
    rkT = colsT(3608 + 128 * c)
    drt, rdec = host_ret_consts(c)
    return {
        "qT": colsT(128 * c),
        "kvcT": np.stack([colsT(1024 + 128 * kvh), colsT(1280 + 128 * kvh)]),
        "w1": np.ascontiguousarray(cmp_w1), "w2": np.ascontiguousarray(cmp_w2),
        "peT": np.ascontiguousarray(cmp_pe.transpose(0, 2, 1)),
        "tc": host_tz(rel_bias[:, c], 0, TCW, 16, 31),
        "ovl": consts["ovl"],
        "rq4T": np.stack([rqT, rqT[perm], rkT, rkT[perm]]),
        "csT": consts["csT"],
        "rv": np.ascontiguousarray(p[:, 4632 + 128 * c: 4632 + 128 * c + 128]),
        "rg": np.ascontiguousarray(p[:, 5656 + 128 * c: 5656 + 128 * c + 128]),
        "rdec": rdec, "drt": drt,
    }


def host_modd_b_inputs(p, c, imps, ocmp_c, rel_bias, consts):
    kvh = c // 4
    def colsT(c0):
        return np.ascontiguousarray(p[:, c0:c0 + 128].T)
    return {
        "qT": colsT(128 * c),
        "kswT": np.stack([colsT(1536 + 128 * kvh), colsT(2048 + 128 * kvh)]),
        "vsw": np.stack([p[:, 1792 + 128 * kvh: 1792 + 128 * kvh + 128], p[:, 2304 + 128 * kvh: 2304 + 128 * kvh + 128]]),
        "gates": np.ascontiguousarray(p[:, 2560 + 3 * c: 2560 + 3 * c + 3]),
        "imp4": np.stack([imps[4 * kvh + g] for g in range(4)]),
        "ocmp": ocmp_c,
        "tz": host_tz(rel_bias[:, c], -384, TZW, 1, 0),
        "tw": host_tz(rel_bias[:, c], -384, TWW, 1, 0, band=512),
        "addmask": consts["addmask"], "esel": consts["esel"],
    }


def _run(nc, maps):
    res = run_bass_kernel_spmd(nc, maps, core_ids=list(range(NCORE)))
    return res.results


def _bc(v, n=128):
    return np.ascontiguousarray(np.broadcast_to(np.asarray(v, np.float32), (n, v.shape[-1])))


def kernel(x, rel_bias, ev_w_in, ev_conv_w, ev_conv_b, ev_dt_bias, ev_a_log, ev_d_skip, ev_norm_w, ev_w_out,
           od_w_in, od_cmp_pe, od_cmp_w1, od_cmp_w2, od_w_out,
           ffn_w_up, ffn_conv_w, ffn_conv_b, ffn_w_down, ln_g, ln_b):
    f32 = np.float32
    A = lambda a: np.ascontiguousarray(np.asarray(a, f32))
    rel_bias = A(rel_bias)
    h = A(x)[0]
    consts = host_modd_consts()
    rows = [slice(c * TPC, (c + 1) * TPC) for c in range(NCORE)]
    depth = ln_g.shape[0]
    for layer in range(depth):
        i = layer // 2
        even = layer % 2 == 0
        w_in = A(ev_w_in[i] if even else od_w_in[i])
        IN = w_in.shape[1]
        res = _run(build_inproj(IN), [{"h": h[rows[c]], "w": w_in} for c in range(NCORE)])
        p = np.concatenate([r["p"] for r in res], 0)
        del res
        if even:
            res = _run(build_meven(), [host_meven_inputs(p, c, A(ev_conv_w[i]), A(ev_conv_b[i]), A(ev_dt_bias[i]),
                                                         A(ev_a_log[i]), A(ev_d_skip[i]), rel_bias) for c in range(NCORE)])
            ymix = np.concatenate([r["y"][:, 0:128] for r in res] + [r["y"][:, 128:256] for r in res], 1)
            w_out = A(ev_w_out[i])
        else:
            ra = _run(build_modd_a(), [host_modd_a_inputs(p, c, A(od_cmp_pe[i]), A(od_cmp_w1[i]), A(od_cmp_w2[i]),
                                                          rel_bias, consts) for c in range(NCORE)])
            imps = [r["imp"] for r in ra]
            rb = _run(build_modd_b(), [host_modd_b_inputs(p, c, imps, ra[c]["ocmp"], rel_bias, consts)
                                       for c in range(NCORE)])
            ymix = np.concatenate([r["y"] for r in rb] + [r["yret"] for r in ra], 1)
            w_out = A(od_w_out[i])
        ymix = np.ascontiguousarray(ymix)
        lng, lnb = _bc(A(ln_g[layer, 0])), _bc(A(ln_b[layer, 0]))
        maps = []
        for c in range(NCORE):
            m = {"ymix": ymix[rows[c]], "hprev": h[rows[c]], "w": w_out, "lng": lng, "lnb": lnb}
            if even:
                m["nw"] = _bc(A(ev_norm_w[i]))
            maps.append(m)
        res = _run(build_p1(even), maps)
        hm = np.concatenate([r["hmid"] for r in res], 0)
        lng, lnb = _bc(A(ln_g[layer, 1])), _bc(A(ln_b[layer, 1]))
        cw = host_cw(A(ffn_conv_w[layer]), A(ffn_conv_b[layer]))
        wup, wdn = A(ffn_w_up[layer]), A(ffn_w_down[layer])
        maps = []
        for c in range(NCORE):
            halo = hm[c * TPC - 2:c * TPC] if c > 0 else np.zeros((2, D), f32)
            maps.append({"hmid": hm[rows[c]], "haloT": np.ascontiguousarray(halo.T), "wup": wup, "cw": cw,
                         "wdn": wdn, "lng": lng, "lnb": lnb})
        res = _run(build_p2(), maps)
        h = np.concatenate([r["hout"] for r in res], 0)
    return h[None].astype(f32)
```

```python
from contextlib import ExitStack
import numpy as np
import concourse.bass as bass
import concourse.mybir as mybir
from concourse.bass_utils import run_bass_kernel_spmd

F32 = mybir.dt.float32
BF16 = mybir.dt.bfloat16
ALU = mybir.AluOpType
AF = mybir.ActivationFunctionType
AX = mybir.AxisListType


class Tok:
    __slots__ = ("w", "rd")

    def __init__(self):
        self.w = None
        self.rd = {}


class V:
    __slots__ = ("ap", "toks")

    def __init__(self, ap, toks):
        self.ap = ap
        self.toks = toks

    def __getitem__(self, idx):
        return V(self.ap[idx], self.toks)

    def re(self, pat, **kw):
        return V(self.ap.rearrange(pat, **kw), self.toks)


def _ap(x):
    return x.ap if isinstance(x, V) else x


def _toks(xs):
    out = []
    for x in xs:
        if isinstance(x, V):
            out.extend(x.toks)
        elif isinstance(x, Tok):
            out.append(x)
    return out


NDEV = 8


class Prog:
    ENGS = ("tensor", "vector", "scalar", "gpsimd", "sync")
    NDMA = 8

    def __init__(self):
        self.nc = bass.Bass("TRN2", target_bir_lowering=False, num_devices=NDEV)
        self.es = ExitStack()
        self.ops = {e: [] for e in self.ENGS}
        self.cnt = {e: 0 for e in self.ENGS}
        self.waited = {e: {} for e in self.ENGS}
        self.sems = {}
        for e in self.ENGS:
            self.sems[e] = self.es.enter_context(self.nc.semaphore("s_" + e))
        self.dsem = {}
        self.dcnt = {}
        self.dnext = {}
        for q in ("sync", "gpsimd", "scalar"):
            for i in range(self.NDMA):
                k = "d_%s_%d" % (q, i)
                self.sems[k] = self.es.enter_context(self.nc.semaphore(k))
                self.dcnt[k] = 0
            self.dnext[q] = 0
        self.out_events = []
        self.pending = {}
        self.strict = ("vector", "scalar", "gpsimd")
        self.nalloc = 0

    def dram(self, name, shape, dt, kind):
        return self.nc.dram_tensor(name, list(shape), dt, kind=kind).ap()

    def sb(self, shape, dt, name=None):
        self.nalloc += 1
        es = self.scopes[-1][0] if getattr(self, "scopes", None) else self.es
        t = es.enter_context(self.nc.sbuf_tensor(name or ("sb%d_%d" % (self.nalloc, len(getattr(self, "scopes", [])))), list(shape), dt))
        return V(t[:], [Tok()])

    def scope_begin(self):
        if not hasattr(self, "scopes"):
            self.scopes = []
        self.scopes.append((ExitStack(), set()))

    def scope_end(self):
        self.barrier()
        es, keys = self.scopes.pop()
        for k in keys:
            self._rot.pop(k, None)
        es.close()

    def barrier(self):
        allev = {x: self.cnt[x] for x in self.ENGS}
        allev.update(self.dcnt)
        if "cc" in self.sems:
            allev["cc"] = self.cccnt
        for e in self.ENGS:
            for k, v in allev.items():
                if k == e or v == 0:
                    continue
                if self.waited[e].get(k, 0) < v:
                    self.waited[e][k] = v
                    self.pending.setdefault(e, []).append((k, v))

    def sb_rot(self, key, shape, dt, n):
        if not hasattr(self, "_rot"):
            self._rot = {}
        if key not in self._rot:
            self.nrot = getattr(self, "nrot", 0) + 1
            self._rot[key] = [[self.sb(shape, dt, name="%s_%d_%d" % (key, j, self.nrot)) for j in range(n)], 0]
            if getattr(self, "scopes", None):
                self.scopes[-1][1].add(key)
        ent = self._rot[key]
        v = ent[0][ent[1] % n]
        ent[1] += 1
        return v

    def ps(self, shape, dt=F32, name=None):
        self.nalloc += 1
        t = self.es.enter_context(self.nc.psum_tensor(name or ("ps%d" % self.nalloc), list(shape), dt))
        return V(t[:], [Tok()])

    def _deps(self, eng, reads, writes):
        need = {}

        def add(ev):
            if ev is None:
                return
            k, v = ev
            if need.get(k, 0) < v:
                need[k] = v
        for t in _toks(reads):
            add(t.w)
        for t in _toks(writes):
            add(t.w)
            for k, v in t.rd.items():
                add((k, v))
        waits = []
        wd = self.waited[eng]
        for k, v in need.items():
            if k == eng and not (self.strict and eng in self.strict):
                continue
            if wd.get(k, 0) >= v:
                continue
            wd[k] = v
            waits.append((k, v))
        return waits

    def _commit(self, ev, reads, writes):
        k, v = ev
        for t in _toks(reads):
            if t.rd.get(k, 0) < v:
                t.rd[k] = v
        for t in _toks(writes):
            t.w = ev
            t.rd = {}

    def op(self, eng, fn, reads, writes):
        waits = self.pending.pop(eng, []) + self._deps(eng, reads, writes)
        self.cnt[eng] += 1
        ev = (eng, self.cnt[eng])
        self.ops[eng].append((waits, fn, (eng, 1)))
        self._commit(ev, reads, writes)
        return ev

    def dma(self, out, in_, q="sync", is_output=False, **kw):
        reads = [in_]
        writes = [out]
        i = self.dnext[q]
        self.dnext[q] = (i + 1) % self.NDMA
        k = "d_%s_%d" % (q, i)
        waits = self.pending.pop(q, []) + self._deps(q, reads, writes)
        prev = self.dcnt[k]
        if prev > 0 and self.waited[q].get(k, 0) < prev:
            self.waited[q][k] = prev
            waits.append((k, prev))
        self.dcnt[k] += 16
        ev = (k, self.dcnt[k])
        o, i_ = _ap(out), _ap(in_)
        self.ops[q].append((waits, lambda e: e.dma_start(out=o, in_=i_, **kw), (k, 16)))
        self._commit(ev, reads, writes)
        if is_output:
            self.out_events.append(ev)
        return ev

    def dma_fn(self, fn, reads, writes, q="sync", is_output=False):
        i = self.dnext[q]
        self.dnext[q] = (i + 1) % self.NDMA
        k = "d_%s_%d" % (q, i)
        waits = self._deps(q, reads, writes)
        prev = self.dcnt[k]
        if prev > 0 and self.waited[q].get(k, 0) < prev:
            self.waited[q][k] = prev
            waits.append((k, prev))
        self.dcnt[k] += 16
        ev = (k, self.dcnt[k])
        self.ops[q].append((waits, fn, (k, 16)))
        self._commit(ev, reads, writes)
        if is_output:
            self.out_events.append(ev)
        return ev

    def coll(self, kind, in_, out, groups, q="gpsimd", op=ALU.bypass):
        if "cc" not in self.sems:
            self.sems["cc"] = self.es.enter_context(self.nc.semaphore("cc_sem"))
            self.cccnt = 0
        waits = self._deps(q, [in_], [out])
        if self.cccnt > 0 and self.waited[q].get("cc", 0) < self.cccnt:
            self.waited[q]["cc"] = self.cccnt
            waits.append(("cc", self.cccnt))
        self.cccnt += 1
        ev = ("cc", self.cccnt)
        i_, o_ = _ap(in_).opt(), _ap(out).opt()
        self.ops[q].append((waits, lambda e: e.collective_compute(kind, op, groups, [i_], [o_]), ("cc", None)))
        self._commit(ev, [in_], [out])
        return ev

    def finish(self):
        nc = self.nc
        fin = {}
        for k, v in self.out_events:
            fin[k] = max(fin.get(k, 0), v)
        sems = self.sems
        ops = self.ops
        with nc.Block() as block:
            def runner(name):
                def body(e):
                    for waits, fn, (sk, inc) in ops[name]:
                        for k, v in waits:
                            e.wait_ge(sems[k], v)
                        ins = fn(e)
                        if inc is None:
                            ins.then_inc(sems[sk])
                        else:
                            ins.then_inc(sems[sk], inc)
                    if name == "sync":
                        for k, v in fin.items():
                            e.wait_ge(sems[k], v)
                return body
            block.tensor(runner("tensor"))
            block.vector(runner("vector"))
            block.scalar(runner("scalar"))
            block.gpsimd(runner("gpsimd"))
            block.sync(runner("sync"))
        self.es.close()
        return nc

    def matmul(self, out, lhsT, rhs, start=True, stop=True):
        o, l, r = _ap(out), _ap(lhsT), _ap(rhs)
        return self.op("tensor", lambda e: e.matmul(o, l, r, start=start, stop=stop), [lhsT, rhs], [out])

    def transpose(self, out, in_, ident):
        o, i, d = _ap(out), _ap(in_), _ap(ident)
        return self.op("tensor", lambda e: e.transpose(o, i, d), [in_, ident], [out])

    def act(self, out, in_, func, bias=None, scale=1.0, accum_out=None, eng="scalar"):
        o, i = _ap(out), _ap(in_)
        b = _ap(bias) if bias is not None else None
        s = _ap(scale)
        a = _ap(accum_out) if accum_out is not None else None
        kw = {}
        if b is not None:
            kw["bias"] = b
        if a is not None:
            kw["accum_out"] = a
        return self.op("scalar", lambda e: e.activation(o, i, func, scale=s, **kw),
                       [in_, bias, scale], [out, accum_out])

    def tt(self, out, in0, in1, op, eng="vector"):
        o, a, b = _ap(out), _ap(in0), _ap(in1)
        return self.op(eng, lambda e: e.tensor_tensor(o, a, b, op), [in0, in1], [out])

    def ts(self, out, in0, s1, op0, s2=None, op1=None, accum_out=None, eng="vector"):
        o, a = _ap(out), _ap(in0)
        x1, x2 = _ap(s1), _ap(s2)
        acc = _ap(accum_out) if accum_out is not None else None
        kw = {}
        if op1 is not None:
            kw["op1"] = op1
        if acc is not None:
            kw["accum_out"] = acc
        return self.op(eng, lambda e: e.tensor_scalar(o, a, x1, x2, op0, **kw),
                       [in0, s1, s2], [out, accum_out])

    def stt(self, out, in0, scalar, in1, op0, op1, eng="vector"):
        o, a, s, b = _ap(out), _ap(in0), _ap(scalar), _ap(in1)
        return self.op(eng, lambda e: e.scalar_tensor_tensor(o, a, s, b, op0, op1), [in0, scalar, in1], [out])

    def copy(self, out, in_, eng="vector"):
        o, i = _ap(out), _ap(in_)
        if eng == "scalar":
            return self.op(eng, lambda e: e.copy(o, i), [in_], [out])
        return self.op(eng, lambda e: e.tensor_copy(o, i), [in_], [out])

    def memset(self, out, val, eng="vector"):
        o = _ap(out)
        return self.op(eng, lambda e: e.memset(o, val), [], [out])

    def reduce(self, out, in_, op, axis=AX.X, eng="vector"):
        o, i = _ap(out), _ap(in_)
        return self.op(eng, lambda e: e.tensor_reduce(o, i, axis, op), [in_], [out])

    def recip(self, out, in_):
        o, i = _ap(out), _ap(in_)
        return self.op("vector", lambda e: e.reciprocal(o, i), [in_], [out])

    def max8(self, out, in_):
        o, i = _ap(out), _ap(in_)
        return self.op("vector", lambda e: e.max(o, i), [in_], [out])

    def match_replace(self, out, to_replace, values, imm):
        o, r, v = _ap(out), _ap(to_replace), _ap(values)
        return self.op("vector", lambda e: e.match_replace(o, r, v, imm), [to_replace, values], [out])

    def bn_stats(self, out, in_):
        o, i = _ap(out), _ap(in_)
        return self.op("vector", lambda e: e.bn_stats(o, i), [in_], [out])

    def bn_aggr(self, out, in_):
        o, i = _ap(out), _ap(in_)
        return self.op("vector", lambda e: e.bn_aggr(o, i), [in_], [out])


T = 8192
D = 2048
NCORE = 8
TPC = T // NCORE
KC = D // 128
EVEN_IN = 5648
ODD_IN = 6680
FFN = 5632
FCH = FFN // 128
ALPHA = 8.0 ** 0.25
LN_EPS = 1e-5
SCALE = 128.0 ** -0.5
NEGBIG = -30000.0


def make_ident(P, dt):
    idf = P.sb([128, 128], F32)
    P.memset(idf, 0.0, eng="gpsimd")
    o = idf.ap
    P.op("gpsimd", lambda e: e.affine_select(out=o, in_=o, pattern=[[-1, 128]], compare_op=ALU.not_equal,
                                              fill=1.0, base=0, channel_multiplier=1), [idf], [idf])
    if dt == F32:
        return idf
    idb = P.sb([128, 128], dt)
    P.copy(idb, idf)
    return idb


class PsumPool:
    def __init__(self, P, nf32, nbf16=0):
        self.f = [P.ps([128, 512], F32) for _ in range(nf32)]
        self.b = [P.ps([128, 1024], BF16) for _ in range(nbf16)]
        self.fi = 0
        self.bi = 0

    def nf(self):
        x = self.f[self.fi % len(self.f)]
        self.fi += 1
        return x

    def nb(self):
        x = self.b[self.bi % len(self.b)]
        self.bi += 1
        return x


def load_hT(P, pp, ident_bf, h_dram, ntiles, hT, col0=0):
    for i in range(ntiles):
        hf = P.sb_rot("ldh_f", [128, D], F32, 2)
        P.dma(hf, h_dram[i * 128:(i + 1) * 128, :], q="sync")
        hb = P.sb_rot("ldh_b", [128, D], BF16, 2)
        P.copy(hb[:, 0:1024], hf[:, 0:1024], eng="vector")
        P.copy(hb[:, 1024:2048], hf[:, 1024:2048], eng="gpsimd")
        for g in range(KC // 8):
            pt = pp.nb()
            for j in range(8):
                kc = g * 8 + j
                P.transpose(pt[:, j * 128:(j + 1) * 128], hb[:, kc * 128:(kc + 1) * 128], ident_bf)
            dst = hT[:, g * 8:(g + 1) * 8, col0 + i * 128: col0 + (i + 1) * 128]
            src = pt.re("p (j t) -> p j t", t=128)
            if g % 2 == 0:
                P.copy(dst, src, eng="vector")
            else:
                P.copy(dst, src, eng="scalar")


def layer_norm_tile(P, pre, out, g_bc, b_bc):
    st = P.sb_rot("ln_st", [128, 4, 6], F32, 2)
    for c in range(4):
        P.bn_stats(st[:, c, :], pre[:, c * 512:(c + 1) * 512])
    mv = P.sb_rot("ln_mv", [128, 2], F32, 2)
    P.bn_aggr(mv, st.re("p a b -> p (a b)"))
    rs = P.sb_rot("ln_rs", [128, 1], F32, 2)
    P.ts(rs, mv[:, 1:2], LN_EPS, ALU.add)
    P.act(rs, rs, AF.Sqrt)
    P.recip(rs, rs)
    P.ts(out, pre, mv[:, 0:1], ALU.subtract, rs, ALU.mult)
    P.tt(out, out, g_bc, ALU.mult, eng="gpsimd")
    P.tt(out, out, b_bc, ALU.add)


def build_inproj(IN):
    P = Prog()
    h = P.dram("h", [TPC, D], F32, "ExternalInput")
    w = P.dram("w", [D, IN], F32, "ExternalInput")
    p = P.dram("p", [TPC, IN], F32, "ExternalOutput")
    pp = PsumPool(P, 4, 2)
    idb = make_ident(P, BF16)
    hT = P.sb([128, KC, TPC], BF16)
    load_hT(P, pp, idb, h, TPC // 128, hT)
    wv = w.rearrange("(kc p) n -> p kc n", p=128)
    nblk = (IN + 511) // 512
    for cb in range(nblk):
        c0 = cb * 512
        nc_ = min(512, IN - c0)
        wb = P.sb_rot("wblk", [128, KC, 512], BF16, 2)
        P.dma(wb[:, :, 0:nc_], wv[:, :, c0:c0 + nc_], q="gpsimd")
        for i in range(TPC // 128):
            ps = pp.nf()
            for kc in range(KC):
                P.matmul(ps[:, 0:nc_], hT[:, kc, i * 128:(i + 1) * 128], wb[:, kc, 0:nc_],
                         start=(kc == 0), stop=(kc == KC - 1))
            ob = P.sb_rot("ob", [128, 512], F32, 3)
            if i % 2 == 0:
                P.copy(ob[:, 0:nc_], ps[:, 0:nc_], eng="vector")
            else:
                P.copy(ob[:, 0:nc_], ps[:, 0:nc_], eng="scalar")
            P.dma(p[i * 128:(i + 1) * 128, c0:c0 + nc_], ob[:, 0:nc_], q="sync", is_output=True)
    return P.finish()


def build_p1(even):
    P = Prog()
    ymix = P.dram("ymix", [TPC, D], F32, "ExternalInput")
    hprev = P.dram("hprev", [TPC, D], F32, "ExternalInput")
    w = P.dram("w", [D, D], F32, "ExternalInput")
    lng = P.dram("lng", [128, D], F32, "ExternalInput")
    lnb = P.dram("lnb", [128, D], F32, "ExternalInput")
    if even:
        nw = P.dram("nw", [128, 1024], F32, "ExternalInput")
    hmid = P.dram("hmid", [TPC, D], F32, "ExternalOutput")
    pp = PsumPool(P, 4, 2)
    idb = make_ident(P, BF16)
    wb = P.sb([128, KC, D], BF16)
    wv = w.rearrange("(kc p) n -> p kc n", p=128)
    for c in range(4):
        P.dma(wb[:, :, c * 512:(c + 1) * 512], wv[:, :, c * 512:(c + 1) * 512], q="gpsimd")
    g_bc = P.sb([128, D], F32)
    b_bc = P.sb([128, D], F32)
    P.dma(g_bc, lng)
    P.dma(b_bc, lnb)
    if even:
        nw_bc = P.sb([128, 1024], F32)
        P.dma(nw_bc, nw)
    for i in range(TPC // 128):
        yf = P.sb_rot("yf", [128, D], F32, 2)
        P.dma(yf, ymix[i * 128:(i + 1) * 128, :])
        hp = P.sb_rot("hp", [128, D], F32, 2)
        P.dma(hp, hprev[i * 128:(i + 1) * 128, :])
        yb = P.sb_rot("yb", [128, D], BF16, 2)
        if even:
            ss = P.sb_rot("ss", [128, 2], F32, 2)
            junk = P.sb_rot("junk", [128, 512], F32, 1)
            for g in range(2):
                P.act(junk, yf[:, g * 512:(g + 1) * 512], AF.Square, accum_out=ss[:, g:g + 1])
            P.ts(ss, ss, 1.0 / 512.0, ALU.mult, LN_EPS, ALU.add)
            P.act(ss, ss, AF.Sqrt)
            P.recip(ss, ss)
            for g in range(2):
                P.stt(yb[:, g * 512:(g + 1) * 512], yf[:, g * 512:(g + 1) * 512], ss[:, g:g + 1],
                      nw_bc[:, g * 512:(g + 1) * 512], ALU.mult, ALU.mult)
            P.copy(yb[:, 1024:2048], yf[:, 1024:2048], eng="gpsimd")
        else:
            P.copy(yb[:, 0:1024], yf[:, 0:1024], eng="vector")
            P.copy(yb[:, 1024:2048], yf[:, 1024:2048], eng="gpsimd")
        yT = P.sb_rot("yT", [128, KC, 128], BF16, 2)
        for g in range(2):
            pt = pp.nb()
            for j in range(8):
                kc = g * 8 + j
                P.transpose(pt[:, j * 128:(j + 1) * 128], yb[:, kc * 128:(kc + 1) * 128], idb)
            if g == 0:
                P.copy(yT[:, 0:8, :], pt.re("p (j t) -> p j t", t=128), eng="vector")
            else:
                P.copy(yT[:, 8:16, :], pt.re("p (j t) -> p j t", t=128), eng="scalar")
        pre = P.sb_rot("pre", [128, D], F32, 2)
        for cb in range(4):
            ps = pp.nf()
            for kc in range(KC):
                P.matmul(ps, yT[:, kc, :], wb[:, kc, cb * 512:(cb + 1) * 512], start=(kc == 0), stop=(kc == KC - 1))
            P.stt(pre[:, cb * 512:(cb + 1) * 512], hp[:, cb * 512:(cb + 1) * 512], ALPHA, ps, ALU.mult, ALU.add)
        ot = P.sb_rot("ot", [128, D], F32, 2)
        layer_norm_tile(P, pre, ot, g_bc, b_bc)
        P.dma(hmid[i * 128:(i + 1) * 128, :], ot, is_output=True)
    return P.finish()


def build_p2(next_IN=None):
    P = Prog()
    hmid = P.dram("hmid", [TPC, D], F32, "ExternalInput")
    haloT = P.dram("haloT", [D, 2], F32, "ExternalInput")
    wup = P.dram("wup", [D, 2 * FFN], F32, "ExternalInput")
    cw = P.dram("cw", [128, FCH, 4], F32, "ExternalInput")
    wdn = P.dram("wdn", [FFN, D], F32, "ExternalInput")
    lng = P.dram("lng", [128, D], F32, "ExternalInput")
    lnb = P.dram("lnb", [128, D], F32, "ExternalInput")
    hout = P.dram("hout", [TPC, D], F32, "ExternalOutput")
    if next_IN is not None:
        wnext = P.dram("wnext", [D, next_IN], F32, "ExternalInput")
        pnext = P.dram("pnext", [TPC, next_IN], F32, "ExternalOutput")
    pp = PsumPool(P, 6, 2)
    idb = make_ident(P, BF16)
    idf = make_ident(P, F32)
    g_bc = P.sb([128, D], F32)
    b_bc = P.sb([128, D], F32)
    P.dma(g_bc, lng)
    P.dma(b_bc, lnb)
    cws = P.sb([128, FCH, 4], F32)
    P.dma(cws, cw)
    wupv = wup.rearrange("(kc p) n -> p kc n", p=128)
    wdnv = wdn.rearrange("(f p) n -> p f n", p=128)
    gT = P.sb([128, FCH, TPC], BF16)
    NT = TPC // 128
    P.scope_begin()
    hT = P.sb([128, KC, 2 + TPC], BF16)
    P.dma(hT[:, :, 0:2], haloT.rearrange("(kc p) t -> p kc t", p=128), q="gpsimd")
    load_hT(P, pp, idb, hmid, NT, hT, col0=2)
    for f2 in range(FCH // 2):
        wa = P.sb_rot("wa", [128, KC, 256], BF16, 2)
        wu = P.sb_rot("wu", [128, KC, 256], BF16, 2)
        P.dma(wa, wupv[:, :, f2 * 256:(f2 + 1) * 256], q="gpsimd")
        P.dma(wu, wupv[:, :, FFN + f2 * 256: FFN + (f2 + 1) * 256], q="gpsimd")
        for fi in range(2):
            f = f2 * 2 + fi
            fs = slice(fi * 128, (fi + 1) * 128)
            for b in range(TPC // 256):
                c0 = b * 256
                pa = pp.nf()
                pu = pp.nf()
                for kc in range(KC):
                    P.matmul(pa[:, 0:258], wa[:, kc, fs], hT[:, kc, c0:c0 + 258], start=(kc == 0), stop=(kc == KC - 1))
                for kc in range(KC):
                    P.matmul(pu[:, 0:256], wu[:, kc, fs], hT[:, kc, c0 + 2:c0 + 258], start=(kc == 0), stop=(kc == KC - 1))
                ac = P.sb_rot("ac", [128, 256], F32, 2)
                P.act(ac, pa[:, 2:258], AF.Identity, bias=cws[:, f, 3:4], scale=cws[:, f, 2:3])
                P.stt(ac, pa[:, 1:257], cws[:, f, 1:2], ac, ALU.mult, ALU.add)
                P.stt(ac, pa[:, 0:256], cws[:, f, 0:1], ac, ALU.mult, ALU.add)
                sg = P.sb_rot("sg", [128, 256], F32, 2)
                P.act(sg, ac, AF.Silu)
                P.tt(gT[:, f, b * 256:(b + 1) * 256], sg, pu[:, 0:256], ALU.mult)
    P.scope_end()
    P.scope_begin()
    pre = [P.sb([128, D], F32) for _ in range(NT)]
    for i in range(NT):
        P.dma(pre[i], hmid[i * 128:(i + 1) * 128, :])
    for cb in range(D // 128):
        wd = P.sb_rot("wd", [128, FCH, 128], BF16, 2)
        P.dma(wd, wdnv[:, :, cb * 128:(cb + 1) * 128], q="gpsimd")
        for hf in range(TPC // 512):
            ps = pp.nf()
            for f in range(FCH):
                P.matmul(ps, wd[:, f, :], gT[:, f, hf * 512:(hf + 1) * 512], start=(f == 0), stop=(f == FCH - 1))
            dst = P.sb_rot("dst", [128, 512], F32, 2)
            P.copy(dst, ps, eng="scalar")
            pt = pp.nf()
            for i in range(4):
                P.transpose(pt[:, i * 128:(i + 1) * 128], dst[:, i * 128:(i + 1) * 128], idf)
            for i in range(4):
                pr = pre[hf * 4 + i]
                P.stt(pr[:, cb * 128:(cb + 1) * 128], pr[:, cb * 128:(cb + 1) * 128], ALPHA,
                      pt[:, i * 128:(i + 1) * 128], ALU.mult, ALU.add)
    for i in range(NT):
        ot = P.sb_rot("ot", [128, D], F32, 1)
        layer_norm_tile(P, pre[i], ot, g_bc, b_bc)
        P.dma(hout[i * 128:(i + 1) * 128, :], ot, is_output=True)
    P.scope_end()
    if next_IN is not None:
        P.scope_begin()
        hT2 = P.sb([128, KC, TPC], BF16)
        load_hT(P, pp, idb, hout, NT, hT2)
        wv = wnext.rearrange("(kc p) n -> p kc n", p=128)
        nblk = (next_IN + 511) // 512
        for cb in range(nblk):
            c0 = cb * 512
            nc_ = min(512, next_IN - c0)
            wb = P.sb_rot("wblk", [128, KC, 512], BF16, 2)
            P.dma(wb[:, :, 0:nc_], wv[:, :, c0:c0 + nc_], q="gpsimd")
            for i in range(NT):
                ps = pp.nf()
                for kc in range(KC):
                    P.matmul(ps[:, 0:nc_], hT2[:, kc, i * 128:(i + 1) * 128], wb[:, kc, 0:nc_],
                             start=(kc == 0), stop=(kc == KC - 1))
                ob = P.sb_rot("ob", [128, 512], F32, 3)
                if i % 2 == 0:
                    P.copy(ob[:, 0:nc_], ps[:, 0:nc_], eng="vector")
                else:
                    P.copy(ob[:, 0:nc_], ps[:, 0:nc_], eng="scalar")
                P.dma(pnext[i * 128:(i + 1) * 128, c0:c0 + nc_], ob[:, 0:nc_], q="sync", is_output=True)
        P.scope_end()
    return P.finish()


def host_cw(conv_w, conv_b):
    a = np.concatenate([conv_w, conv_b[None, :]], axis=0)
    return np.ascontiguousarray(a.reshape(4, FCH, 128).transpose(2, 1, 0)).astype(np.float32)


TZW = 2560


def flash_block(P, pp, O, QT_blk, nq, k_tiles, scale_bias_fn, pv_fn, n_out):
    nk = len(k_tiles)
    nsub = nq // 128
    pss = {}

    def emit_s(jj):
        kt = k_tiles[jj]
        ps = pp.nf()
        has_mask = kt.get("mask") is not None
        P.matmul(ps[:, 0:nq], kt["lhsT"], QT_blk, start=True, stop=not has_mask)
        if has_mask:
            ml, mr = kt["mask"]
            P.matmul(ps[:, 0:nq], ml, mr, start=False, stop=True)
        pss[jj] = ps

    def emit_rest(jj):
        kt = k_tiles[jj]
        ps = pss.pop(jj)
        pt = P.sb_rot("fl_pt", [128, 512], BF16, 3)
        kind, bap = kt["bias"]
        if kind == "far":
            P.act(pt[:, 0:nq], ps[:, 0:nq], AF.Exp, bias=bap, scale=SCALE)
        else:
            tmp = P.sb_rot("fl_tmp", [128, 512], F32, 2)
            P.stt(tmp[:, 0:nq], ps[:, 0:nq], SCALE, bap, ALU.mult, ALU.add)
            P.act(pt[:, 0:nq], tmp[:, 0:nq], AF.Exp)
        for s in range(nsub):
            P.matmul(O[s][:, 0:n_out], pt[:, s * 128:(s + 1) * 128], pv_fn(jj), start=(jj == 0), stop=(jj == nk - 1))

    emit_s(0)
    for jj in range(nk):
        if jj + 1 < nk:
            emit_s(jj + 1)
        emit_rest(jj)


def flash_finish(P, Os, dst_fn, ncols=128, extra=None):
    for s, O in enumerate(Os):
        den = P.sb_rot("fl_den", [128, 1], F32, 4)
        P.ts(den, O[:, 128:129], 1e-30, ALU.max)
        P.recip(den, den)
        P.ts(dst_fn(s), O[:, 0:ncols], den, ALU.mult)
        if extra is not None:
            extra(s, O, den)


def flash_block_gen(P, pp, O, QT_blk, nq, k_tiles, pv_fn, n_out):
    nk = len(k_tiles)
    nsub = nq // 128
    pss = {}

    def emit_s(jj):
        kt = k_tiles[jj]
        ps = pp.nf()
        has_mask = kt.get("mask") is not None
        P.matmul(ps[:, 0:nq], kt["lhsT"], QT_blk, start=True, stop=not has_mask)
        if has_mask:
            ml, mr = kt["mask"]
            P.matmul(ps[:, 0:nq], ml, mr, start=False, stop=True)
        pss[jj] = ps

    def emit_rest(jj):
        kt = k_tiles[jj]
        ps = pss.pop(jj)
        pt = P.sb_rot("fl_pt", [128, 512], BF16, 3)
        kind, bap = kt["bias"]
        if kind == "far":
            P.act(pt[:, 0:nq], ps[:, 0:nq], AF.Exp, bias=bap, scale=SCALE)
        else:
            tmp = P.sb_rot("fl_tmp", [128, 512], F32, 2)
            P.stt(tmp[:, 0:nq], ps[:, 0:nq], SCALE, bap, ALU.mult, ALU.add)
            P.act(pt[:, 0:nq], tmp[:, 0:nq], AF.Exp)
        for s_ in range(nsub):
            P.matmul(O[s_][:, 0:n_out], pt[:, s_ * 128:(s_ + 1) * 128], pv_fn(jj), start=(jj == 0), stop=(jj == nk - 1))

    emit_s(0)
    for jj in range(nk):
        if jj + 1 < nk:
            emit_s(jj + 1)
        emit_rest(jj)
        yield


def interleave(ga, na, gb, nb):
    ia = ib = 0
    da = db = False
    while not (da and db):
        if not da and (db or ia * nb <= ib * na):
            try:
                next(ga)
                ia += 1
            except StopIteration:
                da = True
        else:
            try:
                next(gb)
                ib += 1
            except StopIteration:
                db = True


def build_meven():
    P = Prog()
    z = P.dram("z", [T, 128], F32, "ExternalInput")
    xbcT = P.dram("xbcT", [3, 128, T + 3], F32, "ExternalInput")
    cwx = P.dram("cwx", [128, 3, 5], F32, "ExternalInput")
    dtr = P.dram("dtr", [T, 2], F32, "ExternalInput")
    hpar = P.dram("hpar", [128, 6], F32, "ExternalInput")
    qT = P.dram("qT", [128, T], F32, "ExternalInput")
    kT = P.dram("kT", [128, T], F32, "ExternalInput")
    v = P.dram("v", [T, 128], F32, "ExternalInput")
    tz = P.dram("tz", [128, TZW], F32, "ExternalInput")
    pastneg = P.dram("pastneg", [128, 64, 32], F32, "ExternalInput")
    ownneg = P.dram("ownneg", [128, 64, 32], F32, "ExternalInput")
    ejd = P.dram("ej", [128, 32, 128], F32, "ExternalInput")
    cst = P.dram("cst", [128, 3, 128], F32, "ExternalInput")
    y = P.dram("y", [T, 256], F32, "ExternalOutput")

    banks = [P.ps([128, 512], F32) for _ in range(8)]
    ppS = PsumPool(P, 0, 0)
    ppS.f = banks[0:2]
    ppM = PsumPool(P, 0, 0)
    ppM.f = banks[6:8]
    O = banks[2:6]
    idf = make_ident(P, F32)
    csts = P.sb([128, 3, 128], F32)
    P.dma(csts, cst)
    triu, ones, causneg = csts[:, 0, :], csts[:, 1, :], csts[:, 2, :]
    cw = P.sb([128, 3, 5], F32)
    P.dma(cw, cwx)
    hp = P.sb([128, 6], F32)
    P.dma(hp, hpar)
    a_bc = P.sb([128, 2], F32)
    P.act(a_bc, hp[:, 2:4], AF.Exp)
    P.ts(a_bc, a_bc, -1.0, ALU.mult)

    QT = P.sb([128, T], BF16)
    KT = P.sb([128, T], BF16)
    V1 = P.sb([128, T // 128, 129], BF16)
    for c in range(4):
        P.dma(QT[:, c * 2048:(c + 1) * 2048], qT[:, c * 2048:(c + 1) * 2048], q="gpsimd")
        P.dma(KT[:, c * 2048:(c + 1) * 2048], kT[:, c * 2048:(c + 1) * 2048], q="gpsimd")
        P.dma(V1[:, c * 16:(c + 1) * 16, 0:128], v[c * 2048:(c + 1) * 2048, :].rearrange("(j p) d -> p j d", p=128), q="gpsimd")
    P.memset(V1[:, :, 128:129], 1.0)
    tzs = P.sb([128, TZW], F32)
    P.dma(tzs, tz)
    pn = P.sb([128, 64, 32], F32)
    on = P.sb([128, 64, 32], F32)
    P.dma(pn, pastneg)
    P.dma(on, ownneg)
    EJ = P.sb([128, 32, 128], BF16)
    P.dma(EJ, ejd, q="gpsimd")
    f31 = P.sb([128, 1], F32)
    P.copy(f31, tzs[:, TZW - 1:TZW])
    kmT = P.sb([128, 32], F32)
    negT = P.sb([128, T], BF16)
    P.memset(negT[:, 0:T // 2], 0.0, eng="gpsimd")
    P.memset(negT[:, T // 2:T], 0.0, eng="gpsimd")

    H = P.sb([128, 128], F32)
    Hb = P.sb([128, 128], BF16)
    P.memset(H, 0.0)
    P.memset(Hb, 0.0)

    def ssd_gen():
        for sc in range(T // 512):
            t0 = sc * 512
            fm = []
            for wch in range(3):
                raw = P.sb_rot("raw%d" % wch, [128, 515], F32, 2)
                P.dma(raw, xbcT[wch, :, t0:t0 + 515])
                acc = P.sb_rot("cacc%d" % wch, [128, 512], F32, 2)
                P.act(acc, raw[:, 3:515], AF.Identity, bias=cw[:, wch, 4:5], scale=cw[:, wch, 3:4])
                for k in (2, 1, 0):
                    P.stt(acc, raw[:, k:k + 512], cw[:, wch, k:k + 1], acc, ALU.mult, ALU.add)
                o = P.sb_rot("cv%d" % wch, [128, 512], F32, 2)
                P.act(o, acc, AF.Silu)
                fm.append(o)
                yield
            xs, Bf, Cf = fm
            BT = P.sb_rot("BTb", [128, 512], BF16, 2)
            CT = P.sb_rot("CTb", [128, 512], BF16, 2)
            P.copy(BT, Bf, eng="gpsimd")
            P.copy(CT, Cf, eng="gpsimd")
            zt = P.sb_rot("zt", [128, 4, 128], F32, 2)
            P.dma(zt, z[t0:t0 + 512, :].rearrange("(i p) d -> p i d", p=128))
            dtt = P.sb_rot("dtt", [128, 4, 2], F32, 2)
            P.dma(dtt, dtr[t0:t0 + 512, :].rearrange("(i p) d -> p i d", p=128))
            dt = P.sb_rot("dt", [128, 4, 2], F32, 2)
            for i in range(4):
                P.tt(dt[:, i, :], dtt[:, i, :], hp[:, 0:2], ALU.add)
            P.act(dt, dt, AF.Exp)
            P.act(dt, dt, AF.Ln, bias=1.0)
            dtA = P.sb_rot("dtA", [128, 4, 2], F32, 2)
            for i in range(4):
                P.tt(dtA[:, i, :], dt[:, i, :], a_bc, ALU.mult)
            sz = P.sb_rot("sz", [128, 4, 128], F32, 2)
            P.act(sz, zt, AF.Silu)
            yield
            for i in range(4):
                cs = slice(i * 128, (i + 1) * 128)
                pa = ppS.nf()
                P.matmul(pa[:, 0:2], triu, dtA[:, i, :])
                P.matmul(pa[:, 2:4], ones, dtA[:, i, :])
                ac = P.sb_rot("ac", [128, 4], F32, 2)
                P.copy(ac, pa[:, 0:4])
                dec = P.sb_rot("dec", [128, 6], F32, 2)
                P.act(dec[:, 0:2], ac[:, 0:2], AF.Exp)
                P.tt(dec[:, 2:4], ac[:, 2:4], ac[:, 0:2], ALU.subtract)
                P.act(dec[:, 2:4], dec[:, 2:4], AF.Exp)
                P.act(dec[:, 4:6], ac[:, 2:4], AF.Exp)
                yield
                DT = []
                for h in range(2):
                    dab = P.sb_rot("dab", [128, 128], F32, 2)
                    P.copy(dab, V(dtA.ap[:, i, h:h + 1].to_broadcast([128, 128]), dtA.toks), eng="gpsimd")
                    pb = ppS.nf()
                    P.matmul(pb[:, 0:128], dab, triu)
                    dm = P.sb_rot("dm%d" % h, [128, 128], F32, 2)
                    P.stt(dm, pb[:, 0:128], ac[:, h:h + 1], causneg, ALU.subtract, ALU.add)
                    P.act(dm, dm, AF.Exp)
                    DT.append(dm)
                    yield
                pS = ppS.nf()
                P.matmul(pS[:, 0:128], BT[:, cs], CT[:, cs])
                SmT = []
                for h in range(2):
                    sm = P.sb_rot("smT%d" % h, [128, 128], BF16, 2)
                    P.tt(sm, pS[:, 0:128], DT[h], ALU.mult)
                    SmT.append(sm)
                yield
                px = ppS.nf()
                P.transpose(px[:, 0:128], xs[:, cs], idf)
                xtok = P.sb_rot("xtok", [128, 128], F32, 2)
                P.copy(xtok, px[:, 0:128], eng="scalar")
                xdt = P.sb_rot("xdt", [128, 128], BF16, 2)
                vd = P.sb_rot("vd", [128, 128], BF16, 2)
                for h in range(2):
                    hs = slice(h * 64, (h + 1) * 64)
                    P.ts(xdt[:, hs], xtok[:, hs], dt[:, i, h:h + 1], ALU.mult)
                    P.ts(vd[:, hs], xtok[:, hs], dt[:, i, h:h + 1], ALU.mult, dec[:, 2 + h:3 + h], ALU.mult)
                yield
                pbt = ppS.nf()
                P.transpose(pbt[:, 0:128], Bf[:, cs], idf)
                btok = P.sb_rot("btok", [128, 128], BF16, 2)
                P.copy(btok, pbt[:, 0:128], eng="scalar")
                yield
                pyd = ppS.nf()
                for h in range(2):
                    hs = slice(h * 64, (h + 1) * 64)
                    P.matmul(pyd[:, hs], SmT[h], xdt[:, hs])
                yt = P.sb_rot("yt", [128, 128], F32, 2)
                P.copy(yt, pyd[:, 0:128], eng="scalar")
                pyo = ppS.nf()
                P.matmul(pyo[:, 0:128], CT[:, cs], Hb)
                yield
                for h in range(2):
                    hs = slice(h * 64, (h + 1) * 64)
                    P.stt(yt[:, hs], pyo[:, hs], dec[:, h:h + 1], yt[:, hs], ALU.mult, ALU.add)
                    P.stt(yt[:, hs], xtok[:, hs], hp[:, 4 + h:5 + h], yt[:, hs], ALU.mult, ALU.add)
                yo = P.sb_rot("yo", [128, 128], F32, 2)
                P.tt(yo, yt, sz[:, i, :], ALU.mult, eng="gpsimd")
                P.dma(y[t0 + i * 128: t0 + (i + 1) * 128, 0:128], yo, is_output=True)
                ph = ppS.nf()
                P.matmul(ph[:, 0:128], btok, vd)
                for h in range(2):
                    hs = slice(h * 64, (h + 1) * 64)
                    P.stt(H[:, hs], H[:, hs], dec[:, 4 + h:5 + h], ph[:, hs], ALU.mult, ALU.add)
                P.copy(Hb, H, eng="gpsimd")
                yield

    def moba_gen():
        for c in range(4):
            kf = P.sb_rot("kf", [128, 2048], F32, 2)
            P.dma(kf, kT[:, c * 2048:(c + 1) * 2048])
            P.reduce(kmT[:, c * 8:(c + 1) * 8], kf.re("p (b t) -> p b t", t=256), ALU.add)
            yield
        P.ts(kmT, kmT, 1.0 / 256.0, ALU.mult)
        for i in range(T // 128):
            qf = P.sb_rot("qf", [128, 128], F32, 3)
            P.dma(qf, qT[:, i * 128:(i + 1) * 128])
            pg = ppM.nf()
            P.matmul(pg[:, 0:32], qf, kmT)
            gm = P.sb_rot("gm", [128, 32], F32, 2)
            P.tt(gm, pg[:, 0:32], pn[:, i, :], ALU.add)
            m8 = P.sb_rot("m8", [128, 8], F32, 2)
            P.max8(m8, gm)
            thr = P.sb_rot("thr", [128, 1], F32, 2)
            P.ts(thr, m8[:, 2:3], -1e29, ALU.max)
            P.ts(gm, gm, thr, ALU.is_ge)
            ng = P.sb_rot("ng", [128, 32], F32, 2)
            P.stt(ng, gm, -NEGBIG, on[:, i, :], ALU.mult, ALU.add)
            pt = ppM.nf()
            P.transpose(pt[0:32, 0:128], ng, idf)
            P.copy(negT[0:32, i * 128:(i + 1) * 128], pt[0:32, 0:128], eng="scalar")
            yield
        for Q in range(T // 512):
            t0 = Q * 512
            kts = []
            for j in range(4 * Q + 4):
                off = t0 - j * 128
                if off >= 1664:
                    b_ = ("far", f31)
                else:
                    b_ = ("near", tzs[:, off + 384: off + 384 + 512])
                kts.append({"lhsT": KT[:, j * 128:(j + 1) * 128],
                            "mask": (EJ[:, j // 2, :], negT[:, t0:t0 + 512]),
                            "bias": b_})
            yield from flash_block_gen(P, ppM, O, QT[:, t0:t0 + 512], 512, kts, lambda jj: V1[:, jj, :], 129)
            yq = P.sb_rot("yq", [128, 4, 128], F32, 2)
            flash_finish(P, O, lambda s_: yq[:, s_, :])
            P.dma(y[t0:t0 + 512, 128:256].rearrange("(s p) d -> p s d", p=128), yq, is_output=True)
            yield

    n_ssd = 16 * (4 + 4 * 8)
    n_moba = 4 + 64 + sum(4 * Q + 4 for Q in range(16)) + 16
    interleave(ssd_gen(), n_ssd, moba_gen(), n_moba)
    return P.finish()


def rel_bucket_np(dist):
    n = np.maximum(dist, 0)
    max_exact = 16
    nf = np.maximum(n, max_exact).astype(np.float32)
    large = max_exact + (np.log(nf / np.float32(max_exact)) / np.float32(np.log(2048 / max_exact))
                         * np.float32(32 - max_exact)).astype(np.int32)
    large = np.minimum(large, 31)
    return np.where(n < max_exact, n, large)


def host_tz(rel_bias_h, m_lo, width, stride_p, base, band=None):
    p = np.arange(128)[:, None]
    m = np.arange(width)[None, :] + m_lo
    dist = m - stride_p * p - base
    tab = rel_bias_h[rel_bucket_np(dist)].astype(np.float32)
    bad = dist < 0
    if band is not None:
        bad = bad | (dist >= band)
    return np.where(bad, np.float32(-1e30), tab).astype(np.float32)


def host_meven_inputs(p, c, conv_w, conv_b, dt_bias, a_log, d_skip, rel_bias):
    g = c // 4
    f32 = np.float32
    def padT(a):
        o = np.zeros((128, T + 3), f32)
        o[:, 3:] = a.T
        return o
    xb = p[:, 1024:2560]
    chans = [np.arange(128 * c, 128 * c + 128), 1024 + np.arange(128 * g, 128 * g + 128),
             1280 + np.arange(128 * g, 128 * g + 128)]
    xbcT = np.stack([padT(xb[:, ch]) for ch in chans])
    cwx = np.stack([np.concatenate([conv_w[:, ch], conv_b[None, ch]], 0).T for ch in chans], axis=1).astype(f32)
    hpar = np.array([dt_bias[2 * c], dt_bias[2 * c + 1], a_log[2 * c], a_log[2 * c + 1], d_skip[2 * c], d_skip[2 * c + 1]], f32)
    i = np.arange(64)[:, None]
    n = np.arange(32)[None, :]
    pastneg = np.where(n < i // 2, 0.0, -1e30).astype(f32)
    ownneg = np.where(n == i // 2, 0.0, NEGBIG).astype(f32)
    ej = np.zeros((128, 32, 128), f32)
    for J in range(32):
        ej[J, J, :] = 1.0
    k = np.arange(128)[:, None]
    l = np.arange(128)[None, :]
    cst = np.stack([(k <= l).astype(f32), np.ones((128, 128), f32), np.where(l >= k, 0.0, -1e30).astype(f32)], axis=1)
    return {
        "z": np.ascontiguousarray(p[:, 128 * c:128 * c + 128]),
        "xbcT": xbcT, "cwx": np.ascontiguousarray(cwx),
        "dtr": np.ascontiguousarray(p[:, 2560 + 2 * c: 2560 + 2 * c + 2]),
        "hpar": np.ascontiguousarray(np.broadcast_to(hpar, (128, 6))),
        "qT": np.ascontiguousarray(p[:, 2576 + 128 * c: 2576 + 128 * c + 128].T),
        "kT": np.ascontiguousarray(p[:, 3600 + 128 * c: 3600 + 128 * c + 128].T),
        "v": np.ascontiguousarray(p[:, 4624 + 128 * c: 4624 + 128 * c + 128]),
        "tz": host_tz(rel_bias[:, c], -384, TZW, 1, 0),
        "pastneg": np.ascontiguousarray(np.broadcast_to(pastneg, (128, 64, 32))),
        "ownneg": np.ascontiguousarray(np.broadcast_to(ownneg, (128, 64, 32))),
        "ej": ej, "cst": np.ascontiguousarray(cst),
    }


TCW = 3600
TWW = 1408
GELU_C = 1.5957691216057308


def build_modd_a(parts=("cmp", "att", "ret")):
    P = Prog()
    qT = P.dram("qT", [128, T], F32, "ExternalInput")
    kvcT = P.dram("kvcT", [2, 128, T], F32, "ExternalInput")
    w1 = P.dram("w1", [2, 4096, 128], F32, "ExternalInput")
    w2 = P.dram("w2", [2, 128, 128], F32, "ExternalInput")
    peT = P.dram("peT", [2, 128, 32], F32, "ExternalInput")
    tc = P.dram("tc", [128, TCW], F32, "ExternalInput")
    ovl = P.dram("ovl", [128, 4, 128], F32, "ExternalInput")
    rq4T = P.dram("rq4T", [4, 128, T], F32, "ExternalInput")
    csT = P.dram("csT", [2, 128, T], F32, "ExternalInput")
    rv = P.dram("rv", [T, 128], F32, "ExternalInput")
    rg = P.dram("rg", [T, 128], F32, "ExternalInput")
    rdec = P.dram("rdec", [128, 4], F32, "ExternalInput")
    drt = P.dram("drt", [128, 128], F32, "ExternalInput")
    ocmp = P.dram("ocmp", [T, 128], F32, "ExternalOutput")
    imp = P.dram("imp", [T, 128], F32, "ExternalOutput")
    yret = P.dram("yret", [T, 128], F32, "ExternalOutput")

    pp = PsumPool(P, 7, 1)
    idb = make_ident(P, BF16)

    KcT = P.sb([128, 512], BF16)
    VO = P.sb([128, 4, 257], BF16)
    P.memset(VO[:, :, 128:129], 1.0)
    P.dma(VO[:, :, 129:257], ovl, q="gpsimd")
    if "cmp" in parts:
        for kind in range(2):
            W1 = P.sb_rot("W1", [128, 32, 128], BF16, 1)
            P.dma(W1, w1[kind].rearrange("(l d) e -> d l e", d=128), q="gpsimd")
            W2 = P.sb_rot("W2", [128, 128], BF16, 1)
            P.dma(W2, w2[kind], q="gpsimd")
            pe = P.sb_rot("pe", [128, 32], BF16, 1)
            P.dma(pe, peT[kind], q="gpsimd")
            XT = P.sb_rot("XT", [128, T], BF16, 1)
            for c in range(4):
                P.dma(XT[:, c * 2048:(c + 1) * 2048], kvcT[kind, :, c * 2048:(c + 1) * 2048], q="gpsimd")
            pc = pp.nf()
            for l in range(32):
                P.matmul(pc[:, 0:1], W1[:, l, :], pe[:, l:l + 1], start=(l == 0), stop=(l == 31))
            cvec = P.sb_rot("cvec", [128, 1], F32, 1)
            P.copy(cvec, pc[:, 0:1])
            ph = pp.nf()
            for l in range(32):
                P.matmul(ph[:, 0:511], W1[:, l, :], XT[:, l:l + 8161:16], start=(l == 0), stop=(l == 31))
            u = P.sb_rot("cu", [128, 511], F32, 1)
            P.act(u, ph[:, 0:511], AF.Identity, bias=cvec)
            wk = P.sb_rot("cw_", [128, 511], F32, 1)
            P.tt(wk, u, u, ALU.mult)
            P.ts(wk, wk, 0.044715, ALU.mult, 1.0, ALU.add)
            P.tt(wk, wk, u, ALU.mult)
            P.act(wk, wk, AF.Sigmoid, scale=GELU_C)
            hid = P.sb_rot("hid", [128, 512], BF16, 1)
            P.memset(hid[:, 511:512], 0.0)
            P.tt(hid[:, 0:511], u, wk, ALU.mult)
            if kind == 0:
                pk = pp.nf()
                P.matmul(pk, W2, hid)
                P.copy(KcT, pk)
            else:
                for jt in range(4):
                    pv = pp.nf()
                    P.matmul(pv[:, 0:128], hid[:, jt * 128:(jt + 1) * 128], W2)
                    P.copy(VO[:, jt, 0:128], pv[:, 0:128])

    if "att" in parts:
        QT = P.sb([128, T], BF16)
        for c in range(4):
            P.dma(QT[:, c * 2048:(c + 1) * 2048], qT[:, c * 2048:(c + 1) * 2048], q="gpsimd")
        tcs = P.sb([128, TCW], F32)
        P.dma(tcs, tc)
        O = [pp.f[k] for k in range(4)]
        pp.f = pp.f[4:]
        fconst = P.sb([128, 1], F32)
        P.copy(fconst, tcs[:, TCW - 1:TCW])
        qlim = [int(x[1:]) for x in parts if x[0] == "q" and x[1:].isdigit()]
        for Q in range(qlim[0] if qlim else T // 512):
            t0 = Q * 512
            kts = []
            for jt in range(t0 // 2048 + 1):
                m0 = t0 - 2048 * jt
                if m0 >= 3584:
                    b = ("far", fconst)
                else:
                    b = ("near", tcs[:, m0:m0 + 512])
                kts.append({"lhsT": KcT[:, jt * 128:(jt + 1) * 128], "mask": None, "bias": b})
            NO_ = 257 if "n129" not in parts else 129
            flash_block(P, pp, O, QT[:, t0:t0 + 512], 512, kts, None, lambda jj: VO[:, jj, 0:NO_], NO_)
            oc = P.sb_rot("oc", [128, 4, 128], F32, 2)
            im = P.sb_rot("im", [128, 4, 128], F32, 2)

            def extra(s, Ot, den, im=im):
                if "n129" in parts:
                    P.ts(im[:, s, :], Ot[:, 0:128], den, ALU.mult)
                else:
                    P.ts(im[:, s, :], Ot[:, 129:257], den, ALU.mult)
            flash_finish(P, O, lambda s: oc[:, s, :], extra=extra)
            P.dma(ocmp[t0:t0 + 512, :].rearrange("(s p) d -> p s d", p=128), oc, is_output=True)
            P.dma(imp[t0:t0 + 512, :].rearrange("(s p) d -> p s d", p=128), im, is_output=True)
        pp.f = O + pp.f

    if "ret" in parts:
        rd = P.sb([128, 4], F32)
        P.dma(rd, rdec)
        DRT = P.sb([128, 128], F32)
        P.dma(DRT, drt)
        R = P.sb([128, 128], F32)
        Rb = P.sb([128, 128], BF16)
        P.memset(R, 0.0)
        P.memset(Rb, 0.0)
        for sc in range(T // 512):
            t0 = sc * 512
            rot = []
            cs_ = []
            for w in range(2):
                tbl = P.sb_rot("cs%d" % w, [128, 512], F32, 2)
                P.dma(tbl, csT[w, :, t0:t0 + 512])
                cs_.append(tbl)
            for w in range(2):
                a = P.sb_rot("rqa%d" % w, [128, 512], F32, 2)
                b = P.sb_rot("rqb%d" % w, [128, 512], F32, 2)
                P.dma(a, rq4T[2 * w, :, t0:t0 + 512])
                P.dma(b, rq4T[2 * w + 1, :, t0:t0 + 512])
                P.tt(a, a, cs_[0], ALU.mult)
                P.tt(b, b, cs_[1], ALU.mult, eng="gpsimd")
                o = P.sb_rot("rot%d" % w, [128, 512], BF16, 2)
                P.tt(o, a, b, ALU.add)
                rot.append(o)
            QrT, KrT = rot
            vt = P.sb_rot("rvt", [128, 4, 128], F32, 2)
            P.dma(vt, rv[t0:t0 + 512, :].rearrange("(i p) d -> p i d", p=128))
            vb = P.sb_rot("rvb", [128, 4, 128], BF16, 2)
            P.copy(vb, vt, eng="gpsimd")
            vd = P.sb_rot("rvd", [128, 4, 128], BF16, 2)
            P.ts(vd, vt, rd[:, 1:2], ALU.mult)
            gt = P.sb_rot("rgt", [128, 4, 128], F32, 2)
            P.dma(gt, rg[t0:t0 + 512, :].rearrange("(i p) d -> p i d", p=128))
            P.act(gt, gt, AF.Silu)
            yo4 = P.sb_rot("ryo", [128, 4, 128], F32, 2)
            for i in range(4):
                cs = slice(i * 128, (i + 1) * 128)
                pS = pp.nf()
                P.matmul(pS[:, 0:128], KrT[:, cs], QrT[:, cs])
                sm = P.sb_rot("rsm", [128, 128], BF16, 2)
                P.tt(sm, pS[:, 0:128], DRT, ALU.mult)
                pkt = pp.nb()
                P.transpose(pkt[:, 0:128], KrT[:, cs], idb)
                ktok = P.sb_rot("rktok", [128, 128], BF16, 2)
                P.copy(ktok, pkt[:, 0:128], eng="scalar")
                pY = pp.nf()
                P.matmul(pY[:, 0:128], sm, vb[:, i, :])
                pYo = pp.nf()
                P.matmul(pYo[:, 0:128], QrT[:, cs], Rb)
                pR = pp.nf()
                P.matmul(pR[:, 0:128], ktok, vd[:, i, :])
                yt = P.sb_rot("ryt", [128, 128], F32, 2)
                P.copy(yt, pY[:, 0:128], eng="scalar")
                P.stt(yt, pYo[:, 0:128], rd[:, 0:1], yt, ALU.mult, ALU.add)
                st = P.sb_rot("rst", [128, 6], F32, 2)
                P.bn_stats(st, yt)
                mv = P.sb_rot("rmv", [128, 2], F32, 2)
                P.bn_aggr(mv, st)
                rs = P.sb_rot("rrs", [128, 1], F32, 2)
                P.ts(rs, mv[:, 1:2], LN_EPS, ALU.add)
                P.act(rs, rs, AF.Sqrt)
                P.recip(rs, rs)
                P.ts(yt, yt, mv[:, 0:1], ALU.subtract, rs, ALU.mult)
                P.tt(yo4[:, i, :], yt, gt[:, i, :], ALU.mult, eng="gpsimd")
                P.stt(R, R, rd[:, 2:3], pR[:, 0:128], ALU.mult, ALU.add)
                P.copy(Rb, R, eng="gpsimd")
            P.dma(yret[t0:t0 + 512, :].rearrange("(i p) d -> p i d", p=128), yo4, is_output=True)
    return P.finish()


def build_modd_b():
    P = Prog()
    qT = P.dram("qT", [128, T], F32, "ExternalInput")
    kswT = P.dram("kswT", [2, 128, T], F32, "ExternalInput")
    vsw = P.dram("vsw", [2, T, 128], F32, "ExternalInput")
    gates = P.dram("gates", [T, 3], F32, "ExternalInput")
    imp4 = P.dram("imp4", [4, T, 128], F32, "ExternalInput")
    ocmp = P.dram("ocmp", [T, 128], F32, "ExternalInput")
    tz = P.dram("tz", [128, TZW], F32, "ExternalInput")
    tw = P.dram("tw", [128, TWW], F32, "ExternalInput")
    addmask = P.dram("addmask", [128, 64, 128], F32, "ExternalInput")
    eseld = P.dram("esel", [128, 64, 128], F32, "ExternalInput")
    y = P.dram("y", [T, 128], F32, "ExternalOutput")

    pp = PsumPool(P, 8, 0)
    idf = make_ident(P, F32)
    QT = P.sb([128, T], BF16)
    KsT = P.sb([128, T], BF16)
    KwT = P.sb([128, T], BF16)
    Vs1 = P.sb([128, 64, 129], BF16)
    Vw1 = P.sb([128, 64, 129], BF16)
    for c in range(4):
        sl = slice(c * 2048, (c + 1) * 2048)
        P.dma(QT[:, sl], qT[:, sl], q="gpsimd")
        P.dma(KsT[:, sl], kswT[0, :, sl], q="gpsimd")
        P.dma(KwT[:, sl], kswT[1, :, sl], q="gpsimd")
        P.dma(Vs1[:, c * 16:(c + 1) * 16, 0:128], vsw[0, sl, :].rearrange("(j p) d -> p j d", p=128), q="gpsimd")
        P.dma(Vw1[:, c * 16:(c + 1) * 16, 0:128], vsw[1, sl, :].rearrange("(j p) d -> p j d", p=128), q="gpsimd")
    P.memset(Vs1[:, :, 128:129], 1.0)
    P.memset(Vw1[:, :, 128:129], 1.0)
    tzs = P.sb([128, TZW], F32)
    P.dma(tzs, tz)
    tws = P.sb([128, TWW], F32)
    P.dma(tws, tw)
    ES = P.sb([128, 64, 128], BF16)
    for c in range(4):
        P.dma(ES[:, c * 16:(c + 1) * 16, :], eseld[:, c * 16:(c + 1) * 16, :], q="gpsimd")
    sig = P.sb([128, 64, 3], F32)
    P.dma(sig, gates.rearrange("(i p) d -> p i d", p=128))
    P.act(sig, sig, AF.Sigmoid)
    negT = P.sb([128, T], BF16)
    for i in range(T // 128):
        im4 = P.sb_rot("im4", [128, 4, 128], F32, 2)
        P.dma(im4, imp4[:, i * 128:(i + 1) * 128, :].rearrange("g t j -> t g j"))
        am = P.sb_rot("am", [128, 128], F32, 2)
        P.dma(am, addmask[:, i, :])
        sc = P.sb_rot("sc", [128, 128], F32, 2)
        P.tt(sc, im4[:, 0, :], im4[:, 1, :], ALU.add)
        P.tt(sc, sc, im4[:, 2, :], ALU.add)
        P.tt(sc, sc, im4[:, 3, :], ALU.add)
        P.tt(sc, sc, am, ALU.add)
        m8 = P.sb_rot("m8", [128, 16], F32, 2)
        wk = P.sb_rot("wk", [128, 128], F32, 2)
        P.max8(m8[:, 0:8], sc)
        P.match_replace(wk, m8[:, 0:8], sc, -1e30)
        P.max8(m8[:, 8:16], wk)
        thr = P.sb_rot("thr", [128, 1], F32, 2)
        P.ts(thr, m8[:, 15:16], -1e29, ALU.max)
        P.ts(sc, sc, thr, ALU.is_ge)
        P.ts(sc, sc, -NEGBIG, ALU.mult, NEGBIG, ALU.add)
        pt = pp.nf()
        P.transpose(pt[:, 0:128], sc, idf)
        P.copy(negT[:, i * 128:(i + 1) * 128], pt[:, 0:128], eng="scalar")
    O = [pp.f[k] for k in range(4)]
    pp.f = pp.f[4:]
    f31 = P.sb([128, 1], F32)
    P.copy(f31, tzs[:, TZW - 1:TZW])
    for Q in range(T // 512):
        t0 = Q * 512
        kts = []
        for j in range(4 * Q + 4):
            off = t0 - j * 128
            if off >= 1664:
                b = ("far", f31)
            else:
                b = ("near", tzs[:, off + 384: off + 384 + 512])
            kts.append({"lhsT": KsT[:, j * 128:(j + 1) * 128], "mask": (ES[:, j, :], negT[:, t0:t0 + 512]), "bias": b})
        flash_block(P, pp, O, QT[:, t0:t0 + 512], 512, kts, None, lambda jj: Vs1[:, jj, :], 129)
        osel = P.sb_rot("osel", [128, 4, 128], F32, 2)
        flash_finish(P, O, lambda s: osel[:, s, :])
        j0 = max(0, 4 * Q - 4)
        kts = []
        for j in range(j0, 4 * Q + 4):
            off = t0 - j * 128
            kts.append({"lhsT": KwT[:, j * 128:(j + 1) * 128], "mask": None,
                        "bias": ("near", tws[:, off + 384: off + 384 + 512])})
        flash_block(P, pp, O, QT[:, t0:t0 + 512], 512, kts, None, lambda jj, j0=j0: Vw1[:, j0 + jj, :], 129)
        owin = P.sb_rot("owin", [128, 4, 128], F32, 2)
        flash_finish(P, O, lambda s: owin[:, s, :])
        oc = P.sb_rot("occ", [128, 4, 128], F32, 2)
        P.dma(oc, ocmp[t0:t0 + 512, :].rearrange("(s p) d -> p s d", p=128))
        yo = P.sb_rot("yo", [128, 4, 128], F32, 2)
        for s in range(4):
            ti = Q * 4 + s
            P.ts(yo[:, s, :], oc[:, s, :], sig[:, ti, 0:1], ALU.mult, eng="gpsimd")
            P.stt(yo[:, s, :], osel[:, s, :], sig[:, ti, 1:2], yo[:, s, :], ALU.mult, ALU.add)
            P.stt(yo[:, s, :], owin[:, s, :], sig[:, ti, 2:3], yo[:, s, :], ALU.mult, ALU.add)
        P.dma(y[t0:t0 + 512, :].rearrange("(s p) d -> p s d", p=128), yo, is_output=True)
    return P.finish()


def host_modd_consts():
    f32 = np.float32
    n = (np.arange(4)[None, :, None] * 128 + np.arange(128)[:, None, None])
    j = np.arange(128)[None, None, :]
    ovl = ((16 * n < 64 * j + 64) & (16 * n + 32 > 64 * j) & (n < 511)).astype(f32)
    t = np.arange(T)[:, None]
    jj = np.arange(128)[None, :]
    cur = t // 64
    forced = (jj == 0) | (jj == cur) | (jj == cur - 1)
    am = np.where(forced, 100.0, np.where(jj <= cur, 0.0, -1e30)).astype(f32)
    addmask = np.ascontiguousarray(am.reshape(64, 128, 128).transpose(1, 0, 2))
    b = np.arange(128)[:, None, None]
    jt = np.arange(64)[None, :, None]
    k = np.arange(128)[None, None, :]
    esel = (b == 2 * jt + k // 64).astype(f32)
    inv = (1.0 / (np.float32(10000.0) ** (np.arange(0, 128, 2, dtype=f32) / np.float32(128)))).astype(f32)
    ang = (np.arange(T, dtype=f32)[:, None] * inv[None, :]).astype(f32)
    cos = np.cos(ang).astype(f32)
    sin = np.sin(ang).astype(f32)
    d = np.arange(128)
    sgn = np.where(d % 2 == 0, -1.0, 1.0).astype(f32)
    C = np.ascontiguousarray(cos[:, d // 2].T)
    S = np.ascontiguousarray((sin[:, d // 2] * sgn[None, :]).T)
    csT = np.stack([C, S]).astype(f32)
    return {"ovl": ovl, "addmask": addmask, "esel": esel, "csT": csT}


def host_ret_consts(hh):
    f32 = np.float32
    log_g = np.log(f32(1.0) - f32(2.0) ** (f32(-5.0) - f32(hh))).astype(f32)
    s = np.arange(128)[:, None]
    l = np.arange(128)[None, :]
    drt = np.where(l >= s, SCALE * np.exp(log_g * np.maximum(l - s, 0)), 0.0).astype(f32)
    i = np.arange(128)
    rdec = np.stack([np.exp(log_g * (i + 1)), SCALE * np.exp(log_g * (127 - i)),
                     np.full(128, np.exp(log_g * 128)), np.zeros(128)], axis=1).astype(f32)
    return drt, rdec


def host_modd_a_inputs(p, c, cmp_pe, cmp_w1, cmp_w2, rel_bias, consts):
    kvh = c // 4
    f32 = np.float32
    def colsT(c0):
        return np.ascontiguousarray(p[:, c0:c0 + 128].T)
    perm = np.arange(128) ^ 1
    rqT = colsT(2584 + 128 * c)
    rkT = colsT(3608 + 128 * c)
    drt, rdec = host_ret_consts(c)
    return {
        "qT": colsT(128 * c),
        "kvcT": np.stack([colsT(1024 + 128 * kvh), colsT(1280 + 128 * kvh)]),
        "w1": np.ascontiguousarray(cmp_w1), "w2": np.ascontiguousarray(cmp_w2),
        "peT": np.ascontiguousarray(cmp_pe.transpose(0, 2, 1)),
        "tc": host_tz(rel_bias[:, c], 0, TCW, 16, 31),
        "ovl": consts["ovl"],
        "rq4T": np.stack([rqT, rqT[perm], rkT, rkT[perm]]),
        "csT": consts["csT"],
        "rv": np.ascontiguousarray(p[:, 4632 + 128 * c: 4632 + 128 * c + 128]),
        "rg": np.ascontiguousarray(p[:, 5656 + 128 * c: 5656 + 128 * c + 128]),
        "rdec": rdec, "drt": drt,
    }


def host_modd_b_inputs(p, c, imps, ocmp_c, rel_bias, consts):
    kvh = c // 4
    def colsT(c0):
        return np.ascontiguousarray(p[:, c0:c0 + 128].T)
    return {
        "qT": colsT(128 * c),
        "kswT": np.stack([colsT(1536 + 128 * kvh), colsT(2048 + 128 * kvh)]),
        "vsw": np.stack([p[:, 1792 + 128 * kvh: 1792 + 128 * kvh + 128], p[:, 2304 + 128 * kvh: 2304 + 128 * kvh + 128]]),
        "gates": np.ascontiguousarray(p[:, 2560 + 3 * c: 2560 + 3 * c + 3]),
        "imp4": np.stack([imps[4 * kvh + g] for g in range(4)]),
        "ocmp": ocmp_c,
        "tz": host_tz(rel_bias[:, c], -384, TZW, 1, 0),
        "tw": host_tz(rel_bias[:, c], -384, TWW, 1, 0, band=512),
        "addmask": consts["addmask"], "esel": consts["esel"],
    }


def _run(nc, maps):
    res = run_bass_kernel_spmd(nc, maps, core_ids=list(range(NCORE)))
    return res.results


def _bc(v, n=128):
    return np.ascontiguousarray(np.broadcast_to(np.asarray(v, np.float32), (n, v.shape[-1])))


def kernel(x, rel_bias, ev_w_in, ev_conv_w, ev_conv_b, ev_dt_bias, ev_a_log, ev_d_skip, ev_norm_w, ev_w_out,
           od_w_in, od_cmp_pe, od_cmp_w1, od_cmp_w2, od_w_out,
           ffn_w_up, ffn_conv_w, ffn_conv_b, ffn_w_down, ln_g, ln_b):
    f32 = np.float32
    A = lambda a: np.ascontiguousarray(np.asarray(a, f32))
    rel_bias = A(rel_bias)
    h = A(x)[0]
    consts = host_modd_consts()
    rows = [slice(c * TPC, (c + 1) * TPC) for c in range(NCORE)]
    depth = ln_g.shape[0]
    for layer in range(depth):
        i = layer // 2
        even = layer % 2 == 0
        if layer == 0:
            w_in = A(ev_w_in[i])
            res = _run(build_inproj(w_in.shape[1]), [{"h": h[rows[c]], "w": w_in} for c in range(NCORE)])
            p = np.concatenate([r["p"] for r in res], 0)
            del res
        if even:
            res = _run(build_meven(), [host_meven_inputs(p, c, A(ev_conv_w[i]), A(ev_conv_b[i]), A(ev_dt_bias[i]),
                                                         A(ev_a_log[i]), A(ev_d_skip[i]), rel_bias) for c in range(NCORE)])
            ymix = np.concatenate([r["y"][:, 0:128] for r in res] + [r["y"][:, 128:256] for r in res], 1)
            w_out = A(ev_w_out[i])
        else:
            ra = _run(build_modd_a(), [host_modd_a_inputs(p, c, A(od_cmp_pe[i]), A(od_cmp_w1[i]), A(od_cmp_w2[i]),
                                                          rel_bias, consts) for c in range(NCORE)])
            imps = [r["imp"] for r in ra]
            rb = _run(build_modd_b(), [host_modd_b_inputs(p, c, imps, ra[c]["ocmp"], rel_bias, consts)
                                       for c in range(NCORE)])
            ymix = np.concatenate([r["y"] for r in rb] + [r["yret"] for r in ra], 1)
            w_out = A(od_w_out[i])
        ymix = np.ascontiguousarray(ymix)
        lng, lnb = _bc(A(ln_g[layer, 0])), _bc(A(ln_b[layer, 0]))
        maps = []
        for c in range(NCORE):
            m = {"ymix": ymix[rows[c]], "hprev": h[rows[c]], "w": w_out, "lng": lng, "lnb": lnb}
            if even:
                m["nw"] = _bc(A(ev_norm_w[i]))
            maps.append(m)
        res = _run(build_p1(even), maps)
        hm = np.concatenate([r["hmid"] for r in res], 0)
        lng, lnb = _bc(A(ln_g[layer, 1])), _bc(A(ln_b[layer, 1]))
        cw = host_cw(A(ffn_conv_w[layer]), A(ffn_conv_b[layer]))
        wup, wdn = A(ffn_w_up[layer]), A(ffn_w_down[layer])
        maps = []
        for c in range(NCORE):
            halo = hm[c * TPC - 2:c * TPC] if c > 0 else np.zeros((2, D), f32)
            maps.append({"hmid": hm[rows[c]], "haloT": np.ascontiguousarray(halo.T), "wup": wup, "cw": cw,
                         "wdn": wdn, "lng": lng, "lnb": lnb})
        if layer + 1 < depth:
            wn = A(od_w_in[(layer + 1) // 2] if even else ev_w_in[(layer + 1) // 2])
            for m in maps:
                m["wnext"] = wn
            res = _run(build_p2(wn.shape[1]), maps)
            p = np.concatenate([r["pnext"] for r in res], 0)
        else:
            res = _run(build_p2(), maps)
        h = np.concatenate([r["hout"] for r in res], 0)
    return h[None].astype(f32)
```

```python
from contextlib import ExitStack
import numpy as np
import concourse.bass as bass
import concourse.mybir as mybir
from concourse.bass_utils import run_bass_kernel_spmd

F32 = mybir.dt.float32
BF16 = mybir.dt.bfloat16
ALU = mybir.AluOpType
AF = mybir.ActivationFunctionType
AX = mybir.AxisListType


class Tok:
    __slots__ = ("w", "rd")

    def __init__(self):
        self.w = None
        self.rd = {}


class V:
    __slots__ = ("ap", "toks")

    def __init__(self, ap, toks):
        self.ap = ap
        self.toks = toks

    def __getitem__(self, idx):
        return V(self.ap[idx], self.toks)

    def re(self, pat, **kw):
        return V(self.ap.rearrange(pat, **kw), self.toks)


def _ap(x):
    return x.ap if isinstance(x, V) else x


def _toks(xs):
    out = []
    for x in xs:
        if isinstance(x, V):
            out.extend(x.toks)
        elif isinstance(x, Tok):
            out.append(x)
    return out


NDEV = 8


class Prog:
    ENGS = ("tensor", "vector", "scalar", "gpsimd", "sync")
    NDMA = 8

    def __init__(self):
        self.nc = bass.Bass("TRN2", target_bir_lowering=False, num_devices=NDEV)
        self.es = ExitStack()
        self.ops = {e: [] for e in self.ENGS}
        self.cnt = {e: 0 for e in self.ENGS}
        self.waited = {e: {} for e in self.ENGS}
        self.sems = {}
        for e in self.ENGS:
            self.sems[e] = self.es.enter_context(self.nc.semaphore("s_" + e))
        self.dsem = {}
        self.dcnt = {}
        self.dnext = {}
        for q in ("sync", "gpsimd", "scalar"):
            for i in range(self.NDMA):
                k = "d_%s_%d" % (q, i)
                self.sems[k] = self.es.enter_context(self.nc.semaphore(k))
                self.dcnt[k] = 0
            self.dnext[q] = 0
        self.out_events = []
        self.pending = {}
        self.strict = ("vector", "scalar", "gpsimd")
        self.nalloc = 0

    def dram(self, name, shape, dt, kind):
        return self.nc.dram_tensor(name, list(shape), dt, kind=kind).ap()

    def sb(self, shape, dt, name=None):
        self.nalloc += 1
        es = self.scopes[-1][0] if getattr(self, "scopes", None) else self.es
        t = es.enter_context(self.nc.sbuf_tensor(name or ("sb%d_%d" % (self.nalloc, len(getattr(self, "scopes", [])))), list(shape), dt))
        return V(t[:], [Tok()])

    def scope_begin(self):
        if not hasattr(self, "scopes"):
            self.scopes = []
        self.scopes.append((ExitStack(), set()))

    def scope_end(self):
        self.barrier()
        es, keys = self.scopes.pop()
        for k in keys:
            self._rot.pop(k, None)
        es.close()

    def barrier(self):
        allev = {x: self.cnt[x] for x in self.ENGS}
        allev.update(self.dcnt)
        if "cc" in self.sems:
            allev["cc"] = self.cccnt
        for e in self.ENGS:
            for k, v in allev.items():
                if k == e or v == 0:
                    continue
                if self.waited[e].get(k, 0) < v:
                    self.waited[e][k] = v
                    self.pending.setdefault(e, []).append((k, v))

    def sb_rot(self, key, shape, dt, n):
        if not hasattr(self, "_rot"):
            self._rot = {}
        if key not in self._rot:
            self.nrot = getattr(self, "nrot", 0) + 1
            self._rot[key] = [[self.sb(shape, dt, name="%s_%d_%d" % (key, j, self.nrot)) for j in range(n)], 0]
            if getattr(self, "scopes", None):
                self.scopes[-1][1].add(key)
        ent = self._rot[key]
        v = ent[0][ent[1] % n]
        ent[1] += 1
        return v

    def ps(self, shape, dt=F32, name=None):
        self.nalloc += 1
        t = self.es.enter_context(self.nc.psum_tensor(name or ("ps%d" % self.nalloc), list(shape), dt))
        return V(t[:], [Tok()])

    def _deps(self, eng, reads, writes):
        need = {}

        def add(ev):
            if ev is None:
                return
            k, v = ev
            if need.get(k, 0) < v:
                need[k] = v
        for t in _toks(reads):
            add(t.w)
        for t in _toks(writes):
            add(t.w)
            for k, v in t.rd.items():
                add((k, v))
        waits = []
        wd = self.waited[eng]
        for k, v in need.items():
            if k == eng and not (self.strict and eng in self.strict):
                continue
            if wd.get(k, 0) >= v:
                continue
            wd[k] = v
            waits.append((k, v))
        return waits

    def _commit(self, ev, reads, writes):
        k, v = ev
        for t in _toks(reads):
            if t.rd.get(k, 0) < v:
                t.rd[k] = v
        for t in _toks(writes):
            t.w = ev
            t.rd = {}

    def op(self, eng, fn, reads, writes):
        waits = self.pending.pop(eng, []) + self._deps(eng, reads, writes)
        self.cnt[eng] += 1
        ev = (eng, self.cnt[eng])
        self.ops[eng].append((waits, fn, (eng, 1)))
        self._commit(ev, reads, writes)
        return ev

    def dma(self, out, in_, q="sync", is_output=False, **kw):
        reads = [in_]
        writes = [out]
        i = self.dnext[q]
        self.dnext[q] = (i + 1) % self.NDMA
        k = "d_%s_%d" % (q, i)
        waits = self.pending.pop(q, []) + self._deps(q, reads, writes)
        prev = self.dcnt[k]
        if prev > 0 and self.waited[q].get(k, 0) < prev:
            self.waited[q][k] = prev
            waits.append((k, prev))
        self.dcnt[k] += 16
        ev = (k, self.dcnt[k])
        o, i_ = _ap(out), _ap(in_)
        self.ops[q].append((waits, lambda e: e.dma_start(out=o, in_=i_, **kw), (k, 16)))
        self._commit(ev, reads, writes)
        if is_output:
            self.out_events.append(ev)
        return ev

    def dma_fn(self, fn, reads, writes, q="sync", is_output=False):
        i = self.dnext[q]
        self.dnext[q] = (i + 1) % self.NDMA
        k = "d_%s_%d" % (q, i)
        waits = self._deps(q, reads, writes)
        prev = self.dcnt[k]
        if prev > 0 and self.waited[q].get(k, 0) < prev:
            self.waited[q][k] = prev
            waits.append((k, prev))
        self.dcnt[k] += 16
        ev = (k, self.dcnt[k])
        self.ops[q].append((waits, fn, (k, 16)))
        self._commit(ev, reads, writes)
        if is_output:
            self.out_events.append(ev)
        return ev

    def coll(self, kind, in_, out, groups, q="gpsimd", op=ALU.bypass):
        if "cc" not in self.sems:
            self.sems["cc"] = self.es.enter_context(self.nc.semaphore("cc_sem"))
            self.cccnt = 0
        waits = self._deps(q, [in_], [out])
        if self.cccnt > 0 and self.waited[q].get("cc", 0) < self.cccnt:
            self.waited[q]["cc"] = self.cccnt
            waits.append(("cc", self.cccnt))
        self.cccnt += 1
        ev = ("cc", self.cccnt)
        i_, o_ = _ap(in_).opt(), _ap(out).opt()
        self.ops[q].append((waits, lambda e: e.collective_compute(kind, op, groups, [i_], [o_]), ("cc", None)))
        self._commit(ev, [in_], [out])
        return ev

    def finish(self):
        nc = self.nc
        fin = {}
        for k, v in self.out_events:
            fin[k] = max(fin.get(k, 0), v)
        sems = self.sems
        ops = self.ops
        with nc.Block() as block:
            def runner(name):
                def body(e):
                    for waits, fn, (sk, inc) in ops[name]:
                        for k, v in waits:
                            e.wait_ge(sems[k], v)
                        ins = fn(e)
                        if inc is None:
                            ins.then_inc(sems[sk])
                        else:
                            ins.then_inc(sems[sk], inc)
                    if name == "sync":
                        for k, v in fin.items():
                            e.wait_ge(sems[k], v)
                return body
            block.tensor(runner("tensor"))
            block.vector(runner("vector"))
            block.scalar(runner("scalar"))
            block.gpsimd(runner("gpsimd"))
            block.sync(runner("sync"))
        self.es.close()
        return nc

    def matmul(self, out, lhsT, rhs, start=True, stop=True):
        o, l, r = _ap(out), _ap(lhsT), _ap(rhs)
        return self.op("tensor", lambda e: e.matmul(o, l, r, start=start, stop=stop), [lhsT, rhs], [out])

    def transpose(self, out, in_, ident):
        o, i, d = _ap(out), _ap(in_), _ap(ident)
        return self.op("tensor", lambda e: e.transpose(o, i, d), [in_, ident], [out])

    def act(self, out, in_, func, bias=None, scale=1.0, accum_out=None, eng="scalar"):
        o, i = _ap(out), _ap(in_)
        b = _ap(bias) if bias is not None else None
        s = _ap(scale)
        a = _ap(accum_out) if accum_out is not None else None
        kw = {}
        if b is not None:
            kw["bias"] = b
        if a is not None:
            kw["accum_out"] = a
        return self.op("scalar", lambda e: e.activation(o, i, func, scale=s, **kw),
                       [in_, bias, scale], [out, accum_out])

    def tt(self, out, in0, in1, op, eng="vector"):
        o, a, b = _ap(out), _ap(in0), _ap(in1)
        return self.op(eng, lambda e: e.tensor_tensor(o, a, b, op), [in0, in1], [out])

    def ts(self, out, in0, s1, op0, s2=None, op1=None, accum_out=None, eng="vector"):
        o, a = _ap(out), _ap(in0)
        x1, x2 = _ap(s1), _ap(s2)
        acc = _ap(accum_out) if accum_out is not None else None
        kw = {}
        if op1 is not None:
            kw["op1"] = op1
        if acc is not None:
            kw["accum_out"] = acc
        return self.op(eng, lambda e: e.tensor_scalar(o, a, x1, x2, op0, **kw),
                       [in0, s1, s2], [out, accum_out])

    def stt(self, out, in0, scalar, in1, op0, op1, eng="vector"):
        o, a, s, b = _ap(out), _ap(in0), _ap(scalar), _ap(in1)
        return self.op(eng, lambda e: e.scalar_tensor_tensor(o, a, s, b, op0, op1), [in0, scalar, in1], [out])

    def copy(self, out, in_, eng="vector"):
        o, i = _ap(out), _ap(in_)
        if eng == "scalar":
            return self.op(eng, lambda e: e.copy(o, i), [in_], [out])
        return self.op(eng, lambda e: e.tensor_copy(o, i), [in_], [out])

    def memset(self, out, val, eng="vector"):
        o = _ap(out)
        return self.op(eng, lambda e: e.memset(o, val), [], [out])

    def reduce(self, out, in_, op, axis=AX.X, eng="vector"):
        o, i = _ap(out), _ap(in_)
        return self.op(eng, lambda e: e.tensor_reduce(o, i, axis, op), [in_], [out])

    def recip(self, out, in_):
        o, i = _ap(out), _ap(in_)
        return self.op("vector", lambda e: e.reciprocal(o, i), [in_], [out])

    def max8(self, out, in_):
        o, i = _ap(out), _ap(in_)
        return self.op("vector", lambda e: e.max(o, i), [in_], [out])

    def match_replace(self, out, to_replace, values, imm):
        o, r, v = _ap(out), _ap(to_replace), _ap(values)
        return self.op("vector", lambda e: e.match_replace(o, r, v, imm), [to_replace, values], [out])

    def bn_stats(self, out, in_):
        o, i = _ap(out), _ap(in_)
        return self.op("vector", lambda e: e.bn_stats(o, i), [in_], [out])

    def bn_aggr(self, out, in_):
        o, i = _ap(out), _ap(in_)
        return self.op("vector", lambda e: e.bn_aggr(o, i), [in_], [out])


T = 8192
D = 2048
NCORE = 8
TPC = T // NCORE
KC = D // 128
EVEN_IN = 5648
ODD_IN = 6680
FFN = 5632
FCH = FFN // 128
ALPHA = 8.0 ** 0.25
LN_EPS = 1e-5
SCALE = 128.0 ** -0.5
NEGBIG = -30000.0


def make_ident(P, dt):
    idf = P.sb([128, 128], F32)
    P.memset(idf, 0.0, eng="gpsimd")
    o = idf.ap
    P.op("gpsimd", lambda e: e.affine_select(out=o, in_=o, pattern=[[-1, 128]], compare_op=ALU.not_equal,
                                              fill=1.0, base=0, channel_multiplier=1), [idf], [idf])
    if dt == F32:
        return idf
    idb = P.sb([128, 128], dt)
    P.copy(idb, idf)
    return idb


class PsumPool:
    def __init__(self, P, nf32, nbf16=0):
        self.f = [P.ps([128, 512], F32) for _ in range(nf32)]
        self.b = [P.ps([128, 1024], BF16) for _ in range(nbf16)]
        self.fi = 0
        self.bi = 0

    def nf(self):
        x = self.f[self.fi % len(self.f)]
        self.fi += 1
        return x

    def nb(self):
        x = self.b[self.bi % len(self.b)]
        self.bi += 1
        return x


def load_hT(P, pp, ident_bf, h_dram, ntiles, hT, col0=0):
    for i in range(ntiles):
        hf = P.sb_rot("ldh_f", [128, D], F32, 2)
        P.dma(hf, h_dram[i * 128:(i + 1) * 128, :], q="sync")
        hb = P.sb_rot("ldh_b", [128, D], BF16, 2)
        P.copy(hb[:, 0:1024], hf[:, 0:1024], eng="vector")
        P.copy(hb[:, 1024:2048], hf[:, 1024:2048], eng="gpsimd")
        for g in range(KC // 8):
            pt = pp.nb()
            for j in range(8):
                kc = g * 8 + j
                P.transpose(pt[:, j * 128:(j + 1) * 128], hb[:, kc * 128:(kc + 1) * 128], ident_bf)
            dst = hT[:, g * 8:(g + 1) * 8, col0 + i * 128: col0 + (i + 1) * 128]
            src = pt.re("p (j t) -> p j t", t=128)
            if g % 2 == 0:
                P.copy(dst, src, eng="vector")
            else:
                P.copy(dst, src, eng="scalar")


def layer_norm_tile(P, pre, out, g_bc, b_bc):
    st = P.sb_rot("ln_st", [128, 4, 6], F32, 2)
    for c in range(4):
        P.bn_stats(st[:, c, :], pre[:, c * 512:(c + 1) * 512])
    mv = P.sb_rot("ln_mv", [128, 2], F32, 2)
    P.bn_aggr(mv, st.re("p a b -> p (a b)"))
    rs = P.sb_rot("ln_rs", [128, 1], F32, 2)
    P.ts(rs, mv[:, 1:2], LN_EPS, ALU.add)
    P.act(rs, rs, AF.Sqrt)
    P.recip(rs, rs)
    P.ts(out, pre, mv[:, 0:1], ALU.subtract, rs, ALU.mult)
    P.tt(out, out, g_bc, ALU.mult, eng="gpsimd")
    P.tt(out, out, b_bc, ALU.add)


def build_inproj(IN):
    P = Prog()
    h = P.dram("h", [TPC, D], F32, "ExternalInput")
    w = P.dram("w", [D, IN], F32, "ExternalInput")
    p = P.dram("p", [TPC, IN], F32, "ExternalOutput")
    pp = PsumPool(P, 4, 2)
    idb = make_ident(P, BF16)
    hT = P.sb([128, KC, TPC], BF16)
    load_hT(P, pp, idb, h, TPC // 128, hT)
    wv = w.rearrange("(kc p) n -> p kc n", p=128)
    nblk = (IN + 511) // 512
    for cb in range(nblk):
        c0 = cb * 512
        nc_ = min(512, IN - c0)
        wb = P.sb_rot("wblk", [128, KC, 512], BF16, 2)
        P.dma(wb[:, :, 0:nc_], wv[:, :, c0:c0 + nc_], q="gpsimd")
        for i in range(TPC // 128):
            ps = pp.nf()
            for kc in range(KC):
                P.matmul(ps[:, 0:nc_], hT[:, kc, i * 128:(i + 1) * 128], wb[:, kc, 0:nc_],
                         start=(kc == 0), stop=(kc == KC - 1))
            ob = P.sb_rot("ob", [128, 512], F32, 3)
            if i % 2 == 0:
                P.copy(ob[:, 0:nc_], ps[:, 0:nc_], eng="vector")
            else:
                P.copy(ob[:, 0:nc_], ps[:, 0:nc_], eng="scalar")
            P.dma(p[i * 128:(i + 1) * 128, c0:c0 + nc_], ob[:, 0:nc_], q="sync", is_output=True)
    return P.finish()


def build_p1(even):
    P = Prog()
    ymix = P.dram("ymix", [TPC, D], F32, "ExternalInput")
    hprev = P.dram("hprev", [TPC, D], F32, "ExternalInput")
    w = P.dram("w", [D, D], F32, "ExternalInput")
    lng = P.dram("lng", [128, D], F32, "ExternalInput")
    lnb = P.dram("lnb", [128, D], F32, "ExternalInput")
    if even:
        nw = P.dram("nw", [128, 1024], F32, "ExternalInput")
    hmid = P.dram("hmid", [TPC, D], F32, "ExternalOutput")
    pp = PsumPool(P, 4, 2)
    idb = make_ident(P, BF16)
    wb = P.sb([128, KC, D], BF16)
    wv = w.rearrange("(kc p) n -> p kc n", p=128)
    for c in range(4):
        P.dma(wb[:, :, c * 512:(c + 1) * 512], wv[:, :, c * 512:(c + 1) * 512], q="gpsimd")
    g_bc = P.sb([128, D], F32)
    b_bc = P.sb([128, D], F32)
    P.dma(g_bc, lng)
    P.dma(b_bc, lnb)
    if even:
        nw_bc = P.sb([128, 1024], F32)
        P.dma(nw_bc, nw)
    for i in range(TPC // 128):
        yf = P.sb_rot("yf", [128, D], F32, 2)
        P.dma(yf, ymix[i * 128:(i + 1) * 128, :])
        hp = P.sb_rot("hp", [128, D], F32, 2)
        P.dma(hp, hprev[i * 128:(i + 1) * 128, :])
        yb = P.sb_rot("yb", [128, D], BF16, 2)
        if even:
            ss = P.sb_rot("ss", [128, 2], F32, 2)
            junk = P.sb_rot("junk", [128, 512], F32, 1)
            for g in range(2):
                P.act(junk, yf[:, g * 512:(g + 1) * 512], AF.Square, accum_out=ss[:, g:g + 1])
            P.ts(ss, ss, 1.0 / 512.0, ALU.mult, LN_EPS, ALU.add)
            P.act(ss, ss, AF.Sqrt)
            P.recip(ss, ss)
            for g in range(2):
                P.stt(yb[:, g * 512:(g + 1) * 512], yf[:, g * 512:(g + 1) * 512], ss[:, g:g + 1],
                      nw_bc[:, g * 512:(g + 1) * 512], ALU.mult, ALU.mult)
            P.copy(yb[:, 1024:2048], yf[:, 1024:2048], eng="gpsimd")
        else:
            P.copy(yb[:, 0:1024], yf[:, 0:1024], eng="vector")
            P.copy(yb[:, 1024:2048], yf[:, 1024:2048], eng="gpsimd")
        yT = P.sb_rot("yT", [128, KC, 128], BF16, 2)
        for g in range(2):
            pt = pp.nb()
            for j in range(8):
                kc = g * 8 + j
                P.transpose(pt[:, j * 128:(j + 1) * 128], yb[:, kc * 128:(kc + 1) * 128], idb)
            if g == 0:
                P.copy(yT[:, 0:8, :], pt.re("p (j t) -> p j t", t=128), eng="vector")
            else:
                P.copy(yT[:, 8:16, :], pt.re("p (j t) -> p j t", t=128), eng="scalar")
        pre = P.sb_rot("pre", [128, D], F32, 2)
        for cb in range(4):
            ps = pp.nf()
            for kc in range(KC):
                P.matmul(ps, yT[:, kc, :], wb[:, kc, cb * 512:(cb + 1) * 512], start=(kc == 0), stop=(kc == KC - 1))
            P.stt(pre[:, cb * 512:(cb + 1) * 512], hp[:, cb * 512:(cb + 1) * 512], ALPHA, ps, ALU.mult, ALU.add)
        ot = P.sb_rot("ot", [128, D], F32, 2)
        layer_norm_tile(P, pre, ot, g_bc, b_bc)
        P.dma(hmid[i * 128:(i + 1) * 128, :], ot, is_output=True)
    return P.finish()


def build_p2(next_IN=None):
    P = Prog()
    hmid = P.dram("hmid", [TPC, D], F32, "ExternalInput")
    haloT = P.dram("haloT", [D, 2], F32, "ExternalInput")
    wup = P.dram("wup", [D, 2 * FFN], F32, "ExternalInput")
    cw = P.dram("cw", [128, FCH, 4], F32, "ExternalInput")
    wdn = P.dram("wdn", [FFN, D], F32, "ExternalInput")
    lng = P.dram("lng", [128, D], F32, "ExternalInput")
    lnb = P.dram("lnb", [128, D], F32, "ExternalInput")
    hout = P.dram("hout", [TPC, D], F32, "ExternalOutput")
    if next_IN is not None:
        wnext = P.dram("wnext", [D, next_IN], F32, "ExternalInput")
        pnext = P.dram("pnext", [TPC, next_IN], F32, "ExternalOutput")
    pp = PsumPool(P, 6, 2)
    idb = make_ident(P, BF16)
    idf = make_ident(P, F32)
    g_bc = P.sb([128, D], F32)
    b_bc = P.sb([128, D], F32)
    P.dma(g_bc, lng)
    P.dma(b_bc, lnb)
    cws = P.sb([128, FCH, 4], F32)
    P.dma(cws, cw)
    wupv = wup.rearrange("(kc p) n -> p kc n", p=128)
    wdnv = wdn.rearrange("(f p) n -> p f n", p=128)
    gT = P.sb([128, FCH, TPC], BF16)
    NT = TPC // 128
    P.scope_begin()
    hT = P.sb([128, KC, 2 + TPC], BF16)
    P.dma(hT[:, :, 0:2], haloT.rearrange("(kc p) t -> p kc t", p=128), q="gpsimd")
    load_hT(P, pp, idb, hmid, NT, hT, col0=2)
    for f2 in range(FCH // 2):
        wa = P.sb_rot("wa", [128, KC, 256], BF16, 2)
        wu = P.sb_rot("wu", [128, KC, 256], BF16, 2)
        P.dma(wa, wupv[:, :, f2 * 256:(f2 + 1) * 256], q="gpsimd")
        P.dma(wu, wupv[:, :, FFN + f2 * 256: FFN + (f2 + 1) * 256], q="gpsimd")
        for fi in range(2):
            f = f2 * 2 + fi
            fs = slice(fi * 128, (fi + 1) * 128)
            for b in range(TPC // 256):
                c0 = b * 256
                pa = pp.nf()
                pu = pp.nf()
                for kc in range(KC):
                    P.matmul(pa[:, 0:258], wa[:, kc, fs], hT[:, kc, c0:c0 + 258], start=(kc == 0), stop=(kc == KC - 1))
                for kc in range(KC):
                    P.matmul(pu[:, 0:256], wu[:, kc, fs], hT[:, kc, c0 + 2:c0 + 258], start=(kc == 0), stop=(kc == KC - 1))
                ac = P.sb_rot("ac", [128, 256], F32, 2)
                P.act(ac, pa[:, 2:258], AF.Identity, bias=cws[:, f, 3:4], scale=cws[:, f, 2:3])
                P.stt(ac, pa[:, 1:257], cws[:, f, 1:2], ac, ALU.mult, ALU.add)
                P.stt(ac, pa[:, 0:256], cws[:, f, 0:1], ac, ALU.mult, ALU.add)
                sg = P.sb_rot("sg", [128, 256], F32, 2)
                P.act(sg, ac, AF.Silu)
                P.tt(gT[:, f, b * 256:(b + 1) * 256], sg, pu[:, 0:256], ALU.mult)
    P.scope_end()
    P.scope_begin()
    pre = [P.sb([128, D], F32) for _ in range(NT)]
    for i in range(NT):
        P.dma(pre[i], hmid[i * 128:(i + 1) * 128, :])
    for cb in range(D // 128):
        wd = P.sb_rot("wd", [128, FCH, 128], BF16, 2)
        P.dma(wd, wdnv[:, :, cb * 128:(cb + 1) * 128], q="gpsimd")
        for hf in range(TPC // 512):
            ps = pp.nf()
            for f in range(FCH):
                P.matmul(ps, wd[:, f, :], gT[:, f, hf * 512:(hf + 1) * 512], start=(f == 0), stop=(f == FCH - 1))
            dst = P.sb_rot("dst", [128, 512], F32, 2)
            P.copy(dst, ps, eng="scalar")
            pt = pp.nf()
            for i in range(4):
                P.transpose(pt[:, i * 128:(i + 1) * 128], dst[:, i * 128:(i + 1) * 128], idf)
            for i in range(4):
                pr = pre[hf * 4 + i]
                P.stt(pr[:, cb * 128:(cb + 1) * 128], pr[:, cb * 128:(cb + 1) * 128], ALPHA,
                      pt[:, i * 128:(i + 1) * 128], ALU.mult, ALU.add)
    for i in range(NT):
        ot = P.sb_rot("ot", [128, D], F32, 1)
        layer_norm_tile(P, pre[i], ot, g_bc, b_bc)
        P.dma(hout[i * 128:(i + 1) * 128, :], ot, is_output=True)
    P.scope_end()
    if next_IN is not None:
        P.scope_begin()
        hT2 = P.sb([128, KC, TPC], BF16)
        load_hT(P, pp, idb, hout, NT, hT2)
        wv = wnext.rearrange("(kc p) n -> p kc n", p=128)
        nblk = (next_IN + 511) // 512
        for cb in range(nblk):
            c0 = cb * 512
            nc_ = min(512, next_IN - c0)
            wb = P.sb_rot("wblk", [128, KC, 512], BF16, 2)
            P.dma(wb[:, :, 0:nc_], wv[:, :, c0:c0 + nc_], q="gpsimd")
            for i in range(NT):
                ps = pp.nf()
                for kc in range(KC):
                    P.matmul(ps[:, 0:nc_], hT2[:, kc, i * 128:(i + 1) * 128], wb[:, kc, 0:nc_],
                             start=(kc == 0), stop=(kc == KC - 1))
                ob = P.sb_rot("ob", [128, 512], F32, 3)
                if i % 2 == 0:
                    P.copy(ob[:, 0:nc_], ps[:, 0:nc_], eng="vector")
                else:
                    P.copy(ob[:, 0:nc_], ps[:, 0:nc_], eng="scalar")
                P.dma(pnext[i * 128:(i + 1) * 128, c0:c0 + nc_], ob[:, 0:nc_], q="sync", is_output=True)
        P.scope_end()
    return P.finish()


def host_cw(conv_w, conv_b):
    a = np.concatenate([conv_w, conv_b[None, :]], axis=0)
    return np.ascontiguousarray(a.reshape(4, FCH, 128).transpose(2, 1, 0)).astype(np.float32)


TZW = 2560


FLASH_AHEAD = 3


def flash_block(P, pp, O, QT_blk, nq, k_tiles, scale_bias_fn, pv_fn, n_out):
    nk = len(k_tiles)
    nsub = nq // 128
    pss = {}

    def emit_s(jj):
        kt = k_tiles[jj]
        ps = pp.nf()
        has_mask = kt.get("mask") is not None
        P.matmul(ps[:, 0:nq], kt["lhsT"], QT_blk, start=True, stop=not has_mask)
        if has_mask:
            ml, mr = kt["mask"]
            P.matmul(ps[:, 0:nq], ml, mr, start=False, stop=True)
        pss[jj] = ps

    def emit_rest(jj):
        kt = k_tiles[jj]
        ps = pss.pop(jj)
        pt = P.sb_rot("fl_pt", [128, 512], BF16, 3)
        kind, bap = kt["bias"]
        if kind == "far":
            P.act(pt[:, 0:nq], ps[:, 0:nq], AF.Exp, bias=bap, scale=SCALE)
        else:
            tmp = P.sb_rot("fl_tmp", [128, 512], F32, 2)
            P.stt(tmp[:, 0:nq], ps[:, 0:nq], SCALE, bap, ALU.mult, ALU.add)
            P.act(pt[:, 0:nq], tmp[:, 0:nq], AF.Exp)
        for s in range(nsub):
            P.matmul(O[s][:, 0:n_out], pt[:, s * 128:(s + 1) * 128], pv_fn(jj), start=(jj == 0), stop=(jj == nk - 1))

    ahead = max(1, min(FLASH_AHEAD, len(pp.f) - 1))
    for jj in range(min(ahead, nk)):
        emit_s(jj)
    for jj in range(nk):
        if jj + ahead < nk:
            emit_s(jj + ahead)
        emit_rest(jj)


def flash_finish(P, Os, dst_fn, ncols=128, extra=None):
    for s, O in enumerate(Os):
        den = P.sb_rot("fl_den", [128, 1], F32, 4)
        P.ts(den, O[:, 128:129], 1e-30, ALU.max)
        P.recip(den, den)
        P.ts(dst_fn(s), O[:, 0:ncols], den, ALU.mult)
        if extra is not None:
            extra(s, O, den)


def flash_block_gen(P, pp, O, QT_blk, nq, k_tiles, pv_fn, n_out):
    nk = len(k_tiles)
    nsub = nq // 128
    pss = {}

    def emit_s(jj):
        kt = k_tiles[jj]
        ps = pp.nf()
        has_mask = kt.get("mask") is not None
        P.matmul(ps[:, 0:nq], kt["lhsT"], QT_blk, start=True, stop=not has_mask)
        if has_mask:
            ml, mr = kt["mask"]
            P.matmul(ps[:, 0:nq], ml, mr, start=False, stop=True)
        pss[jj] = ps

    def emit_rest(jj):
        kt = k_tiles[jj]
        ps = pss.pop(jj)
        pt = P.sb_rot("fl_pt", [128, 512], BF16, 3)
        kind, bap = kt["bias"]
        if kind == "far":
            P.act(pt[:, 0:nq], ps[:, 0:nq], AF.Exp, bias=bap, scale=SCALE)
        else:
            tmp = P.sb_rot("fl_tmp", [128, 512], F32, 2)
            P.stt(tmp[:, 0:nq], ps[:, 0:nq], SCALE, bap, ALU.mult, ALU.add)
            P.act(pt[:, 0:nq], tmp[:, 0:nq], AF.Exp)
        for s_ in range(nsub):
            P.matmul(O[s_][:, 0:n_out], pt[:, s_ * 128:(s_ + 1) * 128], pv_fn(jj), start=(jj == 0), stop=(jj == nk - 1))

    ahead = max(1, min(FLASH_AHEAD, len(pp.f) - 1))
    for jj in range(min(ahead, nk)):
        emit_s(jj)
    for jj in range(nk):
        if jj + ahead < nk:
            emit_s(jj + ahead)
        emit_rest(jj)
        yield


def interleave(ga, na, gb, nb):
    ia = ib = 0
    da = db = False
    while not (da and db):
        if not da and (db or ia * nb <= ib * na):
            try:
                next(ga)
                ia += 1
            except StopIteration:
                da = True
        else:
            try:
                next(gb)
                ib += 1
            except StopIteration:
                db = True


def build_meven():
    P = Prog()
    z = P.dram("z", [T, 128], F32, "ExternalInput")
    xbcT = P.dram("xbcT", [3, 128, T + 3], F32, "ExternalInput")
    cwx = P.dram("cwx", [128, 3, 5], F32, "ExternalInput")
    dtr = P.dram("dtr", [T, 2], F32, "ExternalInput")
    hpar = P.dram("hpar", [128, 6], F32, "ExternalInput")
    qT = P.dram("qT", [128, T], F32, "ExternalInput")
    kT = P.dram("kT", [128, T], F32, "ExternalInput")
    v = P.dram("v", [T, 128], F32, "ExternalInput")
    tz = P.dram("tz", [128, TZW], F32, "ExternalInput")
    pastneg = P.dram("pastneg", [128, 64, 32], F32, "ExternalInput")
    ownneg = P.dram("ownneg", [128, 64, 32], F32, "ExternalInput")
    ejd = P.dram("ej", [128, 32, 128], F32, "ExternalInput")
    cst = P.dram("cst", [128, 3, 128], F32, "ExternalInput")
    y = P.dram("y", [T, 256], F32, "ExternalOutput")

    banks = [P.ps([128, 512], F32) for _ in range(8)]
    ppS = PsumPool(P, 0, 0)
    ppS.f = banks[0:2]
    ppM = PsumPool(P, 0, 0)
    ppM.f = banks[6:8]
    O = banks[2:6]
    idf = make_ident(P, F32)
    csts = P.sb([128, 3, 128], F32)
    P.dma(csts, cst)
    triu, ones, causneg = csts[:, 0, :], csts[:, 1, :], csts[:, 2, :]
    cw = P.sb([128, 3, 5], F32)
    P.dma(cw, cwx)
    hp = P.sb([128, 6], F32)
    P.dma(hp, hpar)
    a_bc = P.sb([128, 2], F32)
    P.act(a_bc, hp[:, 2:4], AF.Exp)
    P.ts(a_bc, a_bc, -1.0, ALU.mult)

    QT = P.sb([128, T], BF16)
    KT = P.sb([128, T], BF16)
    V1 = P.sb([128, T // 128, 129], BF16)
    for c in range(4):
        P.dma(QT[:, c * 2048:(c + 1) * 2048], qT[:, c * 2048:(c + 1) * 2048], q="gpsimd")
        P.dma(KT[:, c * 2048:(c + 1) * 2048], kT[:, c * 2048:(c + 1) * 2048], q="gpsimd")
        P.dma(V1[:, c * 16:(c + 1) * 16, 0:128], v[c * 2048:(c + 1) * 2048, :].rearrange("(j p) d -> p j d", p=128), q="gpsimd")
    P.memset(V1[:, :, 128:129], 1.0)
    tzs = P.sb([128, TZW], F32)
    P.dma(tzs, tz)
    pn = P.sb([128, 64, 32], F32)
    on = P.sb([128, 64, 32], F32)
    P.dma(pn, pastneg)
    P.dma(on, ownneg)
    EJ = P.sb([128, 32, 128], BF16)
    P.dma(EJ, ejd, q="gpsimd")
    f31 = P.sb([128, 1], F32)
    P.copy(f31, tzs[:, TZW - 1:TZW])
    kmT = P.sb([128, 32], F32)
    negT = P.sb([128, T], BF16)
    P.memset(negT[:, 0:T // 2], 0.0, eng="gpsimd")
    P.memset(negT[:, T // 2:T], 0.0, eng="gpsimd")

    H = P.sb([128, 128], F32)
    Hb = P.sb([128, 128], BF16)
    P.memset(H, 0.0)
    P.memset(Hb, 0.0)

    def ssd_gen():
        for sc in range(T // 512):
            t0 = sc * 512
            fm = []
            for wch in range(3):
                raw = P.sb_rot("raw%d" % wch, [128, 515], F32, 2)
                P.dma(raw, xbcT[wch, :, t0:t0 + 515])
                acc = P.sb_rot("cacc%d" % wch, [128, 512], F32, 2)
                P.act(acc, raw[:, 3:515], AF.Identity, bias=cw[:, wch, 4:5], scale=cw[:, wch, 3:4])
                for k in (2, 1, 0):
                    P.stt(acc, raw[:, k:k + 512], cw[:, wch, k:k + 1], acc, ALU.mult, ALU.add)
                o = P.sb_rot("cv%d" % wch, [128, 512], F32, 2)
                P.act(o, acc, AF.Silu)
                fm.append(o)
                yield
            xs, Bf, Cf = fm
            BT = P.sb_rot("BTb", [128, 512], BF16, 2)
            CT = P.sb_rot("CTb", [128, 512], BF16, 2)
            P.copy(BT, Bf, eng="gpsimd")
            P.copy(CT, Cf, eng="gpsimd")
            zt = P.sb_rot("zt", [128, 4, 128], F32, 2)
            P.dma(zt, z[t0:t0 + 512, :].rearrange("(i p) d -> p i d", p=128))
            dtt = P.sb_rot("dtt", [128, 4, 2], F32, 2)
            P.dma(dtt, dtr[t0:t0 + 512, :].rearrange("(i p) d -> p i d", p=128))
            dt = P.sb_rot("dt", [128, 4, 2], F32, 2)
            for i in range(4):
                P.tt(dt[:, i, :], dtt[:, i, :], hp[:, 0:2], ALU.add)
            P.act(dt, dt, AF.Exp)
            P.act(dt, dt, AF.Ln, bias=1.0)
            dtA = P.sb_rot("dtA", [128, 4, 2], F32, 2)
            for i in range(4):
                P.tt(dtA[:, i, :], dt[:, i, :], a_bc, ALU.mult)
            sz = P.sb_rot("sz", [128, 4, 128], F32, 2)
            P.act(sz, zt, AF.Silu)
            yield
            for i in range(4):
                cs = slice(i * 128, (i + 1) * 128)
                pa = ppS.nf()
                P.matmul(pa[:, 0:2], triu, dtA[:, i, :])
                P.matmul(pa[:, 2:4], ones, dtA[:, i, :])
                ac = P.sb_rot("ac", [128, 4], F32, 2)
                P.copy(ac, pa[:, 0:4])
                dec = P.sb_rot("dec", [128, 6], F32, 2)
                P.act(dec[:, 0:2], ac[:, 0:2], AF.Exp)
                P.tt(dec[:, 2:4], ac[:, 2:4], ac[:, 0:2], ALU.subtract)
                P.act(dec[:, 2:4], dec[:, 2:4], AF.Exp)
                P.act(dec[:, 4:6], ac[:, 2:4], AF.Exp)
                yield
                DT = []
                for h in range(2):
                    dab = P.sb_rot("dab", [128, 128], F32, 2)
                    P.copy(dab, V(dtA.ap[:, i, h:h + 1].to_broadcast([128, 128]), dtA.toks), eng="gpsimd")
                    pb = ppS.nf()
                    P.matmul(pb[:, 0:128], dab, triu)
                    dm = P.sb_rot("dm%d" % h, [128, 128], F32, 2)
                    P.stt(dm, pb[:, 0:128], ac[:, h:h + 1], causneg, ALU.subtract, ALU.add)
                    P.act(dm, dm, AF.Exp)
                    DT.append(dm)
                    yield
                pS = ppS.nf()
                P.matmul(pS[:, 0:128], BT[:, cs], CT[:, cs])
                SmT = []
                for h in range(2):
                    sm = P.sb_rot("smT%d" % h, [128, 128], BF16, 2)
                    P.tt(sm, pS[:, 0:128], DT[h], ALU.mult)
                    SmT.append(sm)
                yield
                px = ppS.nf()
                P.transpose(px[:, 0:128], xs[:, cs], idf)
                xtok = P.sb_rot("xtok", [128, 128], F32, 2)
                P.copy(xtok, px[:, 0:128], eng="scalar")
                xdt = P.sb_rot("xdt", [128, 128], BF16, 2)
                vd = P.sb_rot("vd", [128, 128], BF16, 2)
                for h in range(2):
                    hs = slice(h * 64, (h + 1) * 64)
                    P.ts(xdt[:, hs], xtok[:, hs], dt[:, i, h:h + 1], ALU.mult)
                    P.ts(vd[:, hs], xtok[:, hs], dt[:, i, h:h + 1], ALU.mult, dec[:, 2 + h:3 + h], ALU.mult)
                yield
                pbt = ppS.nf()
                P.transpose(pbt[:, 0:128], Bf[:, cs], idf)
                btok = P.sb_rot("btok", [128, 128], BF16, 2)
                P.copy(btok, pbt[:, 0:128], eng="scalar")
                yield
                pyd = ppS.nf()
                for h in range(2):
                    hs = slice(h * 64, (h + 1) * 64)
                    P.matmul(pyd[:, hs], SmT[h], xdt[:, hs])
                yt = P.sb_rot("yt", [128, 128], F32, 2)
                P.copy(yt, pyd[:, 0:128], eng="scalar")
                pyo = ppS.nf()
                P.matmul(pyo[:, 0:128], CT[:, cs], Hb)
                yield
                for h in range(2):
                    hs = slice(h * 64, (h + 1) * 64)
                    P.stt(yt[:, hs], pyo[:, hs], dec[:, h:h + 1], yt[:, hs], ALU.mult, ALU.add)
                    P.stt(yt[:, hs], xtok[:, hs], hp[:, 4 + h:5 + h], yt[:, hs], ALU.mult, ALU.add)
                yo = P.sb_rot("yo", [128, 128], F32, 2)
                P.tt(yo, yt, sz[:, i, :], ALU.mult, eng="gpsimd")
                P.dma(y[t0 + i * 128: t0 + (i + 1) * 128, 0:128], yo, is_output=True)
                ph = ppS.nf()
                P.matmul(ph[:, 0:128], btok, vd)
                for h in range(2):
                    hs = slice(h * 64, (h + 1) * 64)
                    P.stt(H[:, hs], H[:, hs], dec[:, 4 + h:5 + h], ph[:, hs], ALU.mult, ALU.add)
                P.copy(Hb, H, eng="gpsimd")
                yield

    def moba_gen():
        for c in range(4):
            kf = P.sb_rot("kf", [128, 2048], F32, 2)
            P.dma(kf, kT[:, c * 2048:(c + 1) * 2048])
            P.reduce(kmT[:, c * 8:(c + 1) * 8], kf.re("p (b t) -> p b t", t=256), ALU.add)
            yield
        P.ts(kmT, kmT, 1.0 / 256.0, ALU.mult)
        for i in range(T // 128):
            qf = P.sb_rot("qf", [128, 128], F32, 3)
            P.dma(qf, qT[:, i * 128:(i + 1) * 128])
            pg = ppM.nf()
            P.matmul(pg[:, 0:32], qf, kmT)
            gm = P.sb_rot("gm", [128, 32], F32, 2)
            P.tt(gm, pg[:, 0:32], pn[:, i, :], ALU.add)
            m8 = P.sb_rot("m8", [128, 8], F32, 2)
            P.max8(m8, gm)
            thr = P.sb_rot("thr", [128, 1], F32, 2)
            P.ts(thr, m8[:, 2:3], -1e29, ALU.max)
            P.ts(gm, gm, thr, ALU.is_ge)
            ng = P.sb_rot("ng", [128, 32], F32, 2)
            P.stt(ng, gm, -NEGBIG, on[:, i, :], ALU.mult, ALU.add)
            pt = ppM.nf()
            P.transpose(pt[0:32, 0:128], ng, idf)
            P.copy(negT[0:32, i * 128:(i + 1) * 128], pt[0:32, 0:128], eng="scalar")
            yield
        for Q in range(T // 512):
            t0 = Q * 512
            kts = []
            for j in range(4 * Q + 4):
                off = t0 - j * 128
                if off >= 1664:
                    b_ = ("far", f31)
                else:
                    b_ = ("near", tzs[:, off + 384: off + 384 + 512])
                kts.append({"lhsT": KT[:, j * 128:(j + 1) * 128],
                            "mask": (EJ[:, j // 2, :], negT[:, t0:t0 + 512]),
                            "bias": b_})
            yield from flash_block_gen(P, ppM, O, QT[:, t0:t0 + 512], 512, kts, lambda jj: V1[:, jj, :], 129)
            yq = P.sb_rot("yq", [128, 4, 128], F32, 2)
            flash_finish(P, O, lambda s_: yq[:, s_, :])
            P.dma(y[t0:t0 + 512, 128:256].rearrange("(s p) d -> p s d", p=128), yq, is_output=True)
            yield

    n_ssd = 16 * (4 + 4 * 8)
    n_moba = 4 + 64 + sum(4 * Q + 4 for Q in range(16)) + 16
    ppS.f = list(banks)
    for _ in ssd_gen():
        pass
    ppM.f = banks[6:8] + banks[0:2]
    for _ in moba_gen():
        pass
    return P.finish()


def rel_bucket_np(dist):
    n = np.maximum(dist, 0)
    max_exact = 16
    nf = np.maximum(n, max_exact).astype(np.float32)
    large = max_exact + (np.log(nf / np.float32(max_exact)) / np.float32(np.log(2048 / max_exact))
                         * np.float32(32 - max_exact)).astype(np.int32)
    large = np.minimum(large, 31)
    return np.where(n < max_exact, n, large)


def host_tz(rel_bias_h, m_lo, width, stride_p, base, band=None):
    p = np.arange(128)[:, None]
    m = np.arange(width)[None, :] + m_lo
    dist = m - stride_p * p - base
    tab = rel_bias_h[rel_bucket_np(dist)].astype(np.float32)
    bad = dist < 0
    if band is not None:
        bad = bad | (dist >= band)
    return np.where(bad, np.float32(-1e30), tab).astype(np.float32)


def host_meven_inputs(p, c, conv_w, conv_b, dt_bias, a_log, d_skip, rel_bias):
    g = c // 4
    f32 = np.float32
    def padT(a):
        o = np.zeros((128, T + 3), f32)
        o[:, 3:] = a.T
        return o
    xb = p[:, 1024:2560]
    chans = [np.arange(128 * c, 128 * c + 128), 1024 + np.arange(128 * g, 128 * g + 128),
             1280 + np.arange(128 * g, 128 * g + 128)]
    xbcT = np.stack([padT(xb[:, ch]) for ch in chans])
    cwx = np.stack([np.concatenate([conv_w[:, ch], conv_b[None, ch]], 0).T for ch in chans], axis=1).astype(f32)
    hpar = np.array([dt_bias[2 * c], dt_bias[2 * c + 1], a_log[2 * c], a_log[2 * c + 1], d_skip[2 * c], d_skip[2 * c + 1]], f32)
    i = np.arange(64)[:, None]
    n = np.arange(32)[None, :]
    pastneg = np.where(n < i // 2, 0.0, -1e30).astype(f32)
    ownneg = np.where(n == i // 2, 0.0, NEGBIG).astype(f32)
    ej = np.zeros((128, 32, 128), f32)
    for J in range(32):
        ej[J, J, :] = 1.0
    k = np.arange(128)[:, None]
    l = np.arange(128)[None, :]
    cst = np.stack([(k <= l).astype(f32), np.ones((128, 128), f32), np.where(l >= k, 0.0, -1e30).astype(f32)], axis=1)
    return {
        "z": np.ascontiguousarray(p[:, 128 * c:128 * c + 128]),
        "xbcT": xbcT, "cwx": np.ascontiguousarray(cwx),
        "dtr": np.ascontiguousarray(p[:, 2560 + 2 * c: 2560 + 2 * c + 2]),
        "hpar": np.ascontiguousarray(np.broadcast_to(hpar, (128, 6))),
        "qT": np.ascontiguousarray(p[:, 2576 + 128 * c: 2576 + 128 * c + 128].T),
        "kT": np.ascontiguousarray(p[:, 3600 + 128 * c: 3600 + 128 * c + 128].T),
        "v": np.ascontiguousarray(p[:, 4624 + 128 * c: 4624 + 128 * c + 128]),
        "tz": host_tz(rel_bias[:, c], -384, TZW, 1, 0),
        "pastneg": np.ascontiguousarray(np.broadcast_to(pastneg, (128, 64, 32))),
        "ownneg": np.ascontiguousarray(np.broadcast_to(ownneg, (128, 64, 32))),
        "ej": ej, "cst": np.ascontiguousarray(cst),
    }


TCW = 3600
TWW = 1408
GELU_C = 1.5957691216057308


def build_modd_a(parts=("cmp", "att", "ret")):
    P = Prog()
    qT = P.dram("qT", [128, T], F32, "ExternalInput")
    kvcT = P.dram("kvcT", [2, 128, T], F32, "ExternalInput")
    w1 = P.dram("w1", [2, 4096, 128], F32, "ExternalInput")
    w2 = P.dram("w2", [2, 128, 128], F32, "ExternalInput")
    peT = P.dram("peT", [2, 128, 32], F32, "ExternalInput")
    tc = P.dram("tc", [128, TCW], F32, "ExternalInput")
    ovl = P.dram("ovl", [128, 4, 128], F32, "ExternalInput")
    rq4T = P.dram("rq4T", [4, 128, T], F32, "ExternalInput")
    csT = P.dram("csT", [2, 128, T], F32, "ExternalInput")
    rv = P.dram("rv", [T, 128], F32, "ExternalInput")
    rg = P.dram("rg", [T, 128], F32, "ExternalInput")
    rdec = P.dram("rdec", [128, 4], F32, "ExternalInput")
    drt = P.dram("drt", [128, 128], F32, "ExternalInput")
    ocmp = P.dram("ocmp", [T, 128], F32, "ExternalOutput")
    imp = P.dram("imp", [T, 128], F32, "ExternalOutput")
    yret = P.dram("yret", [T, 128], F32, "ExternalOutput")

    pp = PsumPool(P, 7, 1)
    idb = make_ident(P, BF16)

    KcT = P.sb([128, 512], BF16)
    VO = P.sb([128, 4, 257], BF16)
    P.memset(VO[:, :, 128:129], 1.0)
    P.dma(VO[:, :, 129:257], ovl, q="gpsimd")
    if "cmp" in parts:
        for kind in range(2):
            W1 = P.sb_rot("W1", [128, 32, 128], BF16, 1)
            P.dma(W1, w1[kind].rearrange("(l d) e -> d l e", d=128), q="gpsimd")
            W2 = P.sb_rot("W2", [128, 128], BF16, 1)
            P.dma(W2, w2[kind], q="gpsimd")
            pe = P.sb_rot("pe", [128, 32], BF16, 1)
            P.dma(pe, peT[kind], q="gpsimd")
            XT = P.sb_rot("XT", [128, T], BF16, 1)
            for c in range(4):
                P.dma(XT[:, c * 2048:(c + 1) * 2048], kvcT[kind, :, c * 2048:(c + 1) * 2048], q="gpsimd")
            pc = pp.nf()
            for l in range(32):
                P.matmul(pc[:, 0:1], W1[:, l, :], pe[:, l:l + 1], start=(l == 0), stop=(l == 31))
            cvec = P.sb_rot("cvec", [128, 1], F32, 1)
            P.copy(cvec, pc[:, 0:1])
            ph = pp.nf()
            for l in range(32):
                P.matmul(ph[:, 0:511], W1[:, l, :], XT[:, l:l + 8161:16], start=(l == 0), stop=(l == 31))
            u = P.sb_rot("cu", [128, 511], F32, 1)
            P.act(u, ph[:, 0:511], AF.Identity, bias=cvec)
            wk = P.sb_rot("cw_", [128, 511], F32, 1)
            P.tt(wk, u, u, ALU.mult)
            P.ts(wk, wk, 0.044715, ALU.mult, 1.0, ALU.add)
            P.tt(wk, wk, u, ALU.mult)
            P.act(wk, wk, AF.Sigmoid, scale=GELU_C)
            hid = P.sb_rot("hid", [128, 512], BF16, 1)
            P.memset(hid[:, 511:512], 0.0)
            P.tt(hid[:, 0:511], u, wk, ALU.mult)
            if kind == 0:
                pk = pp.nf()
                P.matmul(pk, W2, hid)
                P.copy(KcT, pk)
            else:
                for jt in range(4):
                    pv = pp.nf()
                    P.matmul(pv[:, 0:128], hid[:, jt * 128:(jt + 1) * 128], W2)
                    P.copy(VO[:, jt, 0:128], pv[:, 0:128])

    if "att" in parts:
        QT = P.sb([128, T], BF16)
        for c in range(4):
            P.dma(QT[:, c * 2048:(c + 1) * 2048], qT[:, c * 2048:(c + 1) * 2048], q="gpsimd")
        tcs = P.sb([128, TCW], F32)
        P.dma(tcs, tc)
        O = [pp.f[k] for k in range(4)]
        pp.f = pp.f[4:]
        fconst = P.sb([128, 1], F32)
        P.copy(fconst, tcs[:, TCW - 1:TCW])
        qlim = [int(x[1:]) for x in parts if x[0] == "q" and x[1:].isdigit()]
        for Q in range(qlim[0] if qlim else T // 512):
            t0 = Q * 512
            kts = []
            for jt in range(t0 // 2048 + 1):
                m0 = t0 - 2048 * jt
                if m0 >= 3584:
                    b = ("far", fconst)
                else:
                    b = ("near", tcs[:, m0:m0 + 512])
                kts.append({"lhsT": KcT[:, jt * 128:(jt + 1) * 128], "mask": None, "bias": b})
            NO_ = 257 if "n129" not in parts else 129
            flash_block(P, pp, O, QT[:, t0:t0 + 512], 512, kts, None, lambda jj: VO[:, jj, 0:NO_], NO_)
            oc = P.sb_rot("oc", [128, 4, 128], F32, 2)
            im = P.sb_rot("im", [128, 4, 128], F32, 2)

            def extra(s, Ot, den, im=im):
                if "n129" in parts:
                    P.ts(im[:, s, :], Ot[:, 0:128], den, ALU.mult)
                else:
                    P.ts(im[:, s, :], Ot[:, 129:257], den, ALU.mult)
            flash_finish(P, O, lambda s: oc[:, s, :], extra=extra)
            P.dma(ocmp[t0:t0 + 512, :].rearrange("(s p) d -> p s d", p=128), oc, is_output=True)
            P.dma(imp[t0:t0 + 512, :].rearrange("(s p) d -> p s d", p=128), im, is_output=True)
        pp.f = O + pp.f

    if "ret" in parts:
        rd = P.sb([128, 4], F32)
        P.dma(rd, rdec)
        DRT = P.sb([128, 128], F32)
        P.dma(DRT, drt)
        R = P.sb([128, 128], F32)
        Rb = P.sb([128, 128], BF16)
        P.memset(R, 0.0)
        P.memset(Rb, 0.0)
        for sc in range(T // 512):
            t0 = sc * 512
            rot = []
            cs_ = []
            for w in range(2):
                tbl = P.sb_rot("cs%d" % w, [128, 512], F32, 2)
                P.dma(tbl, csT[w, :, t0:t0 + 512])
                cs_.append(tbl)
            for w in range(2):
                a = P.sb_rot("rqa%d" % w, [128, 512], F32, 2)
                b = P.sb_rot("rqb%d" % w, [128, 512], F32, 2)
                P.dma(a, rq4T[2 * w, :, t0:t0 + 512])
                P.dma(b, rq4T[2 * w + 1, :, t0:t0 + 512])
                P.tt(a, a, cs_[0], ALU.mult)
                P.tt(b, b, cs_[1], ALU.mult, eng="gpsimd")
                o = P.sb_rot("rot%d" % w, [128, 512], BF16, 2)
                P.tt(o, a, b, ALU.add)
                rot.append(o)
            QrT, KrT = rot
            vt = P.sb_rot("rvt", [128, 4, 128], F32, 2)
            P.dma(vt, rv[t0:t0 + 512, :].rearrange("(i p) d -> p i d", p=128))
            vb = P.sb_rot("rvb", [128, 4, 128], BF16, 2)
            P.copy(vb, vt, eng="gpsimd")
            vd = P.sb_rot("rvd", [128, 4, 128], BF16, 2)
            P.ts(vd, vt, rd[:, 1:2], ALU.mult)
            gt = P.sb_rot("rgt", [128, 4, 128], F32, 2)
            P.dma(gt, rg[t0:t0 + 512, :].rearrange("(i p) d -> p i d", p=128))
            P.act(gt, gt, AF.Silu)
            yo4 = P.sb_rot("ryo", [128, 4, 128], F32, 2)
            for i in range(4):
                cs = slice(i * 128, (i + 1) * 128)
                pS = pp.nf()
                P.matmul(pS[:, 0:128], KrT[:, cs], QrT[:, cs])
                sm = P.sb_rot("rsm", [128, 128], BF16, 2)
                P.tt(sm, pS[:, 0:128], DRT, ALU.mult)
                pkt = pp.nb()
                P.transpose(pkt[:, 0:128], KrT[:, cs], idb)
                ktok = P.sb_rot("rktok", [128, 128], BF16, 2)
                P.copy(ktok, pkt[:, 0:128], eng="scalar")
                pY = pp.nf()
                P.matmul(pY[:, 0:128], sm, vb[:, i, :])
                pYo = pp.nf()
                P.matmul(pYo[:, 0:128], QrT[:, cs], Rb)
                pR = pp.nf()
                P.matmul(pR[:, 0:128], ktok, vd[:, i, :])
                yt = P.sb_rot("ryt", [128, 128], F32, 2)
                P.copy(yt, pY[:, 0:128], eng="scalar")
                P.stt(yt, pYo[:, 0:128], rd[:, 0:1], yt, ALU.mult, ALU.add)
                st = P.sb_rot("rst", [128, 6], F32, 2)
                P.bn_stats(st, yt)
                mv = P.sb_rot("rmv", [128, 2], F32, 2)
                P.bn_aggr(mv, st)
                rs = P.sb_rot("rrs", [128, 1], F32, 2)
                P.ts(rs, mv[:, 1:2], LN_EPS, ALU.add)
                P.act(rs, rs, AF.Sqrt)
                P.recip(rs, rs)
                P.ts(yt, yt, mv[:, 0:1], ALU.subtract, rs, ALU.mult)
                P.tt(yo4[:, i, :], yt, gt[:, i, :], ALU.mult, eng="gpsimd")
                P.stt(R, R, rd[:, 2:3], pR[:, 0:128], ALU.mult, ALU.add)
                P.copy(Rb, R, eng="gpsimd")
            P.dma(yret[t0:t0 + 512, :].rearrange("(i p) d -> p i d", p=128), yo4, is_output=True)
    return P.finish()


def build_modd_b():
    P = Prog()
    qT = P.dram("qT", [128, T], F32, "ExternalInput")
    kswT = P.dram("kswT", [2, 128, T], F32, "ExternalInput")
    vsw = P.dram("vsw", [2, T, 128], F32, "ExternalInput")
    gates = P.dram("gates", [T, 3], F32, "ExternalInput")
    imp4 = P.dram("imp4", [4, T, 128], F32, "ExternalInput")
    ocmp = P.dram("ocmp", [T, 128], F32, "ExternalInput")
    tz = P.dram("tz", [128, TZW], F32, "ExternalInput")
    tw = P.dram("tw", [128, TWW], F32, "ExternalInput")
    addmask = P.dram("addmask", [128, 64, 128], F32, "ExternalInput")
    eseld = P.dram("esel", [128, 64, 128], F32, "ExternalInput")
    y = P.dram("y", [T, 128], F32, "ExternalOutput")

    pp = PsumPool(P, 8, 0)
    idf = make_ident(P, F32)
    QT = P.sb([128, T], BF16)
    KsT = P.sb([128, T], BF16)
    KwT = P.sb([128, T], BF16)
    Vs1 = P.sb([128, 64, 129], BF16)
    Vw1 = P.sb([128, 64, 129], BF16)
    for c in range(4):
        sl = slice(c * 2048, (c + 1) * 2048)
        P.dma(QT[:, sl], qT[:, sl], q="gpsimd")
        P.dma(KsT[:, sl], kswT[0, :, sl], q="gpsimd")
        P.dma(KwT[:, sl], kswT[1, :, sl], q="gpsimd")
        P.dma(Vs1[:, c * 16:(c + 1) * 16, 0:128], vsw[0, sl, :].rearrange("(j p) d -> p j d", p=128), q="gpsimd")
        P.dma(Vw1[:, c * 16:(c + 1) * 16, 0:128], vsw[1, sl, :].rearrange("(j p) d -> p j d", p=128), q="gpsimd")
    P.memset(Vs1[:, :, 128:129], 1.0)
    P.memset(Vw1[:, :, 128:129], 1.0)
    tzs = P.sb([128, TZW], F32)
    P.dma(tzs, tz)
    tws = P.sb([128, TWW], F32)
    P.dma(tws, tw)
    ES = P.sb([128, 64, 128], BF16)
    for c in range(4):
        P.dma(ES[:, c * 16:(c + 1) * 16, :], eseld[:, c * 16:(c + 1) * 16, :], q="gpsimd")
    sig = P.sb([128, 64, 3], F32)
    P.dma(sig, gates.rearrange("(i p) d -> p i d", p=128))
    P.act(sig, sig, AF.Sigmoid)
    negT = P.sb([128, T], BF16)
    for i in range(T // 128):
        im4 = P.sb_rot("im4", [128, 4, 128], F32, 2)
        P.dma(im4, imp4[:, i * 128:(i + 1) * 128, :].rearrange("g t j -> t g j"))
        am = P.sb_rot("am", [128, 128], F32, 2)
        P.dma(am, addmask[:, i, :])
        sc = P.sb_rot("sc", [128, 128], F32, 2)
        P.tt(sc, im4[:, 0, :], im4[:, 1, :], ALU.add)
        P.tt(sc, sc, im4[:, 2, :], ALU.add)
        P.tt(sc, sc, im4[:, 3, :], ALU.add)
        P.tt(sc, sc, am, ALU.add)
        m8 = P.sb_rot("m8", [128, 16], F32, 2)
        wk = P.sb_rot("wk", [128, 128], F32, 2)
        P.max8(m8[:, 0:8], sc)
        P.match_replace(wk, m8[:, 0:8], sc, -1e30)
        P.max8(m8[:, 8:16], wk)
        thr = P.sb_rot("thr", [128, 1], F32, 2)
        P.ts(thr, m8[:, 15:16], -1e29, ALU.max)
        P.ts(sc, sc, thr, ALU.is_ge)
        P.ts(sc, sc, -NEGBIG, ALU.mult, NEGBIG, ALU.add)
        pt = pp.nf()
        P.transpose(pt[:, 0:128], sc, idf)
        P.copy(negT[:, i * 128:(i + 1) * 128], pt[:, 0:128], eng="scalar")
    O = [pp.f[k] for k in range(4)]
    pp.f = pp.f[4:]
    f31 = P.sb([128, 1], F32)
    P.copy(f31, tzs[:, TZW - 1:TZW])
    for Q in range(T // 512):
        t0 = Q * 512
        kts = []
        for j in range(4 * Q + 4):
            off = t0 - j * 128
            if off >= 1664:
                b = ("far", f31)
            else:
                b = ("near", tzs[:, off + 384: off + 384 + 512])
            kts.append({"lhsT": KsT[:, j * 128:(j + 1) * 128], "mask": (ES[:, j, :], negT[:, t0:t0 + 512]), "bias": b})
        flash_block(P, pp, O, QT[:, t0:t0 + 512], 512, kts, None, lambda jj: Vs1[:, jj, :], 129)
        osel = P.sb_rot("osel", [128, 4, 128], F32, 2)
        flash_finish(P, O, lambda s: osel[:, s, :])
        j0 = max(0, 4 * Q - 4)
        kts = []
        for j in range(j0, 4 * Q + 4):
            off = t0 - j * 128
            kts.append({"lhsT": KwT[:, j * 128:(j + 1) * 128], "mask": None,
                        "bias": ("near", tws[:, off + 384: off + 384 + 512])})
        flash_block(P, pp, O, QT[:, t0:t0 + 512], 512, kts, None, lambda jj, j0=j0: Vw1[:, j0 + jj, :], 129)
        owin = P.sb_rot("owin", [128, 4, 128], F32, 2)
        flash_finish(P, O, lambda s: owin[:, s, :])
        oc = P.sb_rot("occ", [128, 4, 128], F32, 2)
        P.dma(oc, ocmp[t0:t0 + 512, :].rearrange("(s p) d -> p s d", p=128))
        yo = P.sb_rot("yo", [128, 4, 128], F32, 2)
        for s in range(4):
            ti = Q * 4 + s
            P.ts(yo[:, s, :], oc[:, s, :], sig[:, ti, 0:1], ALU.mult, eng="gpsimd")
            P.stt(yo[:, s, :], osel[:, s, :], sig[:, ti, 1:2], yo[:, s, :], ALU.mult, ALU.add)
            P.stt(yo[:, s, :], owin[:, s, :], sig[:, ti, 2:3], yo[:, s, :], ALU.mult, ALU.add)
        P.dma(y[t0:t0 + 512, :].rearrange("(s p) d -> p s d", p=128), yo, is_output=True)
    return P.finish()


def host_modd_consts():
    f32 = np.float32
    n = (np.arange(4)[None, :, None] * 128 + np.arange(128)[:, None, None])
    j = np.arange(128)[None, None, :]
    ovl = ((16 * n < 64 * j + 64) & (16 * n + 32 > 64 * j) & (n < 511)).astype(f32)
    t = np.arange(T)[:, None]
    jj = np.arange(128)[None, :]
    cur = t // 64
    forced = (jj == 0) | (jj == cur) | (jj == cur - 1)
    am = np.where(forced, 100.0, np.where(jj <= cur, 0.0, -1e30)).astype(f32)
    addmask = np.ascontiguousarray(am.reshape(64, 128, 128).transpose(1, 0, 2))
    b = np.arange(128)[:, None, None]
    jt = np.arange(64)[None, :, None]
    k = np.arange(128)[None, None, :]
    esel = (b == 2 * jt + k // 64).astype(f32)
    inv = (1.0 / (np.float32(10000.0) ** (np.arange(0, 128, 2, dtype=f32) / np.float32(128)))).astype(f32)
    ang = (np.arange(T, dtype=f32)[:, None] * inv[None, :]).astype(f32)
    cos = np.cos(ang).astype(f32)
    sin = np.sin(ang).astype(f32)
    d = np.arange(128)
    sgn = np.where(d % 2 == 0, -1.0, 1.0).astype(f32)
    C = np.ascontiguousarray(cos[:, d // 2].T)
    S = np.ascontiguousarray((sin[:, d // 2] * sgn[None, :]).T)
    csT = np.stack([C, S]).astype(f32)
    return {"ovl": ovl, "addmask": addmask, "esel": esel, "csT": csT}


def host_ret_consts(hh):
    f32 = np.float32
    log_g = np.log(f32(1.0) - f32(2.0) ** (f32(-5.0) - f32(hh))).astype(f32)
    s = np.arange(128)[:, None]
    l = np.arange(128)[None, :]
    drt = np.where(l >= s, SCALE * np.exp(log_g * np.maximum(l - s, 0)), 0.0).astype(f32)
    i = np.arange(128)
    rdec = np.stack([np.exp(log_g * (i + 1)), SCALE * np.exp(log_g * (127 - i)),
                     np.full(128, np.exp(log_g * 128)), np.zeros(128)], axis=1).astype(f32)
    return drt, rdec


def host_modd_a_inputs(p, c, cmp_pe, cmp_w1, cmp_w2, rel_bias, consts):
    kvh = c // 4
    f32 = np.float32
    def colsT(c0):
        return np.ascontiguousarray(p[:, c0:c0 + 128].T)
    perm = np.arange(128) ^ 1
    rqT = colsT(2584 + 128 * c)
    rkT = colsT(3608 + 128 * c)
    drt, rdec = host_ret_consts(c)
    return {
        "qT": colsT(128 * c),
        "kvcT": np.stack([colsT(1024 + 128 * kvh), colsT(1280 + 128 * kvh)]),
        "w1": np.ascontiguousarray(cmp_w1), "w2": np.ascontiguousarray(cmp_w2),
        "peT": np.ascontiguousarray(cmp_pe.transpose(0, 2, 1)),
        "tc": host_tz(rel_bias[:, c], 0, TCW, 16, 31),
        "ovl": consts["ovl"],
        "rq4T": np.stack([rqT, rqT[perm], rkT, rkT[perm]]),
        "csT": consts["csT"],
        "rv": np.ascontiguousarray(p[:, 4632 + 128 * c: 4632 + 128 * c + 128]),
        "rg": np.ascontiguousarray(p[:, 5656 + 128 * c: 5656 + 128 * c + 128]),
        "rdec": rdec, "drt": drt,
    }


def host_modd_b_inputs(p, c, imps, ocmp_c, rel_bias, consts):
    kvh = c // 4
    def colsT(c0):
        return np.ascontiguousarray(p[:, c0:c0 + 128].T)
    return {
        "qT": colsT(128 * c),
        "kswT": np.stack([colsT(1536 + 128 * kvh), colsT(2048 + 128 * kvh)]),
        "vsw": np.stack([p[:, 1792 + 128 * kvh: 1792 + 128 * kvh + 128], p[:, 2304 + 128 * kvh: 2304 + 128 * kvh + 128]]),
        "gates": np.ascontiguousarray(p[:, 2560 + 3 * c: 2560 + 3 * c + 3]),
        "imp4": np.stack([imps[4 * kvh + g] for g in range(4)]),
        "ocmp": ocmp_c,
        "tz": host_tz(rel_bias[:, c], -384, TZW, 1, 0),
        "tw": host_tz(rel_bias[:, c], -384, TWW, 1, 0, band=512),
        "addmask": consts["addmask"], "esel": consts["esel"],
    }


def _run(nc, maps):
    res = run_bass_kernel_spmd(nc, maps, core_ids=list(range(NCORE)))
    return res.results


def _bc(v, n=128):
    return np.ascontiguousarray(np.broadcast_to(np.asarray(v, np.float32), (n, v.shape[-1])))


def kernel(x, rel_bias, ev_w_in, ev_conv_w, ev_conv_b, ev_dt_bias, ev_a_log, ev_d_skip, ev_norm_w, ev_w_out,
           od_w_in, od_cmp_pe, od_cmp_w1, od_cmp_w2, od_w_out,
           ffn_w_up, ffn_conv_w, ffn_conv_b, ffn_w_down, ln_g, ln_b):
    f32 = np.float32
    A = lambda a: np.ascontiguousarray(np.asarray(a, f32))
    rel_bias = A(rel_bias)
    h = A(x)[0]
    consts = host_modd_consts()
    rows = [slice(c * TPC, (c + 1) * TPC) for c in range(NCORE)]
    depth = ln_g.shape[0]
    for layer in range(depth):
        i = layer // 2
        even = layer % 2 == 0
        if layer == 0:
            w_in = A(ev_w_in[i])
            res = _run(build_inproj(w_in.shape[1]), [{"h": h[rows[c]], "w": w_in} for c in range(NCORE)])
            p = np.concatenate([r["p"] for r in res], 0)
            del res
        if even:
            res = _run(build_meven(), [host_meven_inputs(p, c, A(ev_conv_w[i]), A(ev_conv_b[i]), A(ev_dt_bias[i]),
                                                         A(ev_a_log[i]), A(ev_d_skip[i]), rel_bias) for c in range(NCORE)])
            ymix = np.concatenate([r["y"][:, 0:128] for r in res] + [r["y"][:, 128:256] for r in res], 1)
            w_out = A(ev_w_out[i])
        else:
            ra = _run(build_modd_a(), [host_modd_a_inputs(p, c, A(od_cmp_pe[i]), A(od_cmp_w1[i]), A(od_cmp_w2[i]),
                                                          rel_bias, consts) for c in range(NCORE)])
            imps = [r["imp"] for r in ra]
            rb = _run(build_modd_b(), [host_modd_b_inputs(p, c, imps, ra[c]["ocmp"], rel_bias, consts)
                                       for c in range(NCORE)])
            ymix = np.concatenate([r["y"] for r in rb] + [r["yret"] for r in ra], 1)
            w_out = A(od_w_out[i])
        ymix = np.ascontiguousarray(ymix)
        lng, lnb = _bc(A(ln_g[layer, 0])), _bc(A(ln_b[layer, 0]))
        maps = []
        for c in range(NCORE):
            m = {"ymix": ymix[rows[c]], "hprev": h[rows[c]], "w": w_out, "lng": lng, "lnb": lnb}
            if even:
                m["nw"] = _bc(A(ev_norm_w[i]))
            maps.append(m)
        res = _run(build_p1(even), maps)
        hm = np.concatenate([r["hmid"] for r in res], 0)
        lng, lnb = _bc(A(ln_g[layer, 1])), _bc(A(ln_b[layer, 1]))
        cw = host_cw(A(ffn_conv_w[layer]), A(ffn_conv_b[layer]))
        wup, wdn = A(ffn_w_up[layer]), A(ffn_w_down[layer])
        maps = []
        for c in range(NCORE):
            halo = hm[c * TPC - 2:c * TPC] if c > 0 else np.zeros((2, D), f32)
            maps.append({"hmid": hm[rows[c]], "haloT": np.ascontiguousarray(halo.T), "wup": wup, "cw": cw,
                         "wdn": wdn, "lng": lng, "lnb": lnb})
        if layer + 1 < depth:
            wn = A(od_w_in[(layer + 1) // 2] if even else ev_w_in[(layer + 1) // 2])
            for m in maps:
                m["wnext"] = wn
            res = _run(build_p2(wn.shape[1]), maps)
            p = np.concatenate([r["pnext"] for r in res], 0)
        else:
            res = _run(build_p2(), maps)
        h = np.concatenate([r["hout"] for r in res], 0)
    return h[None].astype(f32)
```

```python
from contextlib import ExitStack
import numpy as np
import concourse.bass as bass
import concourse.mybir as mybir
from concourse.bass_utils import run_bass_kernel_spmd

F32 = mybir.dt.float32
BF16 = mybir.dt.bfloat16
ALU = mybir.AluOpType
AF = mybir.ActivationFunctionType
AX = mybir.AxisListType


class Tok:
    __slots__ = ("w", "rd")

    def __init__(self):
        self.w = None
        self.rd = {}


class V:
    __slots__ = ("ap", "toks")

    def __init__(self, ap, toks):
        self.ap = ap
        self.toks = toks

    def __getitem__(self, idx):
        return V(self.ap[idx], self.toks)

    def re(self, pat, **kw):
        return V(self.ap.rearrange(pat, **kw), self.toks)


def _ap(x):
    return x.ap if isinstance(x, V) else x


def _toks(xs):
    out = []
    for x in xs:
        if isinstance(x, V):
            out.extend(x.toks)
        elif isinstance(x, Tok):
            out.append(x)
    return out


NDEV = 8


class Prog:
    ENGS = ("tensor", "vector", "scalar", "gpsimd", "sync")
    NDMA = 8

    def __init__(self):
        self.nc = bass.Bass("TRN2", target_bir_lowering=False, num_devices=NDEV)
        self.es = ExitStack()
        self.ops = {e: [] for e in self.ENGS}
        self.cnt = {e: 0 for e in self.ENGS}
        self.waited = {e: {} for e in self.ENGS}
        self.sems = {}
        for e in self.ENGS:
            self.sems[e] = self.es.enter_context(self.nc.semaphore("s_" + e))
        self.dsem = {}
        self.dcnt = {}
        self.dnext = {}
        for q in ("sync", "gpsimd", "scalar"):
            for i in range(self.NDMA):
                k = "d_%s_%d" % (q, i)
                self.sems[k] = self.es.enter_context(self.nc.semaphore(k))
                self.dcnt[k] = 0
            self.dnext[q] = 0
        self.out_events = []
        self.pending = {}
        self.strict = ("vector", "scalar", "gpsimd")
        self.nalloc = 0

    def dram(self, name, shape, dt, kind):
        return self.nc.dram_tensor(name, list(shape), dt, kind=kind).ap()

    def sb(self, shape, dt, name=None):
        self.nalloc += 1
        es = self.scopes[-1][0] if getattr(self, "scopes", None) else self.es
        t = es.enter_context(self.nc.sbuf_tensor(name or ("sb%d_%d" % (self.nalloc, len(getattr(self, "scopes", [])))), list(shape), dt))
        return V(t[:], [Tok()])

    def scope_begin(self):
        if not hasattr(self, "scopes"):
            self.scopes = []
        self.scopes.append((ExitStack(), set()))

    def scope_end(self):
        self.barrier()
        es, keys = self.scopes.pop()
        for k in keys:
            self._rot.pop(k, None)
        es.close()

    def barrier(self):
        allev = {x: self.cnt[x] for x in self.ENGS}
        allev.update(self.dcnt)
        if "cc" in self.sems:
            allev["cc"] = self.cccnt
        for e in self.ENGS:
            for k, v in allev.items():
                if k == e or v == 0:
                    continue
                if self.waited[e].get(k, 0) < v:
                    self.waited[e][k] = v
                    self.pending.setdefault(e, []).append((k, v))

    def sb_rot(self, key, shape, dt, n):
        if not hasattr(self, "_rot"):
            self._rot = {}
        if key not in self._rot:
            self.nrot = getattr(self, "nrot", 0) + 1
            self._rot[key] = [[self.sb(shape, dt, name="%s_%d_%d" % (key, j, self.nrot)) for j in range(n)], 0]
            if getattr(self, "scopes", None):
                self.scopes[-1][1].add(key)
        ent = self._rot[key]
        v = ent[0][ent[1] % n]
        ent[1] += 1
        return v

    def ps(self, shape, dt=F32, name=None):
        self.nalloc += 1
        t = self.es.enter_context(self.nc.psum_tensor(name or ("ps%d" % self.nalloc), list(shape), dt))
        return V(t[:], [Tok()])

    def _deps(self, eng, reads, writes):
        need = {}

        def add(ev):
            if ev is None:
                return
            k, v = ev
            if need.get(k, 0) < v:
                need[k] = v
        for t in _toks(reads):
            add(t.w)
        for t in _toks(writes):
            add(t.w)
            for k, v in t.rd.items():
                add((k, v))
        waits = []
        wd = self.waited[eng]
        for k, v in need.items():
            if k == eng and not (self.strict and eng in self.strict):
                continue
            if wd.get(k, 0) >= v:
                continue
            wd[k] = v
            waits.append((k, v))
        return waits

    def _commit(self, ev, reads, writes):
        k, v = ev
        for t in _toks(reads):
            if t.rd.get(k, 0) < v:
                t.rd[k] = v
        for t in _toks(writes):
            t.w = ev
            t.rd = {}

    def op(self, eng, fn, reads, writes):
        waits = self.pending.pop(eng, []) + self._deps(eng, reads, writes)
        self.cnt[eng] += 1
        ev = (eng, self.cnt[eng])
        self.ops[eng].append((waits, fn, (eng, 1)))
        self._commit(ev, reads, writes)
        return ev

    def dma(self, out, in_, q="sync", is_output=False, **kw):
        reads = [in_]
        writes = [out]
        i = self.dnext[q]
        self.dnext[q] = (i + 1) % self.NDMA
        k = "d_%s_%d" % (q, i)
        waits = self.pending.pop(q, []) + self._deps(q, reads, writes)
        prev = self.dcnt[k]
        if prev > 0 and self.waited[q].get(k, 0) < prev:
            self.waited[q][k] = prev
            waits.append((k, prev))
        self.dcnt[k] += 16
        ev = (k, self.dcnt[k])
        o, i_ = _ap(out), _ap(in_)
        self.ops[q].append((waits, lambda e: e.dma_start(out=o, in_=i_, **kw), (k, 16)))
        self._commit(ev, reads, writes)
        if is_output:
            self.out_events.append(ev)
        return ev

    def dma_fn(self, fn, reads, writes, q="sync", is_output=False):
        i = self.dnext[q]
        self.dnext[q] = (i + 1) % self.NDMA
        k = "d_%s_%d" % (q, i)
        waits = self._deps(q, reads, writes)
        prev = self.dcnt[k]
        if prev > 0 and self.waited[q].get(k, 0) < prev:
            self.waited[q][k] = prev
            waits.append((k, prev))
        self.dcnt[k] += 16
        ev = (k, self.dcnt[k])
        self.ops[q].append((waits, fn, (k, 16)))
        self._commit(ev, reads, writes)
        if is_output:
            self.out_events.append(ev)
        return ev

    def coll(self, kind, in_, out, groups, q="gpsimd", op=ALU.bypass):
        if "cc" not in self.sems:
            self.sems["cc"] = self.es.enter_context(self.nc.semaphore("cc_sem"))
            self.cccnt = 0
        waits = self._deps(q, [in_], [out])
        if self.cccnt > 0 and self.waited[q].get("cc", 0) < self.cccnt:
            self.waited[q]["cc"] = self.cccnt
            waits.append(("cc", self.cccnt))
        self.cccnt += 1
        ev = ("cc", self.cccnt)
        i_, o_ = _ap(in_).opt(), _ap(out).opt()
        self.ops[q].append((waits, lambda e: e.collective_compute(kind, op, groups, [i_], [o_]), ("cc", None)))
        self._commit(ev, [in_], [out])
        return ev

    def finish(self):
        nc = self.nc
        fin = {}
        for k, v in self.out_events:
            fin[k] = max(fin.get(k, 0), v)
        sems = self.sems
        ops = self.ops
        with nc.Block() as block:
            def runner(name):
                def body(e):
                    for waits, fn, (sk, inc) in ops[name]:
                        for k, v in waits:
                            e.wait_ge(sems[k], v)
                        ins = fn(e)
                        if inc is None:
                            ins.then_inc(sems[sk])
                        else:
                            ins.then_inc(sems[sk], inc)
                    if name == "sync":
                        for k, v in fin.items():
                            e.wait_ge(sems[k], v)
                return body
            block.tensor(runner("tensor"))
            block.vector(runner("vector"))
            block.scalar(runner("scalar"))
            block.gpsimd(runner("gpsimd"))
            block.sync(runner("sync"))
        self.es.close()
        return nc

    def matmul(self, out, lhsT, rhs, start=True, stop=True):
        o, l, r = _ap(out), _ap(lhsT), _ap(rhs)
        return self.op("tensor", lambda e: e.matmul(o, l, r, start=start, stop=stop), [lhsT, rhs], [out])

    def transpose(self, out, in_, ident):
        o, i, d = _ap(out), _ap(in_), _ap(ident)
        return self.op("tensor", lambda e: e.transpose(o, i, d), [in_, ident], [out])

    def act(self, out, in_, func, bias=None, scale=1.0, accum_out=None, eng="scalar"):
        o, i = _ap(out), _ap(in_)
        b = _ap(bias) if bias is not None else None
        s = _ap(scale)
        a = _ap(accum_out) if accum_out is not None else None
        kw = {}
        if b is not None:
            kw["bias"] = b
        if a is not None:
            kw["accum_out"] = a
        return self.op("scalar", lambda e: e.activation(o, i, func, scale=s, **kw),
                       [in_, bias, scale], [out, accum_out])

    def tt(self, out, in0, in1, op, eng="vector"):
        o, a, b = _ap(out), _ap(in0), _ap(in1)
        return self.op(eng, lambda e: e.tensor_tensor(o, a, b, op), [in0, in1], [out])

    def ts(self, out, in0, s1, op0, s2=None, op1=None, accum_out=None, eng="vector"):
        o, a = _ap(out), _ap(in0)
        x1, x2 = _ap(s1), _ap(s2)
        acc = _ap(accum_out) if accum_out is not None else None
        kw = {}
        if op1 is not None:
            kw["op1"] = op1
        if acc is not None:
            kw["accum_out"] = acc
        return self.op(eng, lambda e: e.tensor_scalar(o, a, x1, x2, op0, **kw),
                       [in0, s1, s2], [out, accum_out])

    def stt(self, out, in0, scalar, in1, op0, op1, eng="vector"):
        o, a, s, b = _ap(out), _ap(in0), _ap(scalar), _ap(in1)
        return self.op(eng, lambda e: e.scalar_tensor_tensor(o, a, s, b, op0, op1), [in0, scalar, in1], [out])

    def copy(self, out, in_, eng="vector"):
        o, i = _ap(out), _ap(in_)
        if eng == "scalar":
            return self.op(eng, lambda e: e.copy(o, i), [in_], [out])
        return self.op(eng, lambda e: e.tensor_copy(o, i), [in_], [out])

    def memset(self, out, val, eng="vector"):
        o = _ap(out)
        return self.op(eng, lambda e: e.memset(o, val), [], [out])

    def reduce(self, out, in_, op, axis=AX.X, eng="vector"):
        o, i = _ap(out), _ap(in_)
        return self.op(eng, lambda e: e.tensor_reduce(o, i, axis, op), [in_], [out])

    def recip(self, out, in_):
        o, i = _ap(out), _ap(in_)
        return self.op("vector", lambda e: e.reciprocal(o, i), [in_], [out])

    def max8(self, out, in_):
        o, i = _ap(out), _ap(in_)
        return self.op("vector", lambda e: e.max(o, i), [in_], [out])

    def match_replace(self, out, to_replace, values, imm):
        o, r, v = _ap(out), _ap(to_replace), _ap(values)
        return self.op("vector", lambda e: e.match_replace(o, r, v, imm), [to_replace, values], [out])

    def bn_stats(self, out, in_):
        o, i = _ap(out), _ap(in_)
        return self.op("vector", lambda e: e.bn_stats(o, i), [in_], [out])

    def bn_aggr(self, out, in_):
        o, i = _ap(out), _ap(in_)
        return self.op("vector", lambda e: e.bn_aggr(o, i), [in_], [out])


T = 8192
D = 2048
NCORE = 8
TPC = T // NCORE
KC = D // 128
EVEN_IN = 5648
ODD_IN = 6680
FFN = 5632
FCH = FFN // 128
ALPHA = 8.0 ** 0.25
LN_EPS = 1e-5
SCALE = 128.0 ** -0.5
NEGBIG = -30000.0


def make_ident(P, dt):
    idf = P.sb([128, 128], F32)
    P.memset(idf, 0.0, eng="gpsimd")
    o = idf.ap
    P.op("gpsimd", lambda e: e.affine_select(out=o, in_=o, pattern=[[-1, 128]], compare_op=ALU.not_equal,
                                              fill=1.0, base=0, channel_multiplier=1), [idf], [idf])
    if dt == F32:
        return idf
    idb = P.sb([128, 128], dt)
    P.copy(idb, idf)
    return idb


class PsumPool:
    def __init__(self, P, nf32, nbf16=0):
        self.f = [P.ps([128, 512], F32) for _ in range(nf32)]
        self.b = [P.ps([128, 1024], BF16) for _ in range(nbf16)]
        self.fi = 0
        self.bi = 0

    def nf(self):
        x = self.f[self.fi % len(self.f)]
        self.fi += 1
        return x

    def nb(self):
        x = self.b[self.bi % len(self.b)]
        self.bi += 1
        return x


def load_hT(P, pp, ident_bf, h_dram, ntiles, hT, col0=0):
    for i in range(ntiles):
        hf = P.sb_rot("ldh_f", [128, D], F32, 2)
        P.dma(hf, h_dram[i * 128:(i + 1) * 128, :], q="sync")
        hb = P.sb_rot("ldh_b", [128, D], BF16, 2)
        P.copy(hb[:, 0:1024], hf[:, 0:1024], eng="vector")
        P.copy(hb[:, 1024:2048], hf[:, 1024:2048], eng="gpsimd")
        for g in range(KC // 8):
            pt = pp.nb()
            for j in range(8):
                kc = g * 8 + j
                P.transpose(pt[:, j * 128:(j + 1) * 128], hb[:, kc * 128:(kc + 1) * 128], ident_bf)
            dst = hT[:, g * 8:(g + 1) * 8, col0 + i * 128: col0 + (i + 1) * 128]
            src = pt.re("p (j t) -> p j t", t=128)
            if g % 2 == 0:
                P.copy(dst, src, eng="vector")
            else:
                P.copy(dst, src, eng="scalar")


def layer_norm_tile(P, pre, out, g_bc, b_bc):
    st = P.sb_rot("ln_st", [128, 4, 6], F32, 2)
    for c in range(4):
        P.bn_stats(st[:, c, :], pre[:, c * 512:(c + 1) * 512])
    mv = P.sb_rot("ln_mv", [128, 2], F32, 2)
    P.bn_aggr(mv, st.re("p a b -> p (a b)"))
    rs = P.sb_rot("ln_rs", [128, 1], F32, 2)
    P.ts(rs, mv[:, 1:2], LN_EPS, ALU.add)
    P.act(rs, rs, AF.Sqrt)
    P.recip(rs, rs)
    P.ts(out, pre, mv[:, 0:1], ALU.subtract, rs, ALU.mult)
    P.tt(out, out, g_bc, ALU.mult, eng="gpsimd")
    P.tt(out, out, b_bc, ALU.add)


def build_inproj(IN):
    P = Prog()
    h = P.dram("h", [TPC, D], F32, "ExternalInput")
    w = P.dram("w", [D, IN], F32, "ExternalInput")
    p = P.dram("p", [TPC, IN], F32, "ExternalOutput")
    pp = PsumPool(P, 4, 2)
    idb = make_ident(P, BF16)
    hT = P.sb([128, KC, TPC], BF16)
    load_hT(P, pp, idb, h, TPC // 128, hT)
    wv = w.rearrange("(kc p) n -> p kc n", p=128)
    nblk = (IN + 511) // 512
    for cb in range(nblk):
        c0 = cb * 512
        nc_ = min(512, IN - c0)
        wb = P.sb_rot("wblk", [128, KC, 512], BF16, 2)
        P.dma(wb[:, :, 0:nc_], wv[:, :, c0:c0 + nc_], q="gpsimd")
        for i in range(TPC // 128):
            ps = pp.nf()
            for kc in range(KC):
                P.matmul(ps[:, 0:nc_], hT[:, kc, i * 128:(i + 1) * 128], wb[:, kc, 0:nc_],
                         start=(kc == 0), stop=(kc == KC - 1))
            ob = P.sb_rot("ob", [128, 512], F32, 3)
            if i % 2 == 0:
                P.copy(ob[:, 0:nc_], ps[:, 0:nc_], eng="vector")
            else:
                P.copy(ob[:, 0:nc_], ps[:, 0:nc_], eng="scalar")
            P.dma(p[i * 128:(i + 1) * 128, c0:c0 + nc_], ob[:, 0:nc_], q="sync", is_output=True)
    return P.finish()


def build_p1(even):
    P = Prog()
    ymix = P.dram("ymix", [TPC, D], F32, "ExternalInput")
    hprev = P.dram("hprev", [TPC, D], F32, "ExternalInput")
    w = P.dram("w", [D, D], F32, "ExternalInput")
    lng = P.dram("lng", [128, D], F32, "ExternalInput")
    lnb = P.dram("lnb", [128, D], F32, "ExternalInput")
    if even:
        nw = P.dram("nw", [128, 1024], F32, "ExternalInput")
    hmid = P.dram("hmid", [TPC, D], F32, "ExternalOutput")
    pp = PsumPool(P, 4, 2)
    idb = make_ident(P, BF16)
    wb = P.sb([128, KC, D], BF16)
    wv = w.rearrange("(kc p) n -> p kc n", p=128)
    for c in range(4):
        P.dma(wb[:, :, c * 512:(c + 1) * 512], wv[:, :, c * 512:(c + 1) * 512], q="gpsimd")
    g_bc = P.sb([128, D], F32)
    b_bc = P.sb([128, D], F32)
    P.dma(g_bc, lng)
    P.dma(b_bc, lnb)
    if even:
        nw_bc = P.sb([128, 1024], F32)
        P.dma(nw_bc, nw)
    for i in range(TPC // 128):
        yf = P.sb_rot("yf", [128, D], F32, 2)
        P.dma(yf, ymix[i * 128:(i + 1) * 128, :])
        hp = P.sb_rot("hp", [128, D], F32, 2)
        P.dma(hp, hprev[i * 128:(i + 1) * 128, :])
        yb = P.sb_rot("yb", [128, D], BF16, 2)
        if even:
            ss = P.sb_rot("ss", [128, 2], F32, 2)
            junk = P.sb_rot("junk", [128, 512], F32, 1)
            for g in range(2):
                P.act(junk, yf[:, g * 512:(g + 1) * 512], AF.Square, accum_out=ss[:, g:g + 1])
            P.ts(ss, ss, 1.0 / 512.0, ALU.mult, LN_EPS, ALU.add)
            P.act(ss, ss, AF.Sqrt)
            P.recip(ss, ss)
            for g in range(2):
                P.stt(yb[:, g * 512:(g + 1) * 512], yf[:, g * 512:(g + 1) * 512], ss[:, g:g + 1],
                      nw_bc[:, g * 512:(g + 1) * 512], ALU.mult, ALU.mult)
            P.copy(yb[:, 1024:2048], yf[:, 1024:2048], eng="gpsimd")
        else:
            P.copy(yb[:, 0:1024], yf[:, 0:1024], eng="vector")
            P.copy(yb[:, 1024:2048], yf[:, 1024:2048], eng="gpsimd")
        yT = P.sb_rot("yT", [128, KC, 128], BF16, 2)
        for g in range(2):
            pt = pp.nb()
            for j in range(8):
                kc = g * 8 + j
                P.transpose(pt[:, j * 128:(j + 1) * 128], yb[:, kc * 128:(kc + 1) * 128], idb)
            if g == 0:
                P.copy(yT[:, 0:8, :], pt.re("p (j t) -> p j t", t=128), eng="vector")
            else:
                P.copy(yT[:, 8:16, :], pt.re("p (j t) -> p j t", t=128), eng="scalar")
        pre = P.sb_rot("pre", [128, D], F32, 2)
        for cb in range(4):
            ps = pp.nf()
            for kc in range(KC):
                P.matmul(ps, yT[:, kc, :], wb[:, kc, cb * 512:(cb + 1) * 512], start=(kc == 0), stop=(kc == KC - 1))
            P.stt(pre[:, cb * 512:(cb + 1) * 512], hp[:, cb * 512:(cb + 1) * 512], ALPHA, ps, ALU.mult, ALU.add)
        ot = P.sb_rot("ot", [128, D], F32, 2)
        layer_norm_tile(P, pre, ot, g_bc, b_bc)
        P.dma(hmid[i * 128:(i + 1) * 128, :], ot, is_output=True)
    return P.finish()


def build_p2(next_IN=None):
    P = Prog()
    hmid = P.dram("hmid", [TPC, D], F32, "ExternalInput")
    haloT = P.dram("haloT", [D, 2], F32, "ExternalInput")
    wup = P.dram("wup", [D, 2 * FFN], F32, "ExternalInput")
    cw = P.dram("cw", [128, FCH, 4], F32, "ExternalInput")
    wdn = P.dram("wdn", [FFN, D], F32, "ExternalInput")
    lng = P.dram("lng", [128, D], F32, "ExternalInput")
    lnb = P.dram("lnb", [128, D], F32, "ExternalInput")
    hout = P.dram("hout", [TPC, D], F32, "ExternalOutput")
    if next_IN is not None:
        wnext = P.dram("wnext", [D, next_IN], F32, "ExternalInput")
        pnext = P.dram("pnext", [TPC, next_IN], F32, "ExternalOutput")
    pp = PsumPool(P, 6, 2)
    idb = make_ident(P, BF16)
    idf = make_ident(P, F32)
    g_bc = P.sb([128, D], F32)
    b_bc = P.sb([128, D], F32)
    P.dma(g_bc, lng)
    P.dma(b_bc, lnb)
    cws = P.sb([128, FCH, 4], F32)
    P.dma(cws, cw)
    wupv = wup.rearrange("(kc p) n -> p kc n", p=128)
    wdnv = wdn.rearrange("(f p) n -> p f n", p=128)
    gT = P.sb([128, FCH, TPC], BF16)
    NT = TPC // 128
    P.scope_begin()
    hT = P.sb([128, KC, 2 + TPC], BF16)
    P.dma(hT[:, :, 0:2], haloT.rearrange("(kc p) t -> p kc t", p=128), q="gpsimd")
    load_hT(P, pp, idb, hmid, NT, hT, col0=2)
    for f2 in range(FCH // 2):
        wa = P.sb_rot("wa", [128, KC, 256], BF16, 2)
        wu = P.sb_rot("wu", [128, KC, 256], BF16, 2)
        P.dma(wa, wupv[:, :, f2 * 256:(f2 + 1) * 256], q="gpsimd")
        P.dma(wu, wupv[:, :, FFN + f2 * 256: FFN + (f2 + 1) * 256], q="gpsimd")
        for fi in range(2):
            f = f2 * 2 + fi
            fs = slice(fi * 128, (fi + 1) * 128)
            for b in range(TPC // 256):
                c0 = b * 256
                pa = pp.nf()
                pu = pp.nf()
                for kc in range(KC):
                    P.matmul(pa[:, 0:258], wa[:, kc, fs], hT[:, kc, c0:c0 + 258], start=(kc == 0), stop=(kc == KC - 1))
                for kc in range(KC):
                    P.matmul(pu[:, 0:256], wu[:, kc, fs], hT[:, kc, c0 + 2:c0 + 258], start=(kc == 0), stop=(kc == KC - 1))
                ac = P.sb_rot("ac", [128, 256], F32, 2)
                P.act(ac, pa[:, 2:258], AF.Identity, bias=cws[:, f, 3:4], scale=cws[:, f, 2:3])
                P.stt(ac, pa[:, 1:257], cws[:, f, 1:2], ac, ALU.mult, ALU.add)
                P.stt(ac, pa[:, 0:256], cws[:, f, 0:1], ac, ALU.mult, ALU.add)
                sg = P.sb_rot("sg", [128, 256], F32, 2)
                P.act(sg, ac, AF.Silu)
                P.tt(gT[:, f, b * 256:(b + 1) * 256], sg, pu[:, 0:256], ALU.mult)
    P.scope_end()
    P.scope_begin()
    pre = [P.sb([128, D], F32) for _ in range(NT)]
    for i in range(NT):
        P.dma(pre[i], hmid[i * 128:(i + 1) * 128, :])
    for cb in range(D // 128):
        wd = P.sb_rot("wd", [128, FCH, 128], BF16, 2)
        P.dma(wd, wdnv[:, :, cb * 128:(cb + 1) * 128], q="gpsimd")
        for hf in range(TPC // 512):
            ps = pp.nf()
            for f in range(FCH):
                P.matmul(ps, wd[:, f, :], gT[:, f, hf * 512:(hf + 1) * 512], start=(f == 0), stop=(f == FCH - 1))
            dst = P.sb_rot("dst", [128, 512], F32, 2)
            P.copy(dst, ps, eng="scalar")
            pt = pp.nf()
            for i in range(4):
                P.transpose(pt[:, i * 128:(i + 1) * 128], dst[:, i * 128:(i + 1) * 128], idf)
            for i in range(4):
                pr = pre[hf * 4 + i]
                P.stt(pr[:, cb * 128:(cb + 1) * 128], pr[:, cb * 128:(cb + 1) * 128], ALPHA,
                      pt[:, i * 128:(i + 1) * 128], ALU.mult, ALU.add)
    for i in range(NT):
        ot = P.sb_rot("ot", [128, D], F32, 1)
        layer_norm_tile(P, pre[i], ot, g_bc, b_bc)
        P.dma(hout[i * 128:(i + 1) * 128, :], ot, is_output=True)
    P.scope_end()
    if next_IN is not None:
        P.scope_begin()
        hT2 = P.sb([128, KC, TPC], BF16)
        load_hT(P, pp, idb, hout, NT, hT2)
        wv = wnext.rearrange("(kc p) n -> p kc n", p=128)
        nblk = (next_IN + 511) // 512
        for cb in range(nblk):
            c0 = cb * 512
            nc_ = min(512, next_IN - c0)
            wb = P.sb_rot("wblk", [128, KC, 512], BF16, 2)
            P.dma(wb[:, :, 0:nc_], wv[:, :, c0:c0 + nc_], q="gpsimd")
            for i in range(NT):
                ps = pp.nf()
                for kc in range(KC):
                    P.matmul(ps[:, 0:nc_], hT2[:, kc, i * 128:(i + 1) * 128], wb[:, kc, 0:nc_],
                             start=(kc == 0), stop=(kc == KC - 1))
                ob = P.sb_rot("ob", [128, 512], F32, 3)
                if i % 2 == 0:
                    P.copy(ob[:, 0:nc_], ps[:, 0:nc_], eng="vector")
                else:
                    P.copy(ob[:, 0:nc_], ps[:, 0:nc_], eng="scalar")
                P.dma(pnext[i * 128:(i + 1) * 128, c0:c0 + nc_], ob[:, 0:nc_], q="sync", is_output=True)
        P.scope_end()
    return P.finish()


def host_cw(conv_w, conv_b):
    a = np.concatenate([conv_w, conv_b[None, :]], axis=0)
    return np.ascontiguousarray(a.reshape(4, FCH, 128).transpose(2, 1, 0)).astype(np.float32)


TZW = 2560


FLASH_AHEAD = 3


def flash_block(P, pp, O, QT_blk, nq, k_tiles, scale_bias_fn, pv_fn, n_out):
    nk = len(k_tiles)
    nsub = nq // 128
    pss = {}

    def emit_s(jj):
        kt = k_tiles[jj]
        ps = pp.nf()
        has_mask = kt.get("mask") is not None
        P.matmul(ps[:, 0:nq], kt["lhsT"], QT_blk, start=True, stop=not has_mask)
        if has_mask:
            ml, mr = kt["mask"]
            P.matmul(ps[:, 0:nq], ml, mr, start=False, stop=True)
        pss[jj] = ps

    def emit_rest(jj):
        kt = k_tiles[jj]
        ps = pss.pop(jj)
        pt = P.sb_rot("fl_pt", [128, 512], BF16, 5)
        kind, bap = kt["bias"]
        if kind == "far":
            P.act(pt[:, 0:nq], ps[:, 0:nq], AF.Exp, bias=bap, scale=SCALE)
        else:
            tmp = P.sb_rot("fl_tmp", [128, 512], F32, 4)
            P.stt(tmp[:, 0:nq], ps[:, 0:nq], SCALE, bap, ALU.mult, ALU.add)
            P.act(pt[:, 0:nq], tmp[:, 0:nq], AF.Exp)
        for s in range(nsub):
            P.matmul(O[s][:, 0:n_out], pt[:, s * 128:(s + 1) * 128], pv_fn(jj), start=(jj == 0), stop=(jj == nk - 1))

    ahead = max(1, min(FLASH_AHEAD, len(pp.f) - 1))
    for jj in range(min(ahead, nk)):
        emit_s(jj)
    for jj in range(nk):
        if jj + ahead < nk:
            emit_s(jj + ahead)
        emit_rest(jj)


def flash_finish(P, Os, dst_fn, ncols=128, extra=None):
    for s, O in enumerate(Os):
        den = P.sb_rot("fl_den", [128, 1], F32, 4)
        P.ts(den, O[:, 128:129], 1e-30, ALU.max)
        P.recip(den, den)
        P.ts(dst_fn(s), O[:, 0:ncols], den, ALU.mult)
        if extra is not None:
            extra(s, O, den)


def flash_block_gen(P, pp, O, QT_blk, nq, k_tiles, pv_fn, n_out):
    nk = len(k_tiles)
    nsub = nq // 128
    pss = {}

    def emit_s(jj):
        kt = k_tiles[jj]
        ps = pp.nf()
        has_mask = kt.get("mask") is not None
        P.matmul(ps[:, 0:nq], kt["lhsT"], QT_blk, start=True, stop=not has_mask)
        if has_mask:
            ml, mr = kt["mask"]
            P.matmul(ps[:, 0:nq], ml, mr, start=False, stop=True)
        pss[jj] = ps

    def emit_rest(jj):
        kt = k_tiles[jj]
        ps = pss.pop(jj)
        pt = P.sb_rot("fl_pt", [128, 512], BF16, 5)
        kind, bap = kt["bias"]
        if kind == "far":
            P.act(pt[:, 0:nq], ps[:, 0:nq], AF.Exp, bias=bap, scale=SCALE)
        else:
            tmp = P.sb_rot("fl_tmp", [128, 512], F32, 4)
            P.stt(tmp[:, 0:nq], ps[:, 0:nq], SCALE, bap, ALU.mult, ALU.add)
            P.act(pt[:, 0:nq], tmp[:, 0:nq], AF.Exp)
        for s_ in range(nsub):
            P.matmul(O[s_][:, 0:n_out], pt[:, s_ * 128:(s_ + 1) * 128], pv_fn(jj), start=(jj == 0), stop=(jj == nk - 1))

    ahead = max(1, min(FLASH_AHEAD, len(pp.f) - 1))
    for jj in range(min(ahead, nk)):
        emit_s(jj)
    for jj in range(nk):
        if jj + ahead < nk:
            emit_s(jj + ahead)
        emit_rest(jj)
        yield


def interleave(ga, na, gb, nb):
    ia = ib = 0
    da = db = False
    while not (da and db):
        if not da and (db or ia * nb <= ib * na):
            try:
                next(ga)
                ia += 1
            except StopIteration:
                da = True
        else:
            try:
                next(gb)
                ib += 1
            except StopIteration:
                db = True


def build_meven():
    P = Prog()
    z = P.dram("z", [T, 128], F32, "ExternalInput")
    xbcT = P.dram("xbcT", [3, 128, T + 3], F32, "ExternalInput")
    cwx = P.dram("cwx", [128, 3, 5], F32, "ExternalInput")
    dtr = P.dram("dtr", [T, 2], F32, "ExternalInput")
    hpar = P.dram("hpar", [128, 6], F32, "ExternalInput")
    qT = P.dram("qT", [128, T], F32, "ExternalInput")
    kT = P.dram("kT", [128, T], F32, "ExternalInput")
    v = P.dram("v", [T, 128], F32, "ExternalInput")
    tz = P.dram("tz", [128, TZW], F32, "ExternalInput")
    pastneg = P.dram("pastneg", [128, 64, 32], F32, "ExternalInput")
    ownneg = P.dram("ownneg", [128, 64, 32], F32, "ExternalInput")
    ejd = P.dram("ej", [128, 32, 128], F32, "ExternalInput")
    cst = P.dram("cst", [128, 3, 128], F32, "ExternalInput")
    y = P.dram("y", [T, 256], F32, "ExternalOutput")

    banks = [P.ps([128, 512], F32) for _ in range(8)]
    ppS = PsumPool(P, 0, 0)
    ppS.f = banks[0:2]
    ppM = PsumPool(P, 0, 0)
    ppM.f = banks[6:8]
    O = banks[2:6]
    idf = make_ident(P, F32)
    csts = P.sb([128, 3, 128], F32)
    P.dma(csts, cst)
    triu, ones, causneg = csts[:, 0, :], csts[:, 1, :], csts[:, 2, :]
    cw = P.sb([128, 3, 5], F32)
    P.dma(cw, cwx)
    hp = P.sb([128, 6], F32)
    P.dma(hp, hpar)
    a_bc = P.sb([128, 2], F32)
    P.act(a_bc, hp[:, 2:4], AF.Exp)
    P.ts(a_bc, a_bc, -1.0, ALU.mult)

    QT = P.sb([128, T], BF16)
    KT = P.sb([128, T], BF16)
    V1 = P.sb([128, T // 128, 129], BF16)
    for c in range(4):
        P.dma(QT[:, c * 2048:(c + 1) * 2048], qT[:, c * 2048:(c + 1) * 2048], q="gpsimd")
        P.dma(KT[:, c * 2048:(c + 1) * 2048], kT[:, c * 2048:(c + 1) * 2048], q="gpsimd")
        P.dma(V1[:, c * 16:(c + 1) * 16, 0:128], v[c * 2048:(c + 1) * 2048, :].rearrange("(j p) d -> p j d", p=128), q="gpsimd")
    P.memset(V1[:, :, 128:129], 1.0)
    tzs = P.sb([128, TZW], F32)
    P.dma(tzs, tz)
    pn = P.sb([128, 64, 32], F32)
    on = P.sb([128, 64, 32], F32)
    P.dma(pn, pastneg)
    P.dma(on, ownneg)
    EJ = P.sb([128, 32, 128], BF16)
    P.dma(EJ, ejd, q="gpsimd")
    f31 = P.sb([128, 1], F32)
    P.copy(f31, tzs[:, TZW - 1:TZW])
    kmT = P.sb([128, 32], F32)
    negT = P.sb([128, T], BF16)
    P.memset(negT[:, 0:T // 2], 0.0, eng="gpsimd")
    P.memset(negT[:, T // 2:T], 0.0, eng="gpsimd")

    H = P.sb([128, 128], F32)
    Hb = P.sb([128, 128], BF16)
    P.memset(H, 0.0)
    P.memset(Hb, 0.0)

    def ssd_gen():
        for sc in range(T // 512):
            t0 = sc * 512
            fm = []
            for wch in range(3):
                raw = P.sb_rot("raw%d" % wch, [128, 515], F32, 2)
                P.dma(raw, xbcT[wch, :, t0:t0 + 515])
                acc = P.sb_rot("cacc%d" % wch, [128, 512], F32, 2)
                P.act(acc, raw[:, 3:515], AF.Identity, bias=cw[:, wch, 4:5], scale=cw[:, wch, 3:4])
                for k in (2, 1, 0):
                    P.stt(acc, raw[:, k:k + 512], cw[:, wch, k:k + 1], acc, ALU.mult, ALU.add)
                o = P.sb_rot("cv%d" % wch, [128, 512], F32, 2)
                P.act(o, acc, AF.Silu)
                fm.append(o)
                yield
            xs, Bf, Cf = fm
            BT = P.sb_rot("BTb", [128, 512], BF16, 2)
            CT = P.sb_rot("CTb", [128, 512], BF16, 2)
            P.copy(BT, Bf, eng="gpsimd")
            P.copy(CT, Cf, eng="gpsimd")
            zt = P.sb_rot("zt", [128, 4, 128], F32, 2)
            P.dma(zt, z[t0:t0 + 512, :].rearrange("(i p) d -> p i d", p=128))
            dtt = P.sb_rot("dtt", [128, 4, 2], F32, 2)
            P.dma(dtt, dtr[t0:t0 + 512, :].rearrange("(i p) d -> p i d", p=128))
            dt = P.sb_rot("dt", [128, 4, 2], F32, 2)
            for i in range(4):
                P.tt(dt[:, i, :], dtt[:, i, :], hp[:, 0:2], ALU.add)
            P.act(dt, dt, AF.Exp)
            P.act(dt, dt, AF.Ln, bias=1.0)
            dtA = P.sb_rot("dtA", [128, 4, 2], F32, 2)
            for i in range(4):
                P.tt(dtA[:, i, :], dt[:, i, :], a_bc, ALU.mult)
            sz = P.sb_rot("sz", [128, 4, 128], F32, 2)
            P.act(sz, zt, AF.Silu)
            yield
            for i in range(4):
                cs = slice(i * 128, (i + 1) * 128)
                pa = ppS.nf()
                P.matmul(pa[:, 0:2], triu, dtA[:, i, :])
                P.matmul(pa[:, 2:4], ones, dtA[:, i, :])
                ac = P.sb_rot("ac", [128, 4], F32, 2)
                P.copy(ac, pa[:, 0:4])
                dec = P.sb_rot("dec", [128, 6], F32, 2)
                P.act(dec[:, 0:2], ac[:, 0:2], AF.Exp)
                P.tt(dec[:, 2:4], ac[:, 2:4], ac[:, 0:2], ALU.subtract)
                P.act(dec[:, 2:4], dec[:, 2:4], AF.Exp)
                P.act(dec[:, 4:6], ac[:, 2:4], AF.Exp)
                yield
                DT = []
                for h in range(2):
                    dab = P.sb_rot("dab", [128, 128], F32, 2)
                    P.copy(dab, V(dtA.ap[:, i, h:h + 1].to_broadcast([128, 128]), dtA.toks), eng="gpsimd")
                    pb = ppS.nf()
                    P.matmul(pb[:, 0:128], dab, triu)
                    dm = P.sb_rot("dm%d" % h, [128, 128], F32, 2)
                    P.stt(dm, pb[:, 0:128], ac[:, h:h + 1], causneg, ALU.subtract, ALU.add)
                    P.act(dm, dm, AF.Exp)
                    DT.append(dm)
                    yield
                pS = ppS.nf()
                P.matmul(pS[:, 0:128], BT[:, cs], CT[:, cs])
                SmT = []
                for h in range(2):
                    sm = P.sb_rot("smT%d" % h, [128, 128], BF16, 2)
                    P.tt(sm, pS[:, 0:128], DT[h], ALU.mult)
                    SmT.append(sm)
                yield
                px = ppS.nf()
                P.transpose(px[:, 0:128], xs[:, cs], idf)
                xtok = P.sb_rot("xtok", [128, 128], F32, 2)
                P.copy(xtok, px[:, 0:128], eng="scalar")
                xdt = P.sb_rot("xdt", [128, 128], BF16, 2)
                vd = P.sb_rot("vd", [128, 128], BF16, 2)
                for h in range(2):
                    hs = slice(h * 64, (h + 1) * 64)
                    P.ts(xdt[:, hs], xtok[:, hs], dt[:, i, h:h + 1], ALU.mult)
                    P.ts(vd[:, hs], xtok[:, hs], dt[:, i, h:h + 1], ALU.mult, dec[:, 2 + h:3 + h], ALU.mult)
                yield
                pbt = ppS.nf()
                P.transpose(pbt[:, 0:128], Bf[:, cs], idf)
                btok = P.sb_rot("btok", [128, 128], BF16, 2)
                P.copy(btok, pbt[:, 0:128], eng="scalar")
                yield
                pyd = ppS.nf()
                for h in range(2):
                    hs = slice(h * 64, (h + 1) * 64)
                    P.matmul(pyd[:, hs], SmT[h], xdt[:, hs])
                yt = P.sb_rot("yt", [128, 128], F32, 2)
                P.copy(yt, pyd[:, 0:128], eng="scalar")
                pyo = ppS.nf()
                P.matmul(pyo[:, 0:128], CT[:, cs], Hb)
                yield
                for h in range(2):
                    hs = slice(h * 64, (h + 1) * 64)
                    P.stt(yt[:, hs], pyo[:, hs], dec[:, h:h + 1], yt[:, hs], ALU.mult, ALU.add)
                    P.stt(yt[:, hs], xtok[:, hs], hp[:, 4 + h:5 + h], yt[:, hs], ALU.mult, ALU.add)
                yo = P.sb_rot("yo", [128, 128], F32, 2)
                P.tt(yo, yt, sz[:, i, :], ALU.mult, eng="gpsimd")
                P.dma(y[t0 + i * 128: t0 + (i + 1) * 128, 0:128], yo, is_output=True)
                ph = ppS.nf()
                P.matmul(ph[:, 0:128], btok, vd)
                for h in range(2):
                    hs = slice(h * 64, (h + 1) * 64)
                    P.stt(H[:, hs], H[:, hs], dec[:, 4 + h:5 + h], ph[:, hs], ALU.mult, ALU.add)
                P.copy(Hb, H, eng="gpsimd")
                yield

    def moba_gen():
        for c in range(4):
            kf = P.sb_rot("kf", [128, 2048], F32, 2)
            P.dma(kf, kT[:, c * 2048:(c + 1) * 2048])
            P.reduce(kmT[:, c * 8:(c + 1) * 8], kf.re("p (b t) -> p b t", t=256), ALU.add)
            yield
        P.ts(kmT, kmT, 1.0 / 256.0, ALU.mult)
        for i in range(T // 128):
            qf = P.sb_rot("qf", [128, 128], F32, 3)
            P.dma(qf, qT[:, i * 128:(i + 1) * 128])
            pg = ppM.nf()
            P.matmul(pg[:, 0:32], qf, kmT)
            gm = P.sb_rot("gm", [128, 32], F32, 2)
            P.tt(gm, pg[:, 0:32], pn[:, i, :], ALU.add)
            m8 = P.sb_rot("m8", [128, 8], F32, 2)
            P.max8(m8, gm)
            thr = P.sb_rot("thr", [128, 1], F32, 2)
            P.ts(thr, m8[:, 2:3], -1e29, ALU.max)
            P.ts(gm, gm, thr, ALU.is_ge)
            ng = P.sb_rot("ng", [128, 32], F32, 2)
            P.stt(ng, gm, -NEGBIG, on[:, i, :], ALU.mult, ALU.add)
            pt = ppM.nf()
            P.transpose(pt[0:32, 0:128], ng, idf)
            P.copy(negT[0:32, i * 128:(i + 1) * 128], pt[0:32, 0:128], eng="scalar")
            yield
        for Q in range(T // 512):
            t0 = Q * 512
            kts = []
            for j in range(4 * Q + 4):
                off = t0 - j * 128
                if off >= 1664:
                    b_ = ("far", f31)
                else:
                    b_ = ("near", tzs[:, off + 384: off + 384 + 512])
                kts.append({"lhsT": KT[:, j * 128:(j + 1) * 128],
                            "mask": (EJ[:, j // 2, :], negT[:, t0:t0 + 512]),
                            "bias": b_})
            yield from flash_block_gen(P, ppM, O, QT[:, t0:t0 + 512], 512, kts, lambda jj: V1[:, jj, :], 129)
            yq = P.sb_rot("yq", [128, 4, 128], F32, 2)
            flash_finish(P, O, lambda s_: yq[:, s_, :])
            P.dma(y[t0:t0 + 512, 128:256].rearrange("(s p) d -> p s d", p=128), yq, is_output=True)
            yield

    n_ssd = 16 * (4 + 4 * 8)
    n_moba = 4 + 64 + sum(4 * Q + 4 for Q in range(16)) + 16
    ppS.f = list(banks)
    for _ in ssd_gen():
        pass
    ppM.f = banks[6:8] + banks[0:2]
    for _ in moba_gen():
        pass
    return P.finish()


def rel_bucket_np(dist):
    n = np.maximum(dist, 0)
    max_exact = 16
    nf = np.maximum(n, max_exact).astype(np.float32)
    large = max_exact + (np.log(nf / np.float32(max_exact)) / np.float32(np.log(2048 / max_exact))
                         * np.float32(32 - max_exact)).astype(np.int32)
    large = np.minimum(large, 31)
    return np.where(n < max_exact, n, large)


def host_tz(rel_bias_h, m_lo, width, stride_p, base, band=None):
    p = np.arange(128)[:, None]
    m = np.arange(width)[None, :] + m_lo
    dist = m - stride_p * p - base
    tab = rel_bias_h[rel_bucket_np(dist)].astype(np.float32)
    bad = dist < 0
    if band is not None:
        bad = bad | (dist >= band)
    return np.where(bad, np.float32(-1e30), tab).astype(np.float32)


def host_meven_inputs(p, c, conv_w, conv_b, dt_bias, a_log, d_skip, rel_bias):
    g = c // 4
    f32 = np.float32
    def padT(a):
        o = np.zeros((128, T + 3), f32)
        o[:, 3:] = a.T
        return o
    xb = p[:, 1024:2560]
    chans = [np.arange(128 * c, 128 * c + 128), 1024 + np.arange(128 * g, 128 * g + 128),
             1280 + np.arange(128 * g, 128 * g + 128)]
    xbcT = np.stack([padT(xb[:, ch]) for ch in chans])
    cwx = np.stack([np.concatenate([conv_w[:, ch], conv_b[None, ch]], 0).T for ch in chans], axis=1).astype(f32)
    hpar = np.array([dt_bias[2 * c], dt_bias[2 * c + 1], a_log[2 * c], a_log[2 * c + 1], d_skip[2 * c], d_skip[2 * c + 1]], f32)
    i = np.arange(64)[:, None]
    n = np.arange(32)[None, :]
    pastneg = np.where(n < i // 2, 0.0, -1e30).astype(f32)
    ownneg = np.where(n == i // 2, 0.0, NEGBIG).astype(f32)
    ej = np.zeros((128, 32, 128), f32)
    for J in range(32):
        ej[J, J, :] = 1.0
    k = np.arange(128)[:, None]
    l = np.arange(128)[None, :]
    cst = np.stack([(k <= l).astype(f32), np.ones((128, 128), f32), np.where(l >= k, 0.0, -1e30).astype(f32)], axis=1)
    return {
        "z": np.ascontiguousarray(p[:, 128 * c:128 * c + 128]),
        "xbcT": xbcT, "cwx": np.ascontiguousarray(cwx),
        "dtr": np.ascontiguousarray(p[:, 2560 + 2 * c: 2560 + 2 * c + 2]),
        "hpar": np.ascontiguousarray(np.broadcast_to(hpar, (128, 6))),
        "qT": np.ascontiguousarray(p[:, 2576 + 128 * c: 2576 + 128 * c + 128].T),
        "kT": np.ascontiguousarray(p[:, 3600 + 128 * c: 3600 + 128 * c + 128].T),
        "v": np.ascontiguousarray(p[:, 4624 + 128 * c: 4624 + 128 * c + 128]),
        "tz": host_tz(rel_bias[:, c], -384, TZW, 1, 0),
        "pastneg": np.ascontiguousarray(np.broadcast_to(pastneg, (128, 64, 32))),
        "ownneg": np.ascontiguousarray(np.broadcast_to(ownneg, (128, 64, 32))),
        "ej": ej, "cst": np.ascontiguousarray(cst),
    }


TCW = 3600
TWW = 1408
GELU_C = 1.5957691216057308


def build_modd_a(parts=("cmp", "att", "ret")):
    P = Prog()
    qT = P.dram("qT", [128, T], F32, "ExternalInput")
    kvcT = P.dram("kvcT", [2, 128, T], F32, "ExternalInput")
    w1 = P.dram("w1", [2, 4096, 128], F32, "ExternalInput")
    w2 = P.dram("w2", [2, 128, 128], F32, "ExternalInput")
    peT = P.dram("peT", [2, 128, 32], F32, "ExternalInput")
    tc = P.dram("tc", [128, TCW], F32, "ExternalInput")
    ovl = P.dram("ovl", [128, 4, 128], F32, "ExternalInput")
    rq4T = P.dram("rq4T", [4, 128, T], F32, "ExternalInput")
    csT = P.dram("csT", [2, 128, T], F32, "ExternalInput")
    rv = P.dram("rv", [T, 128], F32, "ExternalInput")
    rg = P.dram("rg", [T, 128], F32, "ExternalInput")
    rdec = P.dram("rdec", [128, 4], F32, "ExternalInput")
    drt = P.dram("drt", [128, 128], F32, "ExternalInput")
    ocmp = P.dram("ocmp", [T, 128], F32, "ExternalOutput")
    imp = P.dram("imp", [T, 128], F32, "ExternalOutput")
    yret = P.dram("yret", [T, 128], F32, "ExternalOutput")

    pp = PsumPool(P, 7, 1)
    idb = make_ident(P, BF16)

    KcT = P.sb([128, 512], BF16)
    VO = P.sb([128, 4, 257], BF16)
    P.memset(VO[:, :, 128:129], 1.0)
    P.dma(VO[:, :, 129:257], ovl, q="gpsimd")
    if "cmp" in parts:
        for kind in range(2):
            W1 = P.sb_rot("W1", [128, 32, 128], BF16, 1)
            P.dma(W1, w1[kind].rearrange("(l d) e -> d l e", d=128), q="gpsimd")
            W2 = P.sb_rot("W2", [128, 128], BF16, 1)
            P.dma(W2, w2[kind], q="gpsimd")
            pe = P.sb_rot("pe", [128, 32], BF16, 1)
            P.dma(pe, peT[kind], q="gpsimd")
            XT = P.sb_rot("XT", [128, T], BF16, 1)
            for c in range(4):
                P.dma(XT[:, c * 2048:(c + 1) * 2048], kvcT[kind, :, c * 2048:(c + 1) * 2048], q="gpsimd")
            pc = pp.nf()
            for l in range(32):
                P.matmul(pc[:, 0:1], W1[:, l, :], pe[:, l:l + 1], start=(l == 0), stop=(l == 31))
            cvec = P.sb_rot("cvec", [128, 1], F32, 1)
            P.copy(cvec, pc[:, 0:1])
            ph = pp.nf()
            for l in range(32):
                P.matmul(ph[:, 0:511], W1[:, l, :], XT[:, l:l + 8161:16], start=(l == 0), stop=(l == 31))
            u = P.sb_rot("cu", [128, 511], F32, 1)
            P.act(u, ph[:, 0:511], AF.Identity, bias=cvec)
            wk = P.sb_rot("cw_", [128, 511], F32, 1)
            P.tt(wk, u, u, ALU.mult)
            P.ts(wk, wk, 0.044715, ALU.mult, 1.0, ALU.add)
            P.tt(wk, wk, u, ALU.mult)
            P.act(wk, wk, AF.Sigmoid, scale=GELU_C)
            hid = P.sb_rot("hid", [128, 512], BF16, 1)
            P.memset(hid[:, 511:512], 0.0)
            P.tt(hid[:, 0:511], u, wk, ALU.mult)
            if kind == 0:
                pk = pp.nf()
                P.matmul(pk, W2, hid)
                P.copy(KcT, pk)
            else:
                for jt in range(4):
                    pv = pp.nf()
                    P.matmul(pv[:, 0:128], hid[:, jt * 128:(jt + 1) * 128], W2)
                    P.copy(VO[:, jt, 0:128], pv[:, 0:128])

    if "att" in parts:
        QT = P.sb([128, T], BF16)
        for c in range(4):
            P.dma(QT[:, c * 2048:(c + 1) * 2048], qT[:, c * 2048:(c + 1) * 2048], q="gpsimd")
        tcs = P.sb([128, TCW], F32)
        P.dma(tcs, tc)
        O = [pp.f[k] for k in range(4)]
        pp.f = pp.f[4:]
        fconst = P.sb([128, 1], F32)
        P.copy(fconst, tcs[:, TCW - 1:TCW])
        qlim = [int(x[1:]) for x in parts if x[0] == "q" and x[1:].isdigit()]
        for Q in range(qlim[0] if qlim else T // 512):
            t0 = Q * 512
            kts = []
            for jt in range(t0 // 2048 + 1):
                m0 = t0 - 2048 * jt
                if m0 >= 3584:
                    b = ("far", fconst)
                else:
                    b = ("near", tcs[:, m0:m0 + 512])
                kts.append({"lhsT": KcT[:, jt * 128:(jt + 1) * 128], "mask": None, "bias": b})
            NO_ = 257 if "n129" not in parts else 129
            flash_block(P, pp, O, QT[:, t0:t0 + 512], 512, kts, None, lambda jj: VO[:, jj, 0:NO_], NO_)
            oc = P.sb_rot("oc", [128, 4, 128], F32, 2)
            im = P.sb_rot("im", [128, 4, 128], F32, 2)

            def extra(s, Ot, den, im=im):
                if "n129" in parts:
                    P.ts(im[:, s, :], Ot[:, 0:128], den, ALU.mult)
                else:
                    P.ts(im[:, s, :], Ot[:, 129:257], den, ALU.mult)
            flash_finish(P, O, lambda s: oc[:, s, :], extra=extra)
            P.dma(ocmp[t0:t0 + 512, :].rearrange("(s p) d -> p s d", p=128), oc, is_output=True)
            P.dma(imp[t0:t0 + 512, :].rearrange("(s p) d -> p s d", p=128), im, is_output=True)
        pp.f = O + pp.f

    if "ret" in parts:
        rd = P.sb([128, 4], F32)
        P.dma(rd, rdec)
        DRT = P.sb([128, 128], F32)
        P.dma(DRT, drt)
        R = P.sb([128, 128], F32)
        Rb = P.sb([128, 128], BF16)
        P.memset(R, 0.0)
        P.memset(Rb, 0.0)
        for sc in range(T // 512):
            t0 = sc * 512
            rot = []
            cs_ = []
            for w in range(2):
                tbl = P.sb_rot("cs%d" % w, [128, 512], F32, 2)
                P.dma(tbl, csT[w, :, t0:t0 + 512])
                cs_.append(tbl)
            for w in range(2):
                a = P.sb_rot("rqa%d" % w, [128, 512], F32, 2)
                b = P.sb_rot("rqb%d" % w, [128, 512], F32, 2)
                P.dma(a, rq4T[2 * w, :, t0:t0 + 512])
                P.dma(b, rq4T[2 * w + 1, :, t0:t0 + 512])
                P.tt(a, a, cs_[0], ALU.mult)
                P.tt(b, b, cs_[1], ALU.mult, eng="gpsimd")
                o = P.sb_rot("rot%d" % w, [128, 512], BF16, 2)
                P.tt(o, a, b, ALU.add)
                rot.append(o)
            QrT, KrT = rot
            vt = P.sb_rot("rvt", [128, 4, 128], F32, 2)
            P.dma(vt, rv[t0:t0 + 512, :].rearrange("(i p) d -> p i d", p=128))
            vb = P.sb_rot("rvb", [128, 4, 128], BF16, 2)
            P.copy(vb, vt, eng="gpsimd")
            vd = P.sb_rot("rvd", [128, 4, 128], BF16, 2)
            P.ts(vd, vt, rd[:, 1:2], ALU.mult)
            gt = P.sb_rot("rgt", [128, 4, 128], F32, 2)
            P.dma(gt, rg[t0:t0 + 512, :].rearrange("(i p) d -> p i d", p=128))
            P.act(gt, gt, AF.Silu)
            yo4 = P.sb_rot("ryo", [128, 4, 128], F32, 2)
            for i in range(4):
                cs = slice(i * 128, (i + 1) * 128)
                pS = pp.nf()
                P.matmul(pS[:, 0:128], KrT[:, cs], QrT[:, cs])
                sm = P.sb_rot("rsm", [128, 128], BF16, 2)
                P.tt(sm, pS[:, 0:128], DRT, ALU.mult)
                pkt = pp.nb()
                P.transpose(pkt[:, 0:128], KrT[:, cs], idb)
                ktok = P.sb_rot("rktok", [128, 128], BF16, 2)
                P.copy(ktok, pkt[:, 0:128], eng="scalar")
                pY = pp.nf()
                P.matmul(pY[:, 0:128], sm, vb[:, i, :])
                pYo = pp.nf()
                P.matmul(pYo[:, 0:128], QrT[:, cs], Rb)
                pR = pp.nf()
                P.matmul(pR[:, 0:128], ktok, vd[:, i, :])
                yt = P.sb_rot("ryt", [128, 128], F32, 2)
                P.copy(yt, pY[:, 0:128], eng="scalar")
                P.stt(yt, pYo[:, 0:128], rd[:, 0:1], yt, ALU.mult, ALU.add)
                st = P.sb_rot("rst", [128, 6], F32, 2)
                P.bn_stats(st, yt)
                mv = P.sb_rot("rmv", [128, 2], F32, 2)
                P.bn_aggr(mv, st)
                rs = P.sb_rot("rrs", [128, 1], F32, 2)
                P.ts(rs, mv[:, 1:2], LN_EPS, ALU.add)
                P.act(rs, rs, AF.Sqrt)
                P.recip(rs, rs)
                P.ts(yt, yt, mv[:, 0:1], ALU.subtract, rs, ALU.mult)
                P.tt(yo4[:, i, :], yt, gt[:, i, :], ALU.mult, eng="gpsimd")
                P.stt(R, R, rd[:, 2:3], pR[:, 0:128], ALU.mult, ALU.add)
                P.copy(Rb, R, eng="gpsimd")
            P.dma(yret[t0:t0 + 512, :].rearrange("(i p) d -> p i d", p=128), yo4, is_output=True)
    return P.finish()


def build_modd_b():
    P = Prog()
    qT = P.dram("qT", [128, T], F32, "ExternalInput")
    kswT = P.dram("kswT", [2, 128, T], F32, "ExternalInput")
    vsw = P.dram("vsw", [2, T, 128], F32, "ExternalInput")
    gates = P.dram("gates", [T, 3], F32, "ExternalInput")
    imp4 = P.dram("imp4", [4, T, 128], F32, "ExternalInput")
    ocmp = P.dram("ocmp", [T, 128], F32, "ExternalInput")
    tz = P.dram("tz", [128, TZW], F32, "ExternalInput")
    tw = P.dram("tw", [128, TWW], F32, "ExternalInput")
    addmask = P.dram("addmask", [128, 64, 128], F32, "ExternalInput")
    eseld = P.dram("esel", [128, 64, 128], F32, "ExternalInput")
    y = P.dram("y", [T, 128], F32, "ExternalOutput")

    pp = PsumPool(P, 8, 0)
    idf = make_ident(P, F32)
    QT = P.sb([128, T], BF16)
    KsT = P.sb([128, T], BF16)
    KwT = P.sb([128, T], BF16)
    Vs1 = P.sb([128, 64, 129], BF16)
    Vw1 = P.sb([128, 64, 129], BF16)
    for c in range(4):
        sl = slice(c * 2048, (c + 1) * 2048)
        P.dma(QT[:, sl], qT[:, sl], q="gpsimd")
        P.dma(KsT[:, sl], kswT[0, :, sl], q="gpsimd")
        P.dma(KwT[:, sl], kswT[1, :, sl], q="gpsimd")
        P.dma(Vs1[:, c * 16:(c + 1) * 16, 0:128], vsw[0, sl, :].rearrange("(j p) d -> p j d", p=128), q="gpsimd")
        P.dma(Vw1[:, c * 16:(c + 1) * 16, 0:128], vsw[1, sl, :].rearrange("(j p) d -> p j d", p=128), q="gpsimd")
    P.memset(Vs1[:, :, 128:129], 1.0)
    P.memset(Vw1[:, :, 128:129], 1.0)
    tzs = P.sb([128, TZW], F32)
    P.dma(tzs, tz)
    tws = P.sb([128, TWW], F32)
    P.dma(tws, tw)
    ES = P.sb([128, 64, 128], BF16)
    for c in range(4):
        P.dma(ES[:, c * 16:(c + 1) * 16, :], eseld[:, c * 16:(c + 1) * 16, :], q="gpsimd")
    sig = P.sb([128, 64, 3], F32)
    P.dma(sig, gates.rearrange("(i p) d -> p i d", p=128))
    P.act(sig, sig, AF.Sigmoid)
    negT = P.sb([128, T], BF16)
    for i in range(T // 128):
        im4 = P.sb_rot("im4", [128, 4, 128], F32, 2)
        P.dma(im4, imp4[:, i * 128:(i + 1) * 128, :].rearrange("g t j -> t g j"))
        am = P.sb_rot("am", [128, 128], F32, 2)
        P.dma(am, addmask[:, i, :])
        sc = P.sb_rot("sc", [128, 128], F32, 2)
        P.tt(sc, im4[:, 0, :], im4[:, 1, :], ALU.add)
        P.tt(sc, sc, im4[:, 2, :], ALU.add)
        P.tt(sc, sc, im4[:, 3, :], ALU.add)
        P.tt(sc, sc, am, ALU.add)
        m8 = P.sb_rot("m8", [128, 16], F32, 2)
        wk = P.sb_rot("wk", [128, 128], F32, 2)
        P.max8(m8[:, 0:8], sc)
        P.match_replace(wk, m8[:, 0:8], sc, -1e30)
        P.max8(m8[:, 8:16], wk)
        thr = P.sb_rot("thr", [128, 1], F32, 2)
        P.ts(thr, m8[:, 15:16], -1e29, ALU.max)
        P.ts(sc, sc, thr, ALU.is_ge)
        P.ts(sc, sc, -NEGBIG, ALU.mult, NEGBIG, ALU.add)
        pt = pp.nf()
        P.transpose(pt[:, 0:128], sc, idf)
        P.copy(negT[:, i * 128:(i + 1) * 128], pt[:, 0:128], eng="scalar")
    O = [pp.f[k] for k in range(4)]
    pp.f = pp.f[4:]
    f31 = P.sb([128, 1], F32)
    P.copy(f31, tzs[:, TZW - 1:TZW])
    for Q in range(T // 512):
        t0 = Q * 512
        kts = []
        for j in range(4 * Q + 4):
            off = t0 - j * 128
            if off >= 1664:
                b = ("far", f31)
            else:
                b = ("near", tzs[:, off + 384: off + 384 + 512])
            kts.append({"lhsT": KsT[:, j * 128:(j + 1) * 128], "mask": (ES[:, j, :], negT[:, t0:t0 + 512]), "bias": b})
        flash_block(P, pp, O, QT[:, t0:t0 + 512], 512, kts, None, lambda jj: Vs1[:, jj, :], 129)
        osel = P.sb_rot("osel", [128, 4, 128], F32, 2)
        flash_finish(P, O, lambda s: osel[:, s, :])
        j0 = max(0, 4 * Q - 4)
        kts = []
        for j in range(j0, 4 * Q + 4):
            off = t0 - j * 128
            kts.append({"lhsT": KwT[:, j * 128:(j + 1) * 128], "mask": None,
                        "bias": ("near", tws[:, off + 384: off + 384 + 512])})
        flash_block(P, pp, O, QT[:, t0:t0 + 512], 512, kts, None, lambda jj, j0=j0: Vw1[:, j0 + jj, :], 129)
        owin = P.sb_rot("owin", [128, 4, 128], F32, 2)
        flash_finish(P, O, lambda s: owin[:, s, :])
        oc = P.sb_rot("occ", [128, 4, 128], F32, 2)
        P.dma(oc, ocmp[t0:t0 + 512, :].rearrange("(s p) d -> p s d", p=128))
        yo = P.sb_rot("yo", [128, 4, 128], F32, 2)
        for s in range(4):
            ti = Q * 4 + s
            P.ts(yo[:, s, :], oc[:, s, :], sig[:, ti, 0:1], ALU.mult, eng="gpsimd")
            P.stt(yo[:, s, :], osel[:, s, :], sig[:, ti, 1:2], yo[:, s, :], ALU.mult, ALU.add)
            P.stt(yo[:, s, :], owin[:, s, :], sig[:, ti, 2:3], yo[:, s, :], ALU.mult, ALU.add)
        P.dma(y[t0:t0 + 512, :].rearrange("(s p) d -> p s d", p=128), yo, is_output=True)
    return P.finish()


def host_modd_consts():
    f32 = np.float32
    n = (np.arange(4)[None, :, None] * 128 + np.arange(128)[:, None, None])
    j = np.arange(128)[None, None, :]
    ovl = ((16 * n < 64 * j + 64) & (16 * n + 32 > 64 * j) & (n < 511)).astype(f32)
    t = np.arange(T)[:, None]
    jj = np.arange(128)[None, :]
    cur = t // 64
    forced = (jj == 0) | (jj == cur) | (jj == cur - 1)
    am = np.where(forced, 100.0, np.where(jj <= cur, 0.0, -1e30)).astype(f32)
    addmask = np.ascontiguousarray(am.reshape(64, 128, 128).transpose(1, 0, 2))
    b = np.arange(128)[:, None, None]
    jt = np.arange(64)[None, :, None]
    k = np.arange(128)[None, None, :]
    esel = (b == 2 * jt + k // 64).astype(f32)
    inv = (1.0 / (np.float32(10000.0) ** (np.arange(0, 128, 2, dtype=f32) / np.float32(128)))).astype(f32)
    ang = (np.arange(T, dtype=f32)[:, None] * inv[None, :]).astype(f32)
    cos = np.cos(ang).astype(f32)
    sin = np.sin(ang).astype(f32)
    d = np.arange(128)
    sgn = np.where(d % 2 == 0, -1.0, 1.0).astype(f32)
    C = np.ascontiguousarray(cos[:, d // 2].T)
    S = np.ascontiguousarray((sin[:, d // 2] * sgn[None, :]).T)
    csT = np.stack([C, S]).astype(f32)
    return {"ovl": ovl, "addmask": addmask, "esel": esel, "csT": csT}


def host_ret_consts(hh):
    f32 = np.float32
    log_g = np.log(f32(1.0) - f32(2.0) ** (f32(-5.0) - f32(hh))).astype(f32)
    s = np.arange(128)[:, None]
    l = np.arange(128)[None, :]
    drt = np.where(l >= s, SCALE * np.exp(log_g * np.maximum(l - s, 0)), 0.0).astype(f32)
    i = np.arange(128)
    rdec = np.stack([np.exp(log_g * (i + 1)), SCALE * np.exp(log_g * (127 - i)),
                     np.full(128, np.exp(log_g * 128)), np.zeros(128)], axis=1).astype(f32)
    return drt, rdec


def host_modd_a_inputs(p, c, cmp_pe, cmp_w1, cmp_w2, rel_bias, consts):
    kvh = c // 4
    f32 = np.float32
    def colsT(c0):
        return np.ascontiguousarray(p[:, c0:c0 + 128].T)
    perm = np.arange(128) ^ 1
    rqT = colsT(2584 + 128 * c)
    rkT = colsT(3608 + 128 * c)
    drt, rdec = host_ret_consts(c)
    return {
        "qT": colsT(128 * c),
        "kvcT": np.stack([colsT(1024 + 128 * kvh), colsT(1280 + 128 * kvh)]),
        "w1": np.ascontiguousarray(cmp_w1), "w2": np.ascontiguousarray(cmp_w2),
        "peT": np.ascontiguousarray(cmp_pe.transpose(0, 2, 1)),
        "tc": host_tz(rel_bias[:, c], 0, TCW, 16, 31),
        "ovl": consts["ovl"],
        "rq4T": np.stack([rqT, rqT[perm], rkT, rkT[perm]]),
        "csT": consts["csT"],
        "rv": np.ascontiguousarray(p[:, 4632 + 128 * c: 4632 + 128 * c + 128]),
        "rg": np.ascontiguousarray(p[:, 5656 + 128 * c: 5656 + 128 * c + 128]),
        "rdec": rdec, "drt": drt,
    }


def host_modd_b_inputs(p, c, imps, ocmp_c, rel_bias, consts):
    kvh = c // 4
    def colsT(c0):
        return np.ascontiguousarray(p[:, c0:c0 + 128].T)
    return {
        "qT": colsT(128 * c),
        "kswT": np.stack([colsT(1536 + 128 * kvh), colsT(2048 + 128 * kvh)]),
        "vsw": np.stack([p[:, 1792 + 128 * kvh: 1792 + 128 * kvh + 128], p[:, 2304 + 128 * kvh: 2304 + 128 * kvh + 128]]),
        "gates": np.ascontiguousarray(p[:, 2560 + 3 * c: 2560 + 3 * c + 3]),
        "imp4": np.stack([imps[4 * kvh + g] for g in range(4)]),
        "ocmp": ocmp_c,
        "tz": host_tz(rel_bias[:, c], -384, TZW, 1, 0),
        "tw": host_tz(rel_bias[:, c], -384, TWW, 1, 0, band=512),
        "addmask": consts["addmask"], "esel": consts["esel"],
    }


def _run(nc, maps):
    res = run_bass_kernel_spmd(nc, maps, core_ids=list(range(NCORE)))
    return res.results


def _bc(v, n=128):
    return np.ascontiguousarray(np.broadcast_to(np.asarray(v, np.float32), (n, v.shape[-1])))


def kernel(x, rel_bias, ev_w_in, ev_conv_w, ev_conv_b, ev_dt_bias, ev_a_log, ev_d_skip, ev_norm_w, ev_w_out,
           od_w_in, od_cmp_pe, od_cmp_w1, od_cmp_w2, od_w_out,
           ffn_w_up, ffn_conv_w, ffn_conv_b, ffn_w_down, ln_g, ln_b):
    f32 = np.float32
    A = lambda a: np.ascontiguousarray(np.asarray(a, f32))
    rel_bias = A(rel_bias)
    h = A(x)[0]
    consts = host_modd_consts()
    rows = [slice(c * TPC, (c + 1) * TPC) for c in range(NCORE)]
    depth = ln_g.shape[0]
    for layer in range(depth):
        i = layer // 2
        even = layer % 2 == 0
        if layer == 0:
            w_in = A(ev_w_in[i])
            res = _run(build_inproj(w_in.shape[1]), [{"h": h[rows[c]], "w": w_in} for c in range(NCORE)])
            p = np.concatenate([r["p"] for r in res], 0)
            del res
        if even:
            res = _run(build_meven(), [host_meven_inputs(p, c, A(ev_conv_w[i]), A(ev_conv_b[i]), A(ev_dt_bias[i]),
                                                         A(ev_a_log[i]), A(ev_d_skip[i]), rel_bias) for c in range(NCORE)])
            ymix = np.concatenate([r["y"][:, 0:128] for r in res] + [r["y"][:, 128:256] for r in res], 1)
            w_out = A(ev_w_out[i])
        else:
            ra = _run(build_modd_a(), [host_modd_a_inputs(p, c, A(od_cmp_pe[i]), A(od_cmp_w1[i]), A(od_cmp_w2[i]),
                                                          rel_bias, consts) for c in range(NCORE)])
            imps = [r["imp"] for r in ra]
            rb = _run(build_modd_b(), [host_modd_b_inputs(p, c, imps, ra[c]["ocmp"], rel_bias, consts)
                                       for c in range(NCORE)])
            ymix = np.concatenate([r["y"] for r in rb] + [r["yret"] for r in ra], 1)
            w_out = A(od_w_out[i])
        ymix = np.ascontiguousarray(ymix)
        lng, lnb = _bc(A(ln_g[layer, 0])), _bc(A(ln_b[layer, 0]))
        maps = []
        for c in range(NCORE):
            m = {"ymix": ymix[rows[c]], "hprev": h[rows[c]], "w": w_out, "lng": lng, "lnb": lnb}
            if even:
                m["nw"] = _bc(A(ev_norm_w[i]))
            maps.append(m)
        res = _run(build_p1(even), maps)
        hm = np.concatenate([r["hmid"] for r in res], 0)
        lng, lnb = _bc(A(ln_g[layer, 1])), _bc(A(ln_b[layer, 1]))
        cw = host_cw(A(ffn_conv_w[layer]), A(ffn_conv_b[layer]))
        wup, wdn = A(ffn_w_up[layer]), A(ffn_w_down[layer])
        maps = []
        for c in range(NCORE):
            halo = hm[c * TPC - 2:c * TPC] if c > 0 else np.zeros((2, D), f32)
            maps.append({"hmid": hm[rows[c]], "haloT": np.ascontiguousarray(halo.T), "wup": wup, "cw": cw,
                         "wdn": wdn, "lng": lng, "lnb": lnb})
        if layer + 1 < depth:
            wn = A(od_w_in[(layer + 1) // 2] if even else ev_w_in[(layer + 1) // 2])
            for m in maps:
                m["wnext"] = wn
            res = _run(build_p2(wn.shape[1]), maps)
            p = np.concatenate([r["pnext"] for r in res], 0)
        else:
            res = _run(build_p2(), maps)
        h = np.concatenate([r["hout"] for r in res], 0)
    return h[None].astype(f32)
```
